# Optimizing a Trainium2 kernel written in Bass

```python
import jax
import jax.numpy as jnp
from jax import lax
import numpy as np

D_MODEL = 1024
BATCH = 8
SEQ = 4096
DEPTH = 1

N_MEM = 256
EPS = 1e-6
GLA_HEADS = 4
GLA_DK = D_MODEL // 8
GLA_DV = D_MODEL // 4
GLA_QK = GLA_HEADS * GLA_DK
GLA_V = GLA_HEADS * GLA_DV
GLA_GATE_RANK = 16
GLA_TAU = 16.0
GLA_CHUNK = 64
DIL_GROUPS = ((128, 1), (512, 4), (2048, 16))
N_GROUPS = len(DIL_GROUPS)
DIL_HEADS = 8
HEAD_DIM = 64
DIL_W = N_GROUPS * DIL_HEADS * HEAD_DIM
DIL_OUT = DIL_HEADS * HEAD_DIM
DIL_BLOCK = 128
ROT_DIM = HEAD_DIM // 4
ROPE_THETA = 500000.0
X_HEADS = 4
X_HEAD_DIM = D_MODEL // X_HEADS
D_FF = 2816
CONV_W = 3
IN_SIZES = (GLA_QK, GLA_QK, GLA_V, GLA_V, GLA_GATE_RANK, DIL_W, DIL_W, DIL_W)
IN_COLS = sum(IN_SIZES)
IN_OFFSETS = tuple(sum(IN_SIZES[:i + 1]) for i in range(len(IN_SIZES) - 1))

kernel_name = 'hybrid_gla_dilated_memxattn_convglu'


def rms_norm(t, g):
    tf = t.astype(jnp.float32)
    y = tf * lax.rsqrt(jnp.mean(tf * tf, axis=-1, keepdims=True) + EPS)
    return (y * g.astype(jnp.float32)).astype(t.dtype)


def rope_partial(t, cos, sin):
    shp = cos.shape[:2] + (1,) * (t.ndim - 3) + cos.shape[2:]
    c, s_ = cos.reshape(shp), sin.reshape(shp)
    tf = t.astype(jnp.float32)
    x1, x2 = tf[..., :ROT_DIM // 2], tf[..., ROT_DIM // 2:ROT_DIM]
    return jnp.concatenate([x1 * c - x2 * s_, x2 * c + x1 * s_, tf[..., ROT_DIM:]], axis=-1)


def gla_chunked(q, k, v, log_a):
    b, h, s, dk = q.shape
    dv = v.shape[-1]
    n = s // GLA_CHUNK

    def chunks(t):
        return t.reshape(b, h, n, GLA_CHUNK, t.shape[-1]).transpose(2, 0, 1, 3, 4)

    causal = jnp.tril(jnp.ones((GLA_CHUNK, GLA_CHUNK), dtype=bool))[:, :, None]

    def step(state, inp):
        qc, kc, vc, gc = inp
        cum = jnp.cumsum(gc, axis=-2)
        o_inter = jnp.einsum('bhik,bhkv->bhiv', qc * jnp.exp(cum), state)
        rel = cum[:, :, :, None, :] - cum[:, :, None, :, :]
        decay = jnp.exp(jnp.where(causal, rel, -jnp.inf))
        attn = jnp.einsum('bhik,bhjk,bhijk->bhij', qc, kc, decay)
        o_intra = jnp.einsum('bhij,bhjv->bhiv', attn, vc)
        last = cum[:, :, -1, :]
        state = state * jnp.exp(last)[..., None] + jnp.einsum(
            'bhjk,bhjv->bhkv', kc * jnp.exp(last[:, :, None, :] - cum), vc)
        return state, o_inter + o_intra

    state0 = jnp.zeros((b, h, dk, dv), jnp.float32)
    _, o = lax.scan(step, state0, (chunks(q), chunks(k), chunks(v), chunks(log_a)))
    return o.transpose(1, 2, 0, 3, 4).reshape(b, h, s, dv)


def banded_attention(q, k, v, n_back):
    lead = q.shape[:-2]
    L, hd = q.shape[-2:]
    blk = DIL_BLOCK
    nb = -(-L // blk)
    lp = nb * blk
    pad = [(0, 0)] * len(lead) + [(0, lp - L), (0, 0)]
    q, k, v = (jnp.pad(t, pad) for t in (q, k, v))
    qb = q.reshape(*lead, nb, blk, hd)

    def kv_blocks(t):
        cur = t.reshape(*lead, nb, blk, hd)
        prev = jnp.pad(t, [(0, 0)] * len(lead) + [(blk, 0), (0, 0)])[..., :lp, :].reshape(*lead, nb, blk, hd)
        return jnp.concatenate([prev, cur], axis=-2)

    kb, vb = kv_blocks(k), kv_blocks(v)
    s = jnp.einsum('...nqd,...nkd->...nqk', qb, kb) * (hd ** -0.5)
    kj = jnp.arange(2 * blk)[None, :]
    dist = (jnp.arange(blk)[:, None] + blk) - kj
    band = (dist >= 0) & (dist <= n_back)
    kabs = jnp.arange(nb)[:, None] * blk - blk + kj
    mask = band[None, :, :] & (kabs >= 0)[:, None, :]
    s = jnp.where(mask, s, -jnp.inf)
    m = jnp.max(s, axis=-1, keepdims=True)
    p = jnp.exp(s - m)
    l = jnp.sum(p, axis=-1, keepdims=True)
    o = jnp.einsum('...nqk,...nkd->...nqd', p, vb) / l
    lse = (m + jnp.log(l))[..., 0]
    return o.reshape(*lead, lp, hd)[..., :L, :], lse.reshape(*lead, lp)[..., :L]


def dilated_attention(q, k, v):
    b, s = q.shape[:2]
    outs, lses = [], []
    for g, (win, dil) in enumerate(DIL_GROUPS):
        L = s // dil

        def sub(t):
            return t[:, :, g].reshape(b, L, dil, DIL_HEADS, HEAD_DIM).transpose(0, 2, 3, 1, 4)

        o, lse = banded_attention(sub(q), sub(k), sub(v), win // dil)
        outs.append(o.transpose(0, 3, 1, 2, 4).reshape(b, s, DIL_HEADS, HEAD_DIM))
        lses.append(lse.transpose(0, 3, 1, 2).reshape(b, s, DIL_HEADS))
    w = jax.nn.softmax(jnp.stack(lses, axis=0), axis=0)
    return jnp.einsum('gbsh,gbshd->bshd', w, jnp.stack(outs, axis=0))


def mixer_sublayer(h, cos, sin, norm_mix, w_in, w_gla_gate, b_gla_gate, gla_out_norm, dil_q_norm, dil_k_norm,
                   w_br_gla, w_br_dil, w_merge_gate, b_merge_gate, w_mix_out):
    b, s, _ = h.shape
    dt = h.dtype
    xn = rms_norm(h, norm_mix)
    qa, ka, va, ra, za, qb, kb, vb = jnp.split(xn @ w_in, IN_OFFSETS, axis=-1)

    def heads(t):
        return t.reshape(b, s, GLA_HEADS, -1).transpose(0, 2, 1, 3).astype(jnp.float32)

    log_a = jax.nn.log_sigmoid((za @ w_gla_gate + b_gla_gate).astype(jnp.float32)) / GLA_TAU
    o_a = gla_chunked(heads(qa) * (GLA_DK ** -0.5), heads(ka), heads(va), heads(log_a))
    o_a = rms_norm(o_a, gla_out_norm).transpose(0, 2, 1, 3).reshape(b, s, GLA_V).astype(dt) * jax.nn.silu(ra)

    def dheads(t):
        return t.reshape(b, s, N_GROUPS, DIL_HEADS, HEAD_DIM)

    qd = rope_partial(rms_norm(dheads(qb), dil_q_norm), cos, sin)
    kd = rope_partial(rms_norm(dheads(kb), dil_k_norm), cos, sin)
    vd = dheads(vb).astype(jnp.float32)
    o_b = dilated_attention(qd, kd, vd).reshape(b, s, DIL_OUT).astype(dt)

    gates = jax.nn.sigmoid((xn @ w_merge_gate + b_merge_gate).astype(jnp.float32)).astype(dt)
    g_a, g_b = jnp.split(gates, 2, axis=-1)
    merged = g_a * (o_a @ w_br_gla) + g_b * (o_b @ w_br_dil)
    return h + merged @ w_mix_out


def memory_cross_attention(h, mem, norm_x, norm_mem, w_xq, w_xkv, x_q_norm, x_k_norm, w_xo):
    b, s, _ = h.shape
    m = mem.shape[1]
    xn = rms_norm(h, norm_x)
    mn = rms_norm(mem, norm_mem)
    q = rms_norm((xn @ w_xq).reshape(b, s, X_HEADS, X_HEAD_DIM), x_q_norm).astype(jnp.float32)
    k, v = jnp.split(mn @ w_xkv, 2, axis=-1)
    k = rms_norm(k.reshape(b, m, X_HEADS, X_HEAD_DIM), x_k_norm).astype(jnp.float32)
    v = v.reshape(b, m, X_HEADS, X_HEAD_DIM).astype(jnp.float32)
    p = jax.nn.softmax(jnp.einsum('bshd,bmhd->bhsm', q, k) * (X_HEAD_DIM ** -0.5), axis=-1)
    o = jnp.einsum('bhsm,bmhd->bshd', p, v).reshape(b, s, D_MODEL).astype(h.dtype)
    return h + o @ w_xo


def conv_glu_ffn(h, norm_ffn, w_ffn_up, w_ffn_conv, b_ffn_conv, w_ffn_down):
    s = h.shape[1]
    xn = rms_norm(h, norm_ffn)
    a, u = jnp.split(xn @ w_ffn_up, 2, axis=-1)
    ap = jnp.pad(a, ((0, 0), (CONV_W - 1, 0), (0, 0)))
    c = b_ffn_conv
    for i in range(CONV_W):
        c = c + ap[:, i:i + s, :] * w_ffn_conv[i]
    y = jax.nn.gelu(c, approximate=False) * u
    return h + y @ w_ffn_down


def setup_inputs(seed: int = 0) -> dict:
    key = jax.random.key(seed)
    ks = iter(jax.random.split(key, 32))
    f32 = jnp.float32

    def nrm(shape, fan_in):
        return jax.random.normal(next(ks), shape, f32) * (fan_in ** -0.5)

    def gain(shape):
        return 1.0 + 0.02 * jax.random.normal(next(ks), shape, f32)

    def bias(shape):
        return 0.01 * jax.random.normal(next(ks), shape, f32)

    x = jax.random.normal(next(ks), (BATCH, SEQ, D_MODEL), f32)
    mem = jax.random.normal(next(ks), (BATCH, N_MEM, D_MODEL), f32)
    starts = jax.random.randint(next(ks), (BATCH, 1), 0, 1024, dtype=jnp.int32)
    positions = starts + jnp.arange(SEQ, dtype=jnp.int32)[None, :]
    L = DEPTH
    return {
        'x': x, 'mem': mem, 'positions': positions,
        'norm_mix': gain((L, D_MODEL)),
        'w_in': nrm((L, D_MODEL, IN_COLS), D_MODEL),
        'w_gla_gate': nrm((L, GLA_GATE_RANK, GLA_QK), GLA_GATE_RANK),
        'b_gla_gate': bias((L, GLA_QK)),
        'gla_out_norm': gain((L, GLA_DV)),
        'dil_q_norm': gain((L, HEAD_DIM)),
        'dil_k_norm': gain((L, HEAD_DIM)),
        'w_br_gla': nrm((L, GLA_V, D_MODEL), GLA_V),
        'w_br_dil': nrm((L, DIL_OUT, D_MODEL), DIL_OUT),
        'w_merge_gate': nrm((L, D_MODEL, 2 * D_MODEL), D_MODEL),
        'b_merge_gate': bias((L, 2 * D_MODEL)),
        'w_mix_out': nrm((L, D_MODEL, D_MODEL), D_MODEL),
        'norm_x': gain((L, D_MODEL)),
        'norm_mem': gain((L, D_MODEL)),
        'w_xq': nrm((L, D_MODEL, D_MODEL), D_MODEL),
        'w_xkv': nrm((L, D_MODEL, 2 * D_MODEL), D_MODEL),
        'x_q_norm': gain((L, X_HEAD_DIM)),
        'x_k_norm': gain((L, X_HEAD_DIM)),
        'w_xo': nrm((L, D_MODEL, D_MODEL), D_MODEL),
        'norm_ffn': gain((L, D_MODEL)),
        'w_ffn_up': nrm((L, D_MODEL, 2 * D_FF), D_MODEL),
        'w_ffn_conv': nrm((L, CONV_W, D_FF), CONV_W),
        'b_ffn_conv': bias((L, D_FF)),
        'w_ffn_down': nrm((L, D_FF, D_MODEL), D_FF),
    }


def reference(x, mem, positions, norm_mix, w_in, w_gla_gate, b_gla_gate, gla_out_norm, dil_q_norm, dil_k_norm,
              w_br_gla, w_br_dil, w_merge_gate, b_merge_gate, w_mix_out, norm_x, norm_mem, w_xq, w_xkv,
              x_q_norm, x_k_norm, w_xo, norm_ffn, w_ffn_up, w_ffn_conv, b_ffn_conv, w_ffn_down):
    inv_freq = ROPE_THETA ** (-jnp.arange(0, ROT_DIM, 2, dtype=jnp.float32) / ROT_DIM)
    ang = positions.astype(jnp.float32)[..., None] * inv_freq
    cos, sin = jnp.cos(ang), jnp.sin(ang)
    h = x
    for l in range(DEPTH):
        h = mixer_sublayer(h, cos, sin, norm_mix[l], w_in[l], w_gla_gate[l], b_gla_gate[l], gla_out_norm[l],
                           dil_q_norm[l], dil_k_norm[l], w_br_gla[l], w_br_dil[l], w_merge_gate[l],
                           b_merge_gate[l], w_mix_out[l])
        h = memory_cross_attention(h, mem, norm_x[l], norm_mem[l], w_xq[l], w_xkv[l], x_q_norm[l],
                                   x_k_norm[l], w_xo[l])
        h = conv_glu_ffn(h, norm_ffn[l], w_ffn_up[l], w_ffn_conv[l], b_ffn_conv[l], w_ffn_down[l])
    return h
```

```python
import numpy as np
import concourse.bass as bass
import concourse.mybir as mybir

F32 = mybir.dt.float32
BF16 = mybir.dt.bfloat16
I32 = mybir.dt.int32
AF = mybir.ActivationFunctionType
ALU = mybir.AluOpType
AX = mybir.AxisListType

SAME_ENGINE_SYNC = True
N_DMA_SEMS = 24


class Buf:
    __slots__ = ("t", "w", "r", "name")

    def __init__(self, t, name=""):
        self.t = t
        self.w = []
        self.r = {}
        self.name = name

    def sub(self, name=""):
        return Buf(self.t, name)

    def __getitem__(self, idx):
        return V(self, self.t[idx])

    def full_v(self):
        return V(self, self.t)


class V:
    __slots__ = ("b", "ap")

    def __init__(self, b, ap):
        self.b = b
        self.ap = ap

    def __getitem__(self, idx):
        return V(self.b, self.ap[idx])

    def rearrange(self, s, **kw):
        return V(self.b, self.ap.rearrange(s, **kw))

    def bcast(self, shape):
        return V(self.b, self.ap.to_broadcast(shape))

    def bitcast(self, dt):
        return V(self.b, self.ap.bitcast(dt))

    def raw(self, extra_off, dims):
        a = self.ap
        return V(self.b, bass.AP(a.tensor, a.offset + extra_off, [list(a.ap[0])] + [list(d) for d in dims]))

    @property
    def shape(self):
        return self.ap.shape


class Instr:
    __slots__ = ("stream", "fn", "deps", "signal", "semval", "dma", "dsem", "dval", "idx")

    def __init__(self, stream, fn, dma=False):
        self.stream = stream
        self.fn = fn
        self.deps = []
        self.signal = False
        self.semval = None
        self.dma = dma
        self.dsem = None
        self.dval = None


class FW:
    STREAMS = ("pe", "act", "dve", "pool", "sp")

    def __init__(self, nc):
        self.nc = nc
        self.prog = {s: [] for s in self.STREAMS}
        self.dma_q = {"sp": (0, N_DMA_SEMS), "pool": (N_DMA_SEMS, 8), "act": (N_DMA_SEMS + 8, 8)}
        self.n_dsem = N_DMA_SEMS + 16
        self.dma_rr = {q: 0 for q in self.dma_q}
        self.dma_last = [None] * self.n_dsem
        self.dma_cnt = [0] * self.n_dsem
        self.all_dmas = []
        self.order = []

    def _add(self, stream, fn, reads, writes, dma=False, join=False, xr=(), xw=()):
        ins = Instr(stream, fn, dma)
        deps = []
        reads = list(reads) + list(xr)
        writes = list(writes) + list(xw)
        wb = set(id(v.b) for v in writes)
        for v in reads:
            b = v.b
            if id(b) in wb:
                continue
            for w in b.w:
                deps.append((w, 0))
        for v in writes:
            b = v.b
            if not join:
                for w in b.w:
                    deps.append((w, 0 if any(v2.b is b for v2 in reads) else 1))
            for r in b.r.values():
                deps.append((r, 1))
        if dma:
            base, cnt = self.dma_q[stream]
            k = base + self.dma_rr[stream]
            self.dma_rr[stream] = (self.dma_rr[stream] + 1) % cnt
            if self.dma_last[k] is not None:
                deps.append((self.dma_last[k], 0))
            self.dma_last[k] = ins
            self.dma_cnt[k] += 16
            ins.dsem = k
            ins.dval = self.dma_cnt[k]
            self.all_dmas.append(ins)
        seen = set()
        for d, kind in deps:
            if d is ins or id(d) in seen:
                continue
            if (not d.dma) and (not dma) and d.stream == stream:
                if stream == "pe" or not SAME_ENGINE_SYNC:
                    continue
            seen.add(id(d))
            ins.deps.append(d)
            d.signal = True
        rkey = ("dma", ins.dsem) if dma else stream
        for v in reads:
            if id(v.b) not in wb:
                v.b.r[rkey] = ins
        for v in writes:
            if join:
                v.b.w = v.b.w + [ins]
            else:
                v.b.w = [ins]
                v.b.r = {}
        self.prog[stream].append(ins)
        return ins

    def barrier(self):
        lasts = []
        for s in self.STREAMS:
            for ins in reversed(self.prog[s]):
                if not ins.dma and ins.fn is not None:
                    lasts.append(ins)
                    break
        lasts = lasts + [d for d in self.dma_last if d is not None]
        for s in self.STREAMS:
            ins = Instr(s, None)
            for d in lasts:
                if (not d.dma) and d.stream == s:
                    continue
                ins.deps.append(d)
                d.signal = True
            self.prog[s].append(ins)

    def wait_for(self, stream, instrs):
        ins = Instr(stream, None)
        for d in instrs:
            ins.deps.append(d)
            d.signal = True
        self.prog[stream].append(ins)

    def dma(self, out, in_, q="sp", join=False, **kw):
        o, i = out.ap, in_.ap
        return self._add(q, lambda e: e.dma_start(out=o, in_=i, **kw), [in_], [out], dma=True, join=join)

    def mm(self, out, lhsT, rhs, start=True, stop=True, xr=()):
        o, l, r = out.ap, lhsT.ap, rhs.ap
        return self._add("pe", lambda e: e.matmul(o, l, r, start=start, stop=stop), [lhsT, rhs], [out], xr=xr)

    def tr(self, out, in_, ident):
        o, i, d = out.ap, in_.ap, ident.ap
        return self._add("pe", lambda e: e.transpose(o, i, d), [in_, ident], [out])

    def act(self, out, in_, func, bias=None, scale=None, accum=None, xr=(), xw=()):
        o, i = out.ap, in_.ap
        kw = {}
        reads = [in_]
        writes = [out]
        if bias is not None:
            if isinstance(bias, V):
                kw["bias"] = bias.ap
                reads.append(bias)
            else:
                kw["bias"] = bias
        if scale is not None:
            if isinstance(scale, V):
                kw["scale"] = scale.ap
                reads.append(scale)
            else:
                kw["scale"] = scale
        if accum is not None:
            kw["accum_out"] = accum.ap
            writes.append(accum)
        return self._add("act", lambda e: e.activation(o, i, func, **kw), reads, writes, xr=xr, xw=xw)

    def _veng(self, eng):
        return eng

    def tt(self, eng, out, in0, in1, op):
        o, a, b = out.ap, in0.ap, in1.ap
        return self._add(eng, lambda e: e.tensor_tensor(o, a, b, op), [in0, in1], [out])

    def ts(self, eng, out, in0, s1, op0, s2=None, op1=None, accum=None):
        o, a = out.ap, in0.ap
        reads = [in0]
        writes = [out]
        if isinstance(s1, V):
            reads.append(s1)
            s1 = s1.ap
        if isinstance(s2, V):
            reads.append(s2)
            s2 = s2.ap
        kw = {}
        if op1 is not None:
            kw["op1"] = op1
        if accum is not None:
            kw["accum_out"] = accum.ap
            writes.append(accum)
        return self._add(eng, lambda e: e.tensor_scalar(o, a, s1, s2, op0, **kw), reads, writes)

    def stt(self, eng, out, in0, scalar, in1, op0, op1):
        o, a, b = out.ap, in0.ap, in1.ap
        reads = [in0, in1]
        if isinstance(scalar, V):
            reads.append(scalar)
            scalar = scalar.ap
        return self._add(eng, lambda e: e.scalar_tensor_tensor(o, a, scalar, b, op0, op1), reads, [out])

    def copy(self, eng, out, in_):
        o, i = out.ap, in_.ap
        if eng == "act":
            return self._add("act", lambda e: e.copy(o, i), [in_], [out])
        return self._add(eng, lambda e: e.tensor_copy(o, i), [in_], [out])

    def reduce(self, eng, out, in_, op, axis=AX.X):
        o, i = out.ap, in_.ap
        return self._add(eng, lambda e: e.tensor_reduce(o, i, axis, op), [in_], [out])

    def recip(self, out, in_):
        o, i = out.ap, in_.ap
        return self._add("dve", lambda e: e.reciprocal(o, i), [in_], [out])

    def memset(self, eng, out, val):
        o = out.ap
        return self._add(eng, lambda e: e.memset(o, val), [], [out])

    def emit(self):
        nc = self.nc
        from contextlib import ExitStack
        with ExitStack() as es:
            esem = {s: es.enter_context(nc.semaphore("s_" + s)) for s in self.STREAMS if s != "sp" or True}
            dsem = [es.enter_context(nc.semaphore("d%d" % k)) for k in range(self.n_dsem)]
            self.barrier()
            for s in self.STREAMS:
                c = 0
                for ins in self.prog[s]:
                    if ins.dma or ins.fn is None:
                        continue
                    if ins.signal:
                        c += 1
                        ins.semval = c
            plan = {}
            nwaits = 0
            for s in self.STREAMS:
                known = {}
                acts = []
                for ins in self.prog[s]:
                    for d in ins.deps:
                        if d.dma:
                            key, val, sem = ("d", d.dsem), d.dval, dsem[d.dsem]
                        else:
                            key, val, sem = ("e", d.stream), d.semval, esem[d.stream]
                        assert val is not None
                        if known.get(key, 0) >= val:
                            continue
                        known[key] = val
                        acts.append(("w", sem, val))
                        nwaits += 1
                    if ins.fn is not None:
                        acts.append(("i", ins))
                plan[s] = acts

            def run(eng, acts, s):
                for a in acts:
                    if a[0] == "w":
                        eng.wait_ge(a[1], a[2])
                    else:
                        ins = a[1]
                        bi = ins.fn(eng)
                        if ins.dma:
                            bi.then_inc(dsem[ins.dsem], 16)
                        elif ins.signal:
                            bi.then_inc(esem[s], 1)

            with nc.Block() as block:
                @block.sync
                def _(e):
                    run(e, plan["sp"], "sp")

                @block.tensor
                def _(e):
                    run(e, plan["pe"], "pe")

                @block.scalar
                def _(e):
                    run(e, plan["act"], "act")

                @block.vector
                def _(e):
                    run(e, plan["dve"], "dve")

                @block.gpsimd
                def _(e):
                    run(e, plan["pool"], "pool")
            self.stats = {s: len(self.prog[s]) for s in self.STREAMS}
            self.stats["waits"] = nwaits
from concourse.bass_utils import run_bass_kernel_spmd

import math
import ml_dtypes
from contextlib import ExitStack

S = 4096
D = 1024
NT = S // 128
NS = S // 512
NMEM = 256
DFF = 2816
NFC = DFF // 128
EPS = 1e-6
IN_COLS = 7696
OFF_QA, OFF_KA, OFF_VA, OFF_RA, OFF_ZA, OFF_QB, OFF_KB, OFF_VB = 0, 512, 1024, 2048, 3072, 3088, 4624, 6160
DILS = (1, 4, 16)
PI = math.pi

BIG_W = {
    "w_in": (1024, IN_COLS), "w_br_gla": (1024, 1024), "w_br_dil": (512, 1024),
    "w_merge_gate": (1024, 2048), "w_mix_out": (1024, 1024), "w_xq": (1024, 1024),
    "w_xkv": (1024, 2048), "w_xo": (1024, 1024), "w_ffn_up": (1024, 2 * DFF),
    "w_ffn_down": (DFF, 1024),
}
SMALL = {
    "norm_mix": (1024,), "w_gla_gate": (16, 512), "b_gla_gate": (512,), "gla_out_norm": (256,),
    "dil_q_norm": (64,), "dil_k_norm": (64,), "b_merge_gate": (2048,), "norm_x": (1024,),
    "norm_mem": (1024,), "x_q_norm": (256,), "x_k_norm": (256,), "norm_ffn": (1024,),
    "w_ffn_conv": (3, DFF), "b_ffn_conv": (DFF,),
}


class Rot:
    def __init__(self, items):
        self.items = items
        self.i = 0

    def next(self):
        it = self.items[self.i % len(self.items)]
        self.i += 1
        return it


def make_consts():
    c = np.zeros((8, 128, 128), np.float32)
    i = np.arange(128)
    c[0] = np.eye(128)
    c[1] = (i[:, None] <= i[None, :])
    c[2] = (i[:, None] > i[None, :])
    c[3] = (i[:, None] >= i[None, :])
    c[6] = 1.0
    inv_freq = (500000.0 ** (-np.arange(0, 16, 2, dtype=np.float32) / 16)).astype(np.float32)
    c[7, :, 0:8] = inv_freq[None, :]
    return c


def build(dbg=None, stop_after=None):
    nc = bass.Bass("TRN2", target_bir_lowering=False)
    fw = FW(nc)
    es = ExitStack()

    def din(name, shape, dt=F32):
        return Buf(nc.dram_tensor(name, list(shape), dt, kind="ExternalInput").ap(), name)

    def dscr(name, shape, dt):
        return Buf(nc.dram_tensor(name, list(shape), dt, kind="Internal").ap(), name)

    def dout(name, shape, dt=F32):
        return Buf(nc.dram_tensor(name, list(shape), dt, kind="ExternalOutput").ap(), name)

    x = din("x", (S, D))
    mem = din("mem", (NMEM, D))
    pos = din("positions", (S,), I32)
    consts = din("consts", (8, 128, 128))
    W32 = {k: din(k, v) for k, v in BIG_W.items()}
    SM = {k: din(k, v) for k, v in SMALL.items()}
    out = dout("out", (S, D))
    dbg_out = None
    if dbg is not None:
        dbg_out = dout("dbg", dbg[1], dbg[2])

    WB = {k: dscr(k + "_bf", v, BF16) for k, v in BIG_W.items()}
    oaT_d = dscr("oaT_d", (1024, S), BF16)
    obT_d = dscr("obT_d", (512, S), BF16)
    h1_d = dscr("h1_d", (S, D), F32)
    h2_d = dscr("h2_d", (S, D), F32)

    def sb(stack, name, shape, dt):
        return Buf(stack.enter_context(nc.sbuf_tensor(name, list(shape), dt))[:], name)

    def ps(stack, name, shape, dt):
        return Buf(stack.enter_context(nc.psum_tensor(name, list(shape), dt))[:], name)

    def dram_bc(buf, n, parts=128, off=0):
        a = buf.t
        return V(buf, bass.AP(a.tensor, a.offset + off, [[0, parts], [1, n]]))

    xstack = ExitStack()

    def finish():
        fw.emit()
        stB1w.close()
        xstack.close()
        es.close()
        return nc, fw

    win_gla_b = WB["w_in"].sub("w_in_gla")
    win_dil_b = WB["w_in"].sub("w_in_dil")
    top = es
    cst = sb(top, "cst", (128, 8, 128), F32)
    cstb = sb(top, "cstb", (128, 8, 128), BF16)
    epsb = sb(top, "epsb", (128, 2), F32)
    gall = sb(top, "gall", (128, 32), F32)
    gcol = {k: V(gall, gall.t[:, 8 * j:8 * j + 8]) for j, k in enumerate(("norm_mix", "norm_x", "norm_mem", "norm_ffn"))}

    fw.dma(cst[:, :, :], consts.full_v().rearrange("c p f -> p c f"))
    fw.copy("dve", cstb[:, :, :], cst[:, :, :])
    fw.memset("dve", epsb[:, 0:1], EPS)
    fw.memset("dve", epsb[:, 1:2], 1.0)
    with ExitStack() as st0:
        grow = sb(st0, "grow", (32, 128), F32)
        gps = ps(st0, "gps", (128, 512), F32)
        for j, k in enumerate(("norm_mix", "norm_x", "norm_mem", "norm_ffn")):
            fw.dma(grow[8 * j:8 * j + 8, :], SM[k].full_v().rearrange("(c p) -> c p", p=128))
        fw.tr(gps[:, 0:32], grow[:, :], cst[0:32, 0, 0:32])
        fw.copy("dve", gall[:, :], gps[:, 0:32])
        fw.barrier()
    ident_b = cstb[:, 0, :]
    ones_row_b = cstb[0:1, 6, :]
    eps_c = epsb[:, 0:1]
    one_c = epsb[:, 1:2]

    xn_t = sb(xstack, "xnT", (128, 8, S), BF16)
    xn_sub = [xn_t.sub("xn%d" % j) for j in range(NS)]
    xn_ro = xn_t.sub("xn_ro")

    def xn_w(i):
        return V(xn_sub[i // 4], xn_t.t[:, :, i * 128:(i + 1) * 128])

    def norm_a(nb, xt):
        junk, ssr, xsr, ptr = nb
        jk = junk.next()
        ss = ssr.next()
        fw.act(jk[:, :], xt, AF.Square, accum=ss[:, 0:1])
        fw.act(ss[:, 1:2], ss[:, 0:1], AF.Ln, scale=1.0 / 1024, bias=eps_c)
        fw.act(ss[:, 2:3], ss[:, 1:2], AF.Exp, scale=-0.5)
        xs = xsr.next()
        fw.ts("dve", xs[:, :], xt, ss[:, 2:3], ALU.mult)
        return xs

    def norm_b(nb, xs, g, dst_view):
        junk, ssr, xsr, ptr = nb
        pt = ptr.next()
        ptb = pt[:, :].bitcast(BF16)
        for c in range(8):
            fw.tr(ptb[:, c * 128:(c + 1) * 128], xs[:, c * 128:(c + 1) * 128], ident_b)
        fw.tt("dve", dst_view, ptb.rearrange("p (c t) -> p c t", c=8),
              g[:, :].raw(0, [[1, 8], [0, 128]]), ALU.mult)

    def norm_T(nb, xt, g, dst_view):
        norm_b(nb, norm_a(nb, xt), g, dst_view)

    class HeadNorm:
        def __init__(self, st, tag, n=3):
            self.slots = []
            for i in range(n):
                ss = sb(st, "hn_ss%s%d" % (tag, i), (128, 12), F32)
                jk = sb(st, "hn_jk%s%d" % (tag, i), (128, 4, 256), BF16)
                self.slots.append((ss, [ss.sub() for _ in range(4)], jk, [jk.sub() for _ in range(4)]))
            self.i = 0

        def stats(self, srcs):
            ss, ssub, jk, jsub = self.slots[self.i % len(self.slots)]
            self.i += 1
            for h in range(4):
                fw.act(V(jsub[h], jk.t[:, h, :]), srcs[h], AF.Square, accum=V(ssub[h], ss.t[:, h:h + 1]))
            fw.act(ss[:, 4:8], ss[:, 0:4], AF.Ln, scale=1.0 / 256, bias=eps_c,
                   xr=[V(ssub[h], ss.t[:, h:h + 1]) for h in range(4)])
            fw.act(ss[:, 8:12], ss[:, 4:8], AF.Exp, scale=-0.5)
            return ss

    def norm_bufs(st, tag, nj=2, nxs=2):
        return (Rot([sb(st, "junk%s%d" % (tag, i), (128, 1024), BF16) for i in range(nj)]),
                Rot([sb(st, "ss%s%d" % (tag, i), (128, 4), F32) for i in range(max(4, nxs + 1))]),
                Rot([sb(st, "xs%s%d" % (tag, i), (128, 1024), BF16) for i in range(nxs)]))

    win = V(win_gla_b, WB["w_in"].t.rearrange("(c p) n -> p c n", p=128))
    win_dil = V(win_dil_b, WB["w_in"].t.rearrange("(c p) n -> p c n", p=128))
    stB1w = ExitStack()
    wqk = sb(stB1w, "wqk", (128, 8, 1024), BF16)
    wv = sb(stB1w, "wv", (128, 8, 1024), BF16)
    wr = sb(stB1w, "wr", (128, 8, 1024), BF16)
    wz = sb(stB1w, "wz", (128, 8, 16), BF16)
    wgg = sb(stB1w, "wgg", (16, 512), BF16)
    bgg = sb(stB1w, "bgg", (1, 512), BF16)
    gon = sb(stB1w, "gon", (128, 256), F32)
    fw.dma(gon[:, :], dram_bc(SM["gla_out_norm"], 256))
    fw.dma(wgg[:, :], SM["w_gla_gate"].full_v(), q="pool")
    fw.dma(bgg[:, :], SM["b_gla_gate"].full_v().rearrange("(o n) -> o n", o=1), q="pool")
    b1_loads = []
    with ExitStack() as st:
        xts = Rot([sb(st, "xt%d" % i, (128, 1024), F32) for i in range(4)])
        nb = norm_bufs(st, "A") + (Rot([ps(st, "ptA%d" % i, (128, 512), F32) for i in range(2)]),)
        xloads = []
        for i in range(NT):
            xt = xts.next()
            xloads.append(fw.dma(xt[:, :], x[i * 128:(i + 1) * 128, :]))
            if i == 3:
                fw.wait_for("pool", xloads)
                for (c0, c1, bsub) in ((0, 1544, win_gla_b), (1544, 3088, win_gla_b)):
                    for r0 in (0, 512):
                        fw.dma(V(bsub, WB["w_in"].t[r0:r0 + 512, c0:c1]),
                               V(W32["w_in"], W32["w_in"].t[r0:r0 + 512, c0:c1]), q="pool", join=True)
            if i == NT - 1:
                b1_loads.append(fw.dma(wqk[:, :, :], win[:, :, 0:1024]))
                b1_loads.append(fw.dma(wv[:, :, :], win[:, :, 1024:2048]))
                b1_loads.append(fw.dma(wr[:, :, :], win[:, :, 2048:3072]))
                b1_loads.append(fw.dma(wz[:, :, :], win[:, :, 3072:3088], allow_slow_non_contiguous=True))
            norm_T(nb, xt[:, :], gcol["norm_mix"], xn_w(i))
        fw.barrier()
    fw.wait_for("pool", b1_loads)
    deferred_conv = []
    for (c0, c1, bsub) in ((3088, 4624, win_dil_b), (4624, 6160, win_dil_b), (6160, 7696, win_dil_b)):
        for r0 in (0, 512):
            deferred_conv.append((V(bsub, WB["w_in"].t[r0:r0 + 512, c0:c1]),
                                  V(W32["w_in"], W32["w_in"].t[r0:r0 + 512, c0:c1])))
    for k, shp in BIG_W.items():
        if k == "w_in":
            continue
        n = shp[0] * shp[1]
        rows = n // 2048
        src = W32[k].full_v().rearrange("a b -> (a b)").rearrange("(r c) -> r c", c=2048)
        dst = WB[k].full_v().rearrange("a b -> (a b)").rearrange("(r c) -> r c", c=2048)
        r0 = 0
        while r0 < rows:
            r1 = min(rows, r0 + 512)
            deferred_conv.append((dst[r0:r1, :], src[r0:r1, :]))
            r0 = r1

    def emit_conv(after=None, n=1):
        for _ in range(n):
            if not deferred_conv:
                return
            if after is not None:
                fw.wait_for("pool", [after])
            d_, s_ = deferred_conv.pop(0)
            fw.dma(d_, s_, q="pool", join=True)

    emit_conv(None, 2)

    DKS = 128 ** -0.5
    oaT_v = oaT_d.full_v().rearrange("(c p) t -> p c t", p=128)
    with ExitStack() as st:
        S32 = [sb(st, "S32_%d" % h, (128, 256), F32) for h in range(4)]
        Sbf = [sb(st, "Sbf_%d" % h, (128, 256), BF16) for h in range(4)]
        for h in range(4):
            fw.memset("pool", S32[h][:, :], 0.0)
            fw.memset("pool", Sbf[h][:, :], 0.0)
        PB = Rot([ps(st, "bP%d" % i, (128, 512), F32) for i in range(8)])
        zar = Rot([sb(st, "za%d" % i, (16, 512), BF16) for i in range(2)])
        e1r = Rot([sb(st, "e1_%d" % i, (128, 512), F32) for i in range(1)])
        spb = [sb(st, "sp_%d" % i, (128, 512), F32) for i in range(4)]
        eTpr = Rot([sb(st, "eTp%d" % i, (128, 512), F32) for i in range(2)])
        eTnr = Rot([sb(st, "eTn%d" % i, (128, 512), F32) for i in range(2)])
        ervr = Rot([sb(st, "erv%d" % i, (128, 512), F32) for i in range(2)])
        elr = Rot([sb(st, "elast%d" % i, (128, 16), F32) for i in range(2)])
        qes = [[sb(st, "qe%d_%d" % (i, h), (128, 512), BF16) for h in range(4)] for i in range(2)]
        kes = [[sb(st, "ke%d_%d" % (i, h), (128, 512), BF16) for h in range(4)] for i in range(2)]
        kls = [[sb(st, "kl%d_%d" % (i, c4), (128, 512), BF16) for c4 in range(4)] for i in range(2)]
        vsr = Rot([sb(st, "vs%d" % i, (128, 1024), BF16) for i in range(2)])
        atr = Rot([sb(st, "at%d" % i, (128, 512), BF16) for i in range(3)])
        Gr = Rot([sb(st, "G%d" % i, (128, 1024), F32) for i in range(2)])
        onr = Rot([sb(st, "on%d" % i, (128, 1024), BF16) for i in range(2)])
        oTr = Rot([sb(st, "oT%d" % i, (128, 8, 128), BF16) for i in range(2)])
        hn = HeadNorm(st, "B", 2)
        pst = {}
        cst_ = {}

        def xs_(kc, s_):
            return V(xn_ro, xn_t.t[:, kc, s_ * 512:(s_ + 1) * 512])

        def xc_(kc, c):
            return V(xn_ro, xn_t.t[:, kc, c * 128:(c + 1) * 128])

        def P_a(s_):
            d = pst.setdefault(s_, {})
            d["set"] = s_ % 2
            pza = PB.next()
            for kc in range(8):
                fw.mm(pza[0:16, :], wz[:, kc, :], xs_(kc, s_), start=(kc == 0), stop=(kc == 7))
            za = zar.next()
            fw.copy("act", za[:, :], pza[0:16, :])
            for c4 in range(4):
                pz = PB.next()
                fw.mm(pz[:, :], za[:, c4 * 128:(c4 + 1) * 128], wgg[:, :], start=True, stop=False)
                fw.mm(pz[:, :], ones_row_b, bgg[:, :], start=False, stop=True)
                e1 = e1r.next()
                fw.act(e1[:, :], pz[:, :], AF.Exp, scale=-1.0)
                fw.act(spb[c4][:, :], e1[:, :], AF.Ln, bias=one_c, scale=1.0)
            d["el"] = elr.next()

        def P_b(s_, hs):
            d = pst[s_]
            el = d["el"]
            for h in hs:
                pc = PB.next()
                for c4 in range(4):
                    fw.mm(pc[:, c4 * 128:(c4 + 1) * 128], spb[c4][:, h * 128:(h + 1) * 128], cst[:, 1, :])
                eTp = eTpr.next()
                eTn = eTnr.next()
                fw.act(eTp[:, :], pc[:, :], AF.Exp, scale=-1.0 / 16)
                fw.act(eTn[:, :], pc[:, :], AF.Exp, scale=1.0 / 16)
                fw.copy("pool", el[:, h * 4:(h + 1) * 4], eTp[:, :].raw(127, [[128, 4]]))
                for which in range(2):
                    p_ = PB.next()
                    co = (0 if which == 0 else 512) + h * 128
                    for kc in range(8):
                        fw.mm(p_[:, :], wqk[:, kc, co:co + 128], xs_(kc, s_), start=(kc == 0), stop=(kc == 7))
                    if which == 0:
                        fw.stt("dve", qes[d["set"]][h][:, :], p_[:, :], DKS, eTp[:, :], ALU.mult, ALU.mult)
                    else:
                        fw.tt("dve", kes[d["set"]][h][:, :], p_[:, :], eTn[:, :], ALU.mult)

        def P_c(s_, c4):
            d = pst[s_]
            c = s_ * 4 + c4
            pr = PB.next()
            fw.mm(pr[:, :], cst[:, 2, :], spb[c4][:, :])
            erv = ervr.next()
            fw.act(erv[:, :], pr[:, :], AF.Exp, scale=-1.0 / 16)
            pkt = PB.next()
            for kc in range(8):
                fw.mm(pkt[:, :], xc_(kc, c), wqk[:, kc, 512:1024], start=(kc == 0), stop=(kc == 7))
            fw.tt("dve", kls[d["set"]][c4][:, :], pkt[:, :], erv[:, :], ALU.mult)

        def R1(c):
            s_, c4 = c // 4, c % 4
            d = pst[s_]
            cd = cst_.setdefault(c, {})
            qe, ke = qes[d["set"]], kes[d["set"]]
            tsl = slice(c4 * 128, (c4 + 1) * 128)
            pa = PB.next()
            for h in range(4):
                fw.mm(pa[:, h * 128:(h + 1) * 128], ke[h][:, tsl], qe[h][:, tsl])
            atm = atr.next()
            fw.tt("dve", atm[:, :].rearrange("p (h i) -> p h i", h=4), pa[:, :].rearrange("p (h i) -> p h i", h=4),
                  cst[:, 1, :].raw(0, [[0, 4], [1, 128]]), ALU.mult)
            cd["atm"] = atm
            vs = vsr.next()
            for half in range(2):
                pv = PB.next()
                for kc in range(8):
                    fw.mm(pv[:, :], xc_(kc, c), wv[:, kc, half * 512:(half + 1) * 512], start=(kc == 0), stop=(kc == 7))
                fw.copy("act", vs[:, half * 512:(half + 1) * 512], pv[:, :])
            cd["vs"] = vs
            G = Gr.next()
            for half in range(2):
                prr = PB.next()
                for kc in range(8):
                    fw.mm(prr[:, :], xc_(kc, c), wr[:, kc, half * 512:(half + 1) * 512], start=(kc == 0), stop=(kc == 7))
                fw.act(G[:, half * 512:(half + 1) * 512], prr[:, :], AF.Silu)
            fw.tt("pool", G[:, :].rearrange("p (h v) -> p h v", h=4), G[:, :].rearrange("p (h v) -> p h v", h=4),
                  gon[:, :].raw(0, [[0, 4], [1, 256]]), ALU.mult)
            cd["G"] = G

        def R2(c):
            s_, c4 = c // 4, c % 4
            d = pst[s_]
            cd = cst_[c]
            qe = qes[d["set"]]
            klw = kls[d["set"]][c4]
            vs = cd["vs"]
            el = d["el"]
            atm, G = cd["atm"], cd["G"]
            tsl = slice(c4 * 128, (c4 + 1) * 128)
            pos = []
            for hb in range(2):
                pob = PB.next()
                for hl in range(2):
                    h = hb * 2 + hl
                    fw.mm(pob[:, hl * 256:(hl + 1) * 256], atm[:, h * 128:(h + 1) * 128], vs[:, h * 256:(h + 1) * 256],
                          start=True, stop=False)
                    fw.mm(pob[:, hl * 256:(hl + 1) * 256], qe[h][:, tsl], Sbf[h][:, :], start=False, stop=True)
                pos.append(pob)
            for hb in range(2):
                pdb = PB.next()
                for hl in range(2):
                    h = hb * 2 + hl
                    fw.mm(pdb[:, hl * 256:(hl + 1) * 256], klw[:, h * 128:(h + 1) * 128], vs[:, h * 256:(h + 1) * 256])
                for hl in range(2):
                    h = hb * 2 + hl
                    fw.stt("dve", S32[h][:, :], S32[h][:, :], el[:, h * 4 + c4:h * 4 + c4 + 1],
                           pdb[:, hl * 256:(hl + 1) * 256], ALU.mult, ALU.add)
                    fw.copy("pool", Sbf[h][:, :], S32[h][:, :])
            srcs = [pos[h // 2][:, (h % 2) * 256:(h % 2 + 1) * 256] for h in range(4)]
            ss = hn.stats(srcs)
            on = onr.next()
            for h in range(4):
                fw.stt("dve", on[:, h * 256:(h + 1) * 256], srcs[h], ss[:, 8 + h:9 + h],
                       G[:, h * 256:(h + 1) * 256], ALU.mult, ALU.mult)
            cd["on"] = on

        def R3(c):
            cd = cst_[c]
            on = cd["on"]
            pt = PB.next()
            ptb = pt[:, :].bitcast(BF16)
            for cc in range(8):
                fw.tr(ptb[:, cc * 128:(cc + 1) * 128], on[:, cc * 128:(cc + 1) * 128], ident_b)
            oT = oTr.next()
            fw.copy("act", oT[:, :, :], ptb.rearrange("p (c t) -> p c t", c=8))
            st_ins = fw.dma(oaT_v[:, :, c * 128:(c + 1) * 128], oT[:, :, :], join=True)
            emit_conv(st_ins, 1)

        def P_quarter(s_, qi):
            if qi == 0:
                P_a(s_)
                P_b(s_, (0,))
            elif qi == 1:
                P_b(s_, (1, 2))
            elif qi == 2:
                P_b(s_, (3,))
                P_c(s_, 0)
                P_c(s_, 1)
            else:
                P_c(s_, 2)
                P_c(s_, 3)

        for qi in range(4):
            P_quarter(0, qi)
        R1(0)
        for c in range(NT):
            s_, c4 = c // 4, c % 4
            R2(c)
            if c > 0:
                R3(c - 1)
            if s_ + 1 < NS:
                P_quarter(s_ + 1, c4)
            if c + 1 < NT:
                R1(c + 1)
        R3(NT - 1)
        emit_conv(None, 1000)
        fw.barrier()

    if dbg is not None and dbg[0] == "B1":
        with ExitStack() as st:
            tmp = sb(st, "dbgt", (128, 8, 512), BF16)
            dv = dbg_out.full_v().rearrange("(c p) t -> p c t", p=128)
            for j in range(8):
                fw.dma(tmp[:, :, :], oaT_v[:, :, j * 512:(j + 1) * 512])
                fw.dma(dv[:, :, j * 512:(j + 1) * 512], tmp[:, :, :])
            fw.barrier()
        return finish()

    def run_skewed(n_items, stages):
        K = len(stages)
        ctx = [dict() for _ in range(n_items)]
        for step in range(n_items + K - 1):
            for k in range(K):
                i = step - k
                if 0 <= i < n_items and stages[k] is not None:
                    stages[k](i, ctx[i])

    qkT_d = [dscr("qkT_d%d" % g, (4, 128, 2, S), BF16) for g in range(3)]
    vaug_d = [dscr("vaug_d%d" % g, (128, 32, 8, 65), BF16) for g in range(3)]

    def phase_c0():
        with ExitStack() as st:
            pqk = Rot([ps(st, "pqk%d" % i, (128, 1024), F32) for i in range(2)])
            pvv = Rot([ps(st, "pvv%d" % i, (128, 512), F32) for i in range(2)])
            ptt = Rot([ps(st, "ptt%d" % i, (128, 512), F32) for i in range(2)])
            invf = cst[:, 7, 0:8]
            csn = [sb(st, "cs%d" % g, (128, 32, 16), F32) for g in range(3)]
            snn = [sb(st, "sn%d" % g, (128, 32, 16), F32) for g in range(3)]
            gqk = sb(st, "gqk", (128, 16, 64), F32)
            for a_ in range(2):
                nm = "dil_q_norm" if a_ == 0 else "dil_k_norm"
                a = SM[nm].t
                fw.dma(gqk[:, a_ * 8:(a_ + 1) * 8, :], V(SM[nm], bass.AP(a.tensor, a.offset, [[0, 128], [0, 8], [1, 64]])))
            if True:
                prow_b = sb(st, "prow", (1, S), F32)
                prow = prow_b[:, :]
                ang = sb(st, "ang", (128, 256), F32)
                tA = sb(st, "tA", (128, 256), F32)
                tB = sb(st, "tB", (128, 256), F32)
                tI = sb(st, "tI", (128, 256), I32)
                TWO_PI = 2.0 * PI
                fw.dma(prow, pos.full_v().rearrange("(o n) -> o n", o=1), q="pool")
                aps = [ptt.items[1]]

                def build_table(g):
                    dil = DILS[g]
                    nb_ = 32 // dil
                    ap_ = aps[0]
                    for n in range(32):
                        r, bb = n // nb_, n % nb_
                        t0 = dil * 128 * bb + r
                        fw.mm(ap_[:, n * 8:(n + 1) * 8], prow[0:1, t0:t0 + 127 * dil + 1:dil], cst[0:1, 7, 0:8])
                    fw.copy("dve", ang[:, :], ap_[:, 0:256])
                    for which in range(2):
                        if which == 1:
                            fw.ts("dve", ang[:, :], ang[:, :], PI / 2, ALU.add)
                        fw.ts("dve", tA[:, :], ang[:, :], 1.0 / TWO_PI, ALU.mult)
                        fw.copy("dve", tI[:, :], tA[:, :])
                        fw.copy("dve", tA[:, :], tI[:, :])
                        fw.stt("dve", tB[:, :], tA[:, :], -TWO_PI, ang[:, :], ALU.mult, ALU.add)
                        fw.ts("dve", tA[:, :], tB[:, :], PI, ALU.is_gt)
                        fw.stt("dve", tB[:, :], tA[:, :], -TWO_PI, tB[:, :], ALU.mult, ALU.add)
                        fw.ts("dve", tA[:, :], tB[:, :], -PI, ALU.is_lt)
                        fw.stt("dve", tB[:, :], tA[:, :], TWO_PI, tB[:, :], ALU.mult, ALU.add)
                        fw.ts("dve", tB[:, :], tB[:, :], 3.141592, ALU.min, -3.141592, ALU.max)
                        tb3 = tB[:, :].rearrange("p (n e) -> p n e", e=8)
                        if which == 0:
                            fw.act(snn[g][:, :, 8:16], tb3, AF.Sin)
                            fw.ts("dve", snn[g][:, :, 0:8], snn[g][:, :, 8:16], -1.0, ALU.mult)
                        else:
                            fw.act(csn[g][:, :, 0:8], tb3, AF.Sin)
                            fw.copy("dve", csn[g][:, :, 8:16], csn[g][:, :, 0:8])

            build_table(0)
            wdr = Rot([sb(st, "wd%d" % i, (128, 8, 3, 512), BF16) for i in range(1)])
            sqr = Rot([sb(st, "sq%d" % i, (128, 1024), F32) for i in range(2)])
            s16r = Rot([sb(st, "s16_%d" % i, (128, 48), F32) for i in range(3)])
            qk1r = Rot([sb(st, "qk1_%d" % i, (128, 16, 64), F32) for i in range(3)])
            qk2r = Rot([sb(st, "qk2_%d" % i, (128, 16, 64), F32) for i in range(2)])
            Ar = Rot([sb(st, "ropeA%d" % i, (128, 16, 16), F32) for i in range(2)])
            Br = Rot([sb(st, "ropeB%d" % i, (128, 16, 16), F32) for i in range(2)])
            qkbr = Rot([sb(st, "qkb%d" % i, (128, 16, 64), BF16) for i in range(3)])
            oT4r = Rot([sb(st, "oT4_%d" % i, (128, 8, 512), BF16) for i in range(2)])
            vt4r = Rot([sb(st, "vt4_%d" % i, (128, 4, 8, 65), BF16) for i in range(2)])
            for vt in vt4r.items:
                fw.memset("pool", vt[:, :, :, :], 1.0)
            wds = {}
            c0items = [(g, n) for g in range(3) for n in range(32)]
            hold = {}

            def c0s1(it, c):
                g, n = c0items[it]
                if it == 6:
                    build_table(1)
                if it == 14:
                    build_table(2)
                dil = DILS[g]
                nb_ = 32 // dil
                if n == 0:
                    wd = wdr.next()
                    for j, off in enumerate((OFF_QB, OFF_KB, OFF_VB)):
                        fw.dma(wd[:, :, j, :], win_dil[:, :, off + g * 512:off + (g + 1) * 512])
                    wds[g] = wd
                wd = wds[g]
                r, bb = n // nb_, n % nb_
                t0 = dil * 128 * bb + r

                def lhsT(kc):
                    return V(xn_ro, xn_t.t[:, kc, t0:t0 + 127 * dil + 1:dil])
                pq_ = pqk.next()
                pv_ = pvv.next()
                for j in range(2):
                    for kc in range(8):
                        fw.mm(pq_[:, j * 512:(j + 1) * 512], lhsT(kc), wd[:, kc, j, :], start=(kc == 0), stop=(kc == 7))
                for kc in range(8):
                    fw.mm(pv_[:, :], lhsT(kc), wd[:, kc, 2, :], start=(kc == 0), stop=(kc == 7))
                sq = sqr.next()
                fw.act(sq[:, :], pq_[:, :], AF.Square)
                s16 = s16r.next()
                fw.reduce("dve", s16[:, 0:16], sq[:, :].rearrange("p (a d) -> p a d", a=16), ALU.add)
                fw.act(s16[:, 16:32], s16[:, 0:16], AF.Ln, scale=1.0 / 64, bias=eps_c)
                fw.act(s16[:, 32:48], s16[:, 16:32], AF.Exp, scale=-0.5)
                qk1 = qk1r.next()
                fw.tt("dve", qk1[:, :, :], pq_[:, :].rearrange("p (a d) -> p a d", a=16),
                      s16[:, 32:48].raw(0, [[1, 16], [0, 64]]), ALU.mult)
                c["qk1"] = qk1
                m4 = n % 4
                if m4 == 0:
                    hold["vt4"] = vt4r.next()
                vt4 = hold["vt4"]
                fw.copy("dve", vt4[:, m4, :, 0:64], pv_[:, :].rearrange("p (a d) -> p a d", a=8))
                if m4 == 3:
                    fw.dma(vaug_d[g].full_v()[:, n - 3:n + 1, :, :], vt4[:, :, :, :], join=True)

            def c0s2(it, c):
                g, n = c0items[it]
                qk1 = c["qk1"]
                qk2 = qk2r.next()
                fw.tt("pool", qk2[:, :, :], qk1[:, :, :], gqk[:, :, :], ALU.mult)
                A = Ar.next()
                B = Br.next()
                fw.tt("pool", A[:, :, :], qk2[:, :, 0:16], csn[g][:, n, :].raw(0, [[0, 16], [1, 16]]), ALU.mult)
                fw.tt("pool", B[:, :, 0:8], qk2[:, :, 8:16], snn[g][:, n, 0:8].raw(0, [[0, 16], [1, 8]]), ALU.mult)
                fw.tt("pool", B[:, :, 8:16], qk2[:, :, 0:8], snn[g][:, n, 8:16].raw(0, [[0, 16], [1, 8]]), ALU.mult)
                qkb = qkbr.next()
                fw.copy("act", qkb[:, :, 16:64], qk2[:, :, 16:64])
                fw.tt("dve", qkb[:, :, 0:16], A[:, :, :], B[:, :, :], ALU.add)
                c["qkb"] = qkb

            def c0s3(it, c):
                g, n = c0items[it]
                qkb = c["qkb"]
                pt_ = ptt.next()
                ptb = pt_[:, :].bitcast(BF16)
                qkb2 = qkb[:, :, :].rearrange("p a d -> p (a d)")
                for cc in range(8):
                    fw.tr(ptb[:, cc * 128:(cc + 1) * 128], qkb2[:, cc * 128:(cc + 1) * 128], ident_b)
                m4 = n % 4
                if m4 == 0:
                    hold["oT4"] = oT4r.next()
                oT4 = hold["oT4"]
                fw.copy("act", oT4[:, :, m4 * 128:(m4 + 1) * 128], ptb.rearrange("p (c t) -> p c t", c=8))
                if m4 == 3:
                    n0 = n - 3
                    dst = qkT_d[g].full_v().rearrange("h p a t -> p a h t")[:, :, :, n0 * 128:(n0 + 4) * 128]
                    fw.dma(dst, oT4[:, :, :].rearrange("p (a h) t -> p a h t", a=2), join=True)

            run_skewed(len(c0items), [c0s1, c0s2, c0s3])
            fw.barrier()

    def phase_c1():
        with ExitStack() as st:
            mask4 = sb(st, "mask4", (128, 4, 128), BF16)
            for a_ in range(4):
                fw.copy("dve", mask4[:, a_, :], cstb[:, 1 if a_ % 2 == 0 else 3, :])
            mask4f = mask4[:, :, :].rearrange("p a i -> p (a i)")
            obT_v = obT_d.full_v()
            accs = [[sb(st, "acc%d_%d" % (k_, hh), (65, S), F32) for hh in range(2)] for k_ in range(2)]
            norm_jobs = []
            qkTr = Rot([sb(st, "qkT%d" % i, (128, 2, S), BF16) for i in range(2)])
            vaugr = Rot([sb(st, "vaug%d" % i, (128, 32, 2, 65), BF16) for i in range(2)])
            pS = Rot([ps(st, "pS%d" % i, (128, 512), F32) for i in range(3)])
            pO = Rot([ps(st, "pO%d" % i, (128, 512), F32) for i in range(3)])
            pB = Rot([ps(st, "pB%d" % i, (128, 512), F32) for i in range(2)])
            ptr_ = Rot([sb(st, "pTs%d" % i, (128, 512), BF16) for i in range(7)])
            obr = Rot([sb(st, "ob%d" % i, (64, 512), BF16) for i in range(2)])
            rbr = Rot([sb(st, "rb%d" % i, (64, 512), F32) for i in range(2)])

            def loads(hp, g):
                qkT = qkTr.next()
                va = vaugr.next()
                fw.dma(qkT[:, :, :], V(qkT_d[g], qkT_d[g].t[hp]))
                fw.dma(va[:, :, :, :], vaug_d[g].full_v()[:, :, hp * 2:hp * 2 + 2, :])
                return qkT, va

            seq = [(hp, g) for hp in range(4) for g in range(3)]
            bufs = {0: loads(*seq[0])}
            items = [(idx, hh, n0) for idx in range(len(seq)) for hh in range(2) for n0 in range(0, 32, 2)]

            def s1(it, c):
                idx, hh, n0 = items[it]
                hp, g = seq[idx]
                nb_ = 32 // DILS[g]
                qkT, vaug = bufs[idx]
                hs = slice(hh * 64, (hh + 1) * 64)
                b0 = n0 % nb_
                pS_ = pS.next()
                for q_ in range(2):
                    n = n0 + q_
                    qv = qkT[hs, 0, n * 128:(n + 1) * 128]
                    fw.mm(pS_[:, (2 * q_) * 128:(2 * q_ + 1) * 128], qkT[hs, 1, n * 128:(n + 1) * 128], qv)
                    if b0 + q_ > 0:
                        fw.mm(pS_[:, (2 * q_ + 1) * 128:(2 * q_ + 2) * 128], qkT[hs, 1, (n - 1) * 128:n * 128], qv)
                pT_ = ptr_.next()
                meng = "pool" if (it % 3) != 2 else "dve"
                if b0 == 0:
                    fw.act(pT_[:, 0:128], pS_[:, 0:128], AF.Exp, scale=0.125)
                    fw.act(pT_[:, 256:512], pS_[:, 256:512], AF.Exp, scale=0.125)
                    fw.tt(meng, pT_[:, 0:128], pT_[:, 0:128], mask4f[:, 0:128], ALU.mult)
                    fw.tt(meng, pT_[:, 256:512], pT_[:, 256:512], mask4f[:, 256:512], ALU.mult)
                else:
                    fw.act(pT_[:, :], pS_[:, :], AF.Exp, scale=0.125)
                    fw.tt(meng, pT_[:, :], pT_[:, :], mask4f, ALU.mult)
                c["pT"] = pT_

            def do_norm(hp_, h2_, blk):
                acc_ = accs[hp_ % 2]
                pB_ = pB.next()
                fw.mm(pB_[0:64, :], cst[64:65, 6, 0:64], acc_[h2_][64:65, blk * 512:(blk + 1) * 512])
                ob = obr.next()
                rb = rbr.next()
                fw.recip(rb[:, :], pB_[0:64, :])
                fw.tt("dve", ob[:, :], acc_[h2_][0:64, blk * 512:(blk + 1) * 512], rb[:, :], ALU.mult)
                hg = hp_ * 2 + h2_
                fw.dma(obT_v[hg * 64:(hg + 1) * 64, blk * 512:(blk + 1) * 512], ob[:, :], join=True)

            def s2(it, c):
                idx, hh, n0 = items[it]
                hp, g = seq[idx]
                dil = DILS[g]
                nb_ = 32 // dil
                qkT, vaug = bufs[idx]
                r, b0 = n0 // nb_, n0 % nb_
                pT_ = c["pT"]
                if hh == 0 and n0 == 0 and idx + 1 < len(seq):
                    bufs[idx + 1] = loads(*seq[idx + 1])
                pO_ = pO.next()
                for q_ in range(2):
                    n = n0 + q_
                    bb = b0 + q_
                    fw.mm(pO_[0:65, q_ * 128:(q_ + 1) * 128], vaug[:, n, hh, :],
                          pT_[:, (2 * q_) * 128:(2 * q_ + 1) * 128], start=True, stop=(bb == 0))
                    if bb > 0:
                        fw.mm(pO_[0:65, q_ * 128:(q_ + 1) * 128], vaug[:, n - 1, hh, :],
                              pT_[:, (2 * q_ + 1) * 128:(2 * q_ + 2) * 128], start=False, stop=True)
                a0 = dil * 128 * b0 + r
                acc = accs[hp % 2]
                av = acc[hh][:, :].raw(a0, [[dil, 256]])
                if g == 0:
                    fw.copy("dve", av, pO_[0:65, 0:256])
                else:
                    fw.tt("dve", av, av, pO_[0:65, 0:256], ALU.add)
                if norm_jobs:
                    do_norm(*norm_jobs.pop(0))
                if g == 2 and hh == 1 and n0 == 30:
                    for h2_ in range(2):
                        for blk in range(8):
                            norm_jobs.append((hp, h2_, blk))

            run_skewed(len(items), [s1, None, None, s2])
            while norm_jobs:
                do_norm(*norm_jobs.pop(0))
            fw.barrier()

    stB1w.close()
    phase_c0()
    phase_c1()
    if dbg is not None and dbg[0] == "C1":
        with ExitStack() as st:
            tmp = sb(st, "dbgt", (128, 4, 512), BF16)
            sv = obT_d.full_v().rearrange("(c p) t -> p c t", p=128)
            dv = dbg_out.full_v().rearrange("(c p) t -> p c t", p=128)
            for j in range(8):
                fw.dma(tmp[:, :, :], sv[:, :, j * 512:(j + 1) * 512])
                fw.dma(dv[:, :, j * 512:(j + 1) * 512], tmp[:, :, :])
            fw.barrier()
        return finish()

    def phase_t1a():
        with ExitStack() as st:
            wbr = sb(st, "wbr", (128, 8, 1024), BF16)
            wbd = sb(st, "wbd", (128, 4, 1024), BF16)
            wmg = sb(st, "wmg", (128, 8, 2048), BF16)
            wmo = sb(st, "wmo", (128, 8, 1024), BF16)
            fw.dma(wbr[:, :, :], WB["w_br_gla"].full_v().rearrange("(c p) n -> p c n", p=128))
            fw.dma(wbd[:, :, :], WB["w_br_dil"].full_v().rearrange("(c p) n -> p c n", p=128))
            fw.dma(wmg[:, :, :], WB["w_merge_gate"].full_v().rearrange("(c p) n -> p c n", p=128))
            fw.dma(wmo[:, :, :], WB["w_mix_out"].full_v().rearrange("(c p) n -> p c n", p=128))
            bmg = sb(st, "bmg", (1, 2048), BF16)
            fw.dma(bmg[:, :], SM["b_merge_gate"].full_v().rearrange("(o n) -> o n", o=1), q="pool")
            P = Rot([ps(st, "qP%d" % i, (128, 512), F32) for i in range(8)])
            nb = norm_bufs(st, "T") + (P,)
            oatr = Rot([sb(st, "oat%d" % i, (128, 8, 128), BF16) for i in range(2)])
            obtr = Rot([sb(st, "obt%d" % i, (128, 4, 128), BF16) for i in range(2)])
            xtr = Rot([sb(st, "xtT%d" % i, (128, 1024), F32) for i in range(3)])
            gar = Rot([sb(st, "ga%d" % i, (128, 512), F32) for i in range(3)])
            mar = Rot([sb(st, "ma%d" % i, (128, 1024), F32) for i in range(3)])
            mrr = Rot([sb(st, "mr%d" % i, (128, 1024), BF16) for i in range(2)])
            mTr = Rot([sb(st, "mT%d" % i, (128, 8, 128), BF16) for i in range(2)])
            h1r = Rot([sb(st, "h1t%d" % i, (128, 1024), F32) for i in range(2)])
            oaT_v = oaT_d.full_v().rearrange("(c p) t -> p c t", p=128)
            obT_v = obT_d.full_v().rearrange("(c p) t -> p c t", p=128)

            def sL(i, c):
                c["oat"] = oatr.next()
                c["obt"] = obtr.next()
                c["xt"] = xtr.next()
                fw.dma(c["oat"][:, :, :], oaT_v[:, :, i * 128:(i + 1) * 128])
                fw.dma(c["obt"][:, :, :], obT_v[:, :, i * 128:(i + 1) * 128])
                fw.dma(c["xt"][:, :], x[i * 128:(i + 1) * 128, :])

            def s1(i, c):
                oat, obt = c["oat"], c["obt"]
                xnv = V(xn_sub[i // 4], xn_t.t[:, :, i * 128:(i + 1) * 128])
                ms = []
                for br in range(2):
                    ma = mar.next()
                    for half in range(2):
                        cs_ = slice(half * 512, (half + 1) * 512)
                        gs_ = slice(br * 1024 + half * 512, br * 1024 + (half + 1) * 512)
                        py = P.next()
                        if br == 0:
                            for kc in range(8):
                                fw.mm(py[:, :], oat[:, kc, :], wbr[:, kc, cs_], start=(kc == 0), stop=(kc == 7))
                        else:
                            for kc in range(4):
                                fw.mm(py[:, :], obt[:, kc, :], wbd[:, kc, cs_], start=(kc == 0), stop=(kc == 3))
                        pg = P.next()
                        for kc in range(8):
                            fw.mm(pg[:, :], xnv[:, kc, :], wmg[:, kc, gs_], start=(kc == 0), stop=False)
                        fw.mm(pg[:, :], ones_row_b, bmg[0:1, gs_], start=False, stop=True)
                        ga = gar.next()
                        fw.act(ga[:, :], pg[:, :], AF.Sigmoid)
                        fw.tt("dve", ma[:, cs_], py[:, :], ga[:, :], ALU.mult)
                    ms.append(ma)
                mr = mrr.next()
                fw.tt("pool", mr[:, :], ms[0][:, :], ms[1][:, :], ALU.add)
                c["mr"] = mr

            def s2(i, c):
                mr, xt = c["mr"], c["xt"]
                pt = P.next()
                ptb = pt[:, :].bitcast(BF16)
                for cc in range(8):
                    fw.tr(ptb[:, cc * 128:(cc + 1) * 128], mr[:, cc * 128:(cc + 1) * 128], ident_b)
                mT = mTr.next()
                fw.copy("act", mT[:, :, :], ptb.rearrange("p (c t) -> p c t", c=8))
                h1t = h1r.next()
                for half in range(2):
                    cs_ = slice(half * 512, (half + 1) * 512)
                    pm = P.next()
                    for kc in range(8):
                        fw.mm(pm[:, :], mT[:, kc, :], wmo[:, kc, cs_], start=(kc == 0), stop=(kc == 7))
                    fw.tt("dve", h1t[:, cs_], pm[:, :], xt[:, cs_], ALU.add)
                fw.dma(h1_d[i * 128:(i + 1) * 128, :], h1t[:, :], join=True)
                c["xs"] = norm_a(nb, h1t[:, :])

            def s3(i, c):
                xnv = V(xn_sub[i // 4], xn_t.t[:, :, i * 128:(i + 1) * 128])
                norm_b(nb, c["xs"], gcol["norm_x"], xnv)

            sL(0, None) if False else None
            ctxs = [dict() for _ in range(NT)]
            sL(0, ctxs[0])
            for step in range(NT + 2):
                if step + 1 < NT:
                    sL(step + 1, ctxs[step + 1])
                if step < NT:
                    s1(step, ctxs[step])
                if 0 <= step - 1 < NT:
                    s2(step - 1, ctxs[step - 1])
                if 0 <= step - 2 < NT:
                    s3(step - 2, ctxs[step - 2])
            fw.barrier()

    def phase_t1b():
        with ExitStack() as st:
            wxq = sb(st, "wxq", (128, 8, 1024), BF16)
            wxo = sb(st, "wxo", (128, 8, 1024), BF16)
            fw.dma(wxq[:, :, :], WB["w_xq"].full_v().rearrange("(c p) n -> p c n", p=128))
            fw.dma(wxo[:, :, :], WB["w_xo"].full_v().rearrange("(c p) n -> p c n", p=128))
            gq = sb(st, "gqx", (128, 256), F32)
            gk = sb(st, "gkx", (128, 256), F32)
            fw.dma(gq[:, :], dram_bc(SM["x_q_norm"], 256))
            fw.dma(gk[:, :], dram_bc(SM["x_k_norm"], 256))
            kmT = sb(st, "kmT", (128, 8, 256), BF16)
            Vm = sb(st, "Vm", (128, 2, 1024), BF16)
            PB = Rot([ps(st, "rP%d" % i, (128, 512), F32) for i in range(8)])
            p1 = PB
            nb = norm_bufs(st, "X") + (PB,)
            ssr = Rot([sb(st, "ssXh%d" % i, (128, 12), F32) for i in range(3)])
            jkr = Rot([sb(st, "jkXh%d" % i, (128, 256), BF16) for i in range(2)])
            qnr = Rot([sb(st, "qn%d" % i, (128, 1024), BF16) for i in range(2)])

            def head_norm(psrc, gain, dst):
                ss = ssr.next()
                jk = jkr.next()
                for h in range(4):
                    fw.act(jk[:, :], psrc[:, h * 256:(h + 1) * 256], AF.Square, accum=ss[:, h:h + 1])
                fw.act(ss[:, 4:8], ss[:, 0:4], AF.Ln, scale=1.0 / 256, bias=eps_c)
                fw.act(ss[:, 8:12], ss[:, 4:8], AF.Exp, scale=-0.5)
                for h in range(4):
                    fw.stt("dve", dst[:, h * 256:(h + 1) * 256], psrc[:, h * 256:(h + 1) * 256], ss[:, 8 + h:9 + h],
                           gain[:, :], ALU.mult, ALU.mult)

            with ExitStack() as st2:
                wkv = sb(st2, "wkv", (128, 8, 2048), BF16)
                fw.dma(wkv[:, :, :], WB["w_xkv"].full_v().rearrange("(c p) n -> p c n", p=128))
                mnT = sb(st2, "mnT", (128, 8, 256), BF16)
                mtr = Rot([sb(st2, "mt%d" % i, (128, 1024), F32) for i in range(2)])
                for m in range(2):
                    mt = mtr.next()
                    fw.dma(mt[:, :], mem[m * 128:(m + 1) * 128, :])
                    norm_T(nb, mt[:, :], gcol["norm_mem"], mnT[:, :, m * 128:(m + 1) * 128])
                for m in range(2):
                    pk = []
                    for half in range(2):
                        cs_ = slice(half * 512, (half + 1) * 512)
                        p_ = PB.next()
                        for kc in range(8):
                            fw.mm(p_[:, :], mnT[:, kc, m * 128:(m + 1) * 128], wkv[:, kc, cs_],
                                  start=(kc == 0), stop=(kc == 7))
                        pk.append(p_)
                        pv_ = PB.next()
                        for kc in range(8):
                            fw.mm(pv_[:, :], mnT[:, kc, m * 128:(m + 1) * 128],
                                  wkv[:, kc, 1024 + half * 512:1024 + (half + 1) * 512], start=(kc == 0), stop=(kc == 7))
                        fw.copy("act", Vm[:, m, cs_], pv_[:, :])
                    srcs = [pk[h // 2][:, (h % 2) * 256:(h % 2 + 1) * 256] for h in range(4)]
                    hnk = HeadNorm(st2, "K%d" % m, 1)
                    ssk = hnk.stats(srcs)
                    kn = qnr.next()
                    for h in range(4):
                        fw.stt("dve", kn[:, h * 256:(h + 1) * 256], srcs[h], ssk[:, 8 + h:9 + h], gk[:, :], ALU.mult, ALU.mult)
                    pt = PB.next()
                    ptb = pt[:, :].bitcast(BF16)
                    for cc in range(8):
                        fw.tr(ptb[:, cc * 128:(cc + 1) * 128], kn[:, cc * 128:(cc + 1) * 128], ident_b)
                    fw.copy("act", kmT[:, :, m * 128:(m + 1) * 128], ptb.rearrange("p (c t) -> p c t", c=8))
                fw.barrier()

            h1r = Rot([sb(st, "h1x%d" % i, (128, 1024), F32) for i in range(5)])
            qTr = Rot([sb(st, "qTx%d" % i, (128, 8, 128), BF16) for i in range(2)])
            ptsr = Rot([sb(st, "ptsx%d" % i, (128, 1024), BF16) for i in range(2)])
            rdr = Rot([sb(st, "rdx%d" % i, (128, 512), F32) for i in range(2)])
            oxr = Rot([sb(st, "oxT%d" % i, (128, 8, 128), BF16) for i in range(2)])
            h2r = Rot([sb(st, "h2x%d" % i, (128, 1024), F32) for i in range(2)])
            qn2r = Rot([sb(st, "qn2_%d" % i, (128, 1024), BF16) for i in range(2)])
            ones_b = cstb[:, 6, :]
            hn = HeadNorm(st, "X")

            def sL(i, c):
                c["h1"] = h1r.next()
                fw.dma(c["h1"][:, :], h1_d[i * 128:(i + 1) * 128, :])

            def s1(i, c):
                xnv = V(xn_ro, xn_t.t[:, :, i * 128:(i + 1) * 128])
                pq = []
                for half in range(2):
                    cs_ = slice(half * 512, (half + 1) * 512)
                    p_ = PB.next()
                    for kc in range(8):
                        fw.mm(p_[:, :], xnv[:, kc, :], wxq[:, kc, cs_], start=(kc == 0), stop=(kc == 7))
                    pq.append(p_)
                srcs = [pq[h // 2][:, (h % 2) * 256:(h % 2 + 1) * 256] for h in range(4)]
                ss = hn.stats(srcs)
                qn = qn2r.next()
                for h in range(4):
                    fw.stt("dve", qn[:, h * 256:(h + 1) * 256], srcs[h], ss[:, 8 + h:9 + h], gq[:, :], ALU.mult, ALU.mult)
                c["qn"] = qn

            def s2(i, c):
                qn = c["qn"]
                pt = PB.next()
                ptb = pt[:, :].bitcast(BF16)
                for cc in range(8):
                    fw.tr(ptb[:, cc * 128:(cc + 1) * 128], qn[:, cc * 128:(cc + 1) * 128], ident_b)
                qT = qTr.next()
                fw.copy("act", qT[:, :, :], ptb.rearrange("p (c t) -> p c t", c=8))
                pts = ptsr.next()
                for hb in range(2):
                    pS2 = PB.next()
                    for hl in range(2):
                        h = hb * 2 + hl
                        for mb in range(2):
                            sl = slice((hl * 2 + mb) * 128, (hl * 2 + mb + 1) * 128)
                            for dc in range(2):
                                fw.mm(pS2[:, sl], kmT[:, h * 2 + dc, mb * 128:(mb + 1) * 128], qT[:, h * 2 + dc, :],
                                      start=(dc == 0), stop=(dc == 1))
                    fw.act(pts[:, hb * 512:(hb + 1) * 512], pS2[:, :], AF.Exp, scale=1.0 / 16)
                c["pts"] = pts

            def s3(i, c):
                pts = c["pts"]
                pD = PB.next()
                for h in range(4):
                    for mb in range(2):
                        fw.mm(pD[:, h * 128:(h + 1) * 128], ones_b, pts[:, (h * 2 + mb) * 128:(h * 2 + mb + 1) * 128],
                              start=(mb == 0), stop=(mb == 1))
                rd = rdr.next()
                fw.recip(rd[:, :], pD[:, :])
                ox = oxr.next()
                for hb in range(2):
                    pO2 = PB.next()
                    for hl in range(2):
                        h = hb * 2 + hl
                        for dc in range(2):
                            sl = slice((hl * 2 + dc) * 128, (hl * 2 + dc + 1) * 128)
                            for mb in range(2):
                                fw.mm(pO2[:, sl], Vm[:, mb, h * 256 + dc * 128:h * 256 + (dc + 1) * 128],
                                      pts[:, (h * 2 + mb) * 128:(h * 2 + mb + 1) * 128], start=(mb == 0), stop=(mb == 1))
                    fw.tt("dve", ox[:, hb * 4:(hb + 1) * 4, :].rearrange("p (h c) t -> p h c t", h=2),
                          pO2[:, :].rearrange("p (h c t) -> p h c t", h=2, c=2),
                          rd[:, hb * 256:(hb + 1) * 256].raw(0, [[128, 2], [0, 2], [1, 128]]), ALU.mult)
                c["ox"] = ox

            def s4(i, c):
                ox, h1t = c["ox"], c["h1"]
                h2t = h2r.next()
                for half in range(2):
                    cs_ = slice(half * 512, (half + 1) * 512)
                    px = PB.next()
                    for kc in range(8):
                        fw.mm(px[:, :], ox[:, kc, :], wxo[:, kc, cs_], start=(kc == 0), stop=(kc == 7))
                    fw.tt("dve", h2t[:, cs_], px[:, :], h1t[:, cs_], ALU.add)
                fw.dma(h2_d[i * 128:(i + 1) * 128, :], h2t[:, :], join=True)

            ctxs = [dict() for _ in range(NT)]
            sL(0, ctxs[0])
            for step in range(NT + 3):
                if step + 1 < NT:
                    sL(step + 1, ctxs[step + 1])
                for k, fn in enumerate((s1, s2, s3, s4)):
                    i = step - k
                    if 0 <= i < NT:
                        fn(i, ctxs[i])
            fw.barrier()

    def phase_t2():
        with ExitStack() as st:
            wup = sb(st, "wup", (128, 8, 2 * DFF), BF16)
            wdn = sb(st, "wdn", (128, NFC, 1024), BF16)
            wcb = sb(st, "wcb", (128, 4 * NFC), F32)
            crow = sb(st, "crow", (4 * NFC, 128), F32)
            fw.dma(crow[0:3 * NFC, :], SM["w_ffn_conv"].full_v().rearrange("t (c p) -> (t c) p", p=128))
            fw.dma(crow[3 * NFC:4 * NFC, :], SM["b_ffn_conv"].full_v().rearrange("(c p) -> c p", p=128))
            hist = sb(st, "hist", (128, NFC, 2), F32)
            hsub = [hist.sub("hist%d" % fc) for fc in range(NFC)]
            for fc in range(NFC):
                fw.memset("pool", V(hsub[fc], hist.t[:, fc, :]), 0.0)
            p1 = Rot([ps(st, "s1_%d" % i, (128, 512), F32) for i in range(4)])
            p2 = Rot([ps(st, "s2_%d" % i, (128, 1024), F32) for i in range(2)])
            nb = norm_bufs(st, "F", 1, 4) + (p1,)
            cps = p1.next()
            fw.tr(cps[:, 0:4 * NFC], crow[:, :], cst[0:4 * NFC, 0, 0:4 * NFC])
            fw.copy("dve", wcb[:, :], cps[:, 0:4 * NFC])
            h2r = Rot([sb(st, "h2f%d" % i, (128, 1024), F32) for i in range(2)])
            x3r = Rot([sb(st, "x3T%d" % i, (128, 8, 512), BF16) for i in range(1)])
            abr = Rot([sb(st, "abt%d" % i, (128, 514), F32) for i in range(2)])
            c1r = Rot([sb(st, "cv1_%d" % i, (128, 512), F32) for i in range(2)])
            c2r = Rot([sb(st, "cv2_%d" % i, (128, 512), F32) for i in range(2)])
            glr = Rot([sb(st, "gl%d" % i, (128, 512), F32) for i in range(2)])
            yT = sb(st, "yT", (128, NFC, 512), BF16)
            otr = Rot([sb(st, "ot%d" % i, (128, 1024), F32) for i in range(1)])

            def stage_na(s_):
                xsl = []
                for t_ in range(4):
                    i = s_ * 4 + t_
                    h2t = h2r.next()
                    fw.dma(h2t[:, :], h2_d[i * 128:(i + 1) * 128, :])
                    xsl.append(norm_a(nb, h2t[:, :]))
                return xsl

            def stage_nb(xsl):
                x3 = x3r.next()
                for t_ in range(4):
                    norm_b(nb, xsl[t_], gcol["norm_ffn"], x3[:, :, t_ * 128:(t_ + 1) * 128])
                return x3

            xsl0 = stage_na(0)
            wupv = WB["w_ffn_up"].full_v().rearrange("(c p) n -> p c n", p=128)
            for cb in range(4):
                for base in (0, DFF):
                    c0_ = base + cb * 704
                    fw.dma(wup[:, :, c0_:c0_ + 704], wupv[:, :, c0_:c0_ + 704])
            wdnv = WB["w_ffn_down"].full_v().rearrange("(c p) n -> p c n", p=128)
            for c0 in range(0, NFC, 6):
                c1_ = min(NFC, c0 + 6)
                fw.dma(wdn[:, c0:c1_, :], wdnv[:, c0:c1_, :])
            x3n = stage_nb(xsl0)
            for s_ in range(NS):
                x3 = x3n
                xsl_next = [] if s_ + 1 < NS else None
                pend = {}
                for fc in range(NFC):
                    if xsl_next is not None and fc % 5 == 0 and fc // 5 < 4:
                        t_ = fc // 5
                        i_ = (s_ + 1) * 4 + t_
                        h2n = h2r.next()
                        fw.dma(h2n[:, :], h2_d[i_ * 128:(i_ + 1) * 128, :])
                        pend[t_] = h2n
                    if xsl_next is not None and fc % 5 == 3 and fc // 5 < 4:
                        xsl_next.append(norm_a(nb, pend[fc // 5][:, :]))
                    pa = p1.next()
                    for kc in range(8):
                        fw.mm(pa[:, :], wup[:, kc, fc * 128:(fc + 1) * 128], x3[:, kc, :], start=(kc == 0), stop=(kc == 7))
                    pu = p1.next()
                    for kc in range(8):
                        fw.mm(pu[:, :], wup[:, kc, DFF + fc * 128:DFF + (fc + 1) * 128], x3[:, kc, :],
                              start=(kc == 0), stop=(kc == 7))
                    ab = abr.next()
                    hv = V(hsub[fc], hist.t[:, fc, :])
                    fw.copy("pool", ab[:, 0:2], hv)
                    fw.copy("act", ab[:, 2:514], pa[:, :])
                    c1 = c1r.next()
                    c2 = c2r.next()
                    fw.ts("dve", c1[:, :], ab[:, 2:514], wcb[:, 2 * NFC + fc:2 * NFC + fc + 1], ALU.mult, wcb[:, 3 * NFC + fc:3 * NFC + fc + 1], ALU.add)
                    fw.stt("dve", c2[:, :], ab[:, 1:513], wcb[:, NFC + fc:NFC + fc + 1], c1[:, :], ALU.mult, ALU.add)
                    fw.stt("dve", c1[:, :], ab[:, 0:512], wcb[:, fc:fc + 1], c2[:, :], ALU.mult, ALU.add)
                    fw.copy("pool", hv, ab[:, 512:514])
                    gl = glr.next()
                    fw.act(gl[:, :], c1[:, :], AF.Gelu)
                    fw.tt("dve", yT[:, fc, :], gl[:, :], pu[:, :], ALU.mult)
                if xsl_next is not None:
                    x3n = stage_nb(xsl_next)
                for t_ in range(4):
                    i = s_ * 4 + t_
                    h2t = h2r.next()
                    fw.dma(h2t[:, :], h2_d[i * 128:(i + 1) * 128, :])
                    pdn = p2.next()
                    for half in range(2):
                        cs_ = slice(half * 512, (half + 1) * 512)
                        for fc in range(NFC):
                            fw.mm(pdn[:, cs_], yT[:, fc, t_ * 128:(t_ + 1) * 128], wdn[:, fc, cs_],
                                  start=(fc == 0), stop=(fc == NFC - 1))
                    ot = otr.next()
                    fw.tt("dve", ot[:, :], pdn[:, :], h2t[:, :], ALU.add)
                    fw.dma(out[i * 128:(i + 1) * 128, :], ot[:, :], join=True)
            fw.barrier()

    def dump_f32(src_d):
        with ExitStack() as st:
            tmp = Rot([sb(st, "dbgf%d" % i, (128, 1024), F32) for i in range(2)])
            for i in range(NT):
                t_ = tmp.next()
                fw.dma(t_[:, :], src_d[i * 128:(i + 1) * 128, :])
                fw.dma(dbg_out[i * 128:(i + 1) * 128, :], t_[:, :])
            fw.barrier()

    phase_t1a()
    if dbg is not None and dbg[0] == "T1a":
        dump_f32(h1_d)
        return finish()
    phase_t1b()
    if dbg is not None and dbg[0] == "T1b":
        dump_f32(h2_d)
        return finish()
    xstack.close()
    phase_t2()
    return finish()


def make_in_maps(inputs):
    consts = make_consts()
    maps = []
    for b in range(8):
        m = {"x": np.ascontiguousarray(inputs["x"][b]), "mem": np.ascontiguousarray(inputs["mem"][b]),
             "positions": np.ascontiguousarray(inputs["positions"][b]).astype(np.int32), "consts": consts}
        for k in BIG_W:
            m[k] = np.ascontiguousarray(np.asarray(inputs[k])[0])
        for k in SMALL:
            m[k] = np.ascontiguousarray(np.asarray(inputs[k])[0])
        maps.append(m)
    return maps


def kernel(**inputs):
    inputs = {k: np.asarray(v) for k, v in inputs.items()}
    nc, fw = build()
    res = run_bass_kernel_spmd(nc, make_in_maps(inputs), core_ids=list(range(8)))
    return np.stack([np.asarray(r["out"]) for r in res.results], axis=0).astype(np.float32)
```

```python
import numpy as np
import concourse.bass as bass
import concourse.mybir as mybir

F32 = mybir.dt.float32
BF16 = mybir.dt.bfloat16
I32 = mybir.dt.int32
AF = mybir.ActivationFunctionType
ALU = mybir.AluOpType
AX = mybir.AxisListType

SAME_ENGINE_SYNC = True
N_DMA_SEMS = 24


class Buf:
    __slots__ = ("t", "w", "r", "name")

    def __init__(self, t, name=""):
        self.t = t
        self.w = []
        self.r = {}
        self.name = name

    def sub(self, name=""):
        return Buf(self.t, name)

    def __getitem__(self, idx):
        return V(self, self.t[idx])

    def full_v(self):
        return V(self, self.t)


class V:
    __slots__ = ("b", "ap")

    def __init__(self, b, ap):
        self.b = b
        self.ap = ap

    def __getitem__(self, idx):
        return V(self.b, self.ap[idx])

    def rearrange(self, s, **kw):
        return V(self.b, self.ap.rearrange(s, **kw))

    def bcast(self, shape):
        return V(self.b, self.ap.to_broadcast(shape))

    def bitcast(self, dt):
        return V(self.b, self.ap.bitcast(dt))

    def raw(self, extra_off, dims):
        a = self.ap
        return V(self.b, bass.AP(a.tensor, a.offset + extra_off, [list(a.ap[0])] + [list(d) for d in dims]))

    @property
    def shape(self):
        return self.ap.shape


class Instr:
    __slots__ = ("stream", "fn", "deps", "signal", "semval", "dma", "dsem", "dval", "idx")

    def __init__(self, stream, fn, dma=False):
        self.stream = stream
        self.fn = fn
        self.deps = []
        self.signal = False
        self.semval = None
        self.dma = dma
        self.dsem = None
        self.dval = None


class FW:
    STREAMS = ("pe", "act", "dve", "pool", "sp")

    def __init__(self, nc):
        self.nc = nc
        self.prog = {s: [] for s in self.STREAMS}
        self.dma_q = {"sp": (0, N_DMA_SEMS), "pool": (N_DMA_SEMS, 8), "act": (N_DMA_SEMS + 8, 8)}
        self.n_dsem = N_DMA_SEMS + 16
        self.dma_rr = {q: 0 for q in self.dma_q}
        self.dma_last = [None] * self.n_dsem
        self.dma_cnt = [0] * self.n_dsem
        self.all_dmas = []
        self.order = []

    def _add(self, stream, fn, reads, writes, dma=False, join=False, xr=(), xw=()):
        ins = Instr(stream, fn, dma)
        deps = []
        reads = list(reads) + list(xr)
        writes = list(writes) + list(xw)
        wb = set(id(v.b) for v in writes)
        for v in reads:
            b = v.b
            if id(b) in wb:
                continue
            for w in b.w:
                deps.append((w, 0))
        for v in writes:
            b = v.b
            if not join:
                for w in b.w:
                    deps.append((w, 0 if any(v2.b is b for v2 in reads) else 1))
            for r in b.r.values():
                deps.append((r, 1))
        if dma:
            base, cnt = self.dma_q[stream]
            k = base + self.dma_rr[stream]
            self.dma_rr[stream] = (self.dma_rr[stream] + 1) % cnt
            if self.dma_last[k] is not None:
                deps.append((self.dma_last[k], 0))
            self.dma_last[k] = ins
            self.dma_cnt[k] += 16
            ins.dsem = k
            ins.dval = self.dma_cnt[k]
            self.all_dmas.append(ins)
        seen = set()
        for d, kind in deps:
            if d is ins or id(d) in seen:
                continue
            if (not d.dma) and (not dma) and d.stream == stream:
                if stream == "pe" or not SAME_ENGINE_SYNC:
                    continue
            seen.add(id(d))
            ins.deps.append(d)
            d.signal = True
        rkey = ("dma", ins.dsem) if dma else stream
        for v in reads:
            if id(v.b) not in wb:
                v.b.r[rkey] = ins
        for v in writes:
            if join:
                v.b.w = v.b.w + [ins]
            else:
                v.b.w = [ins]
                v.b.r = {}
        self.prog[stream].append(ins)
        return ins

    def barrier(self):
        lasts = []
        for s in self.STREAMS:
            for ins in reversed(self.prog[s]):
                if not ins.dma and ins.fn is not None:
                    lasts.append(ins)
                    break
        lasts = lasts + [d for d in self.dma_last if d is not None]
        for s in self.STREAMS:
            ins = Instr(s, None)
            for d in lasts:
                if (not d.dma) and d.stream == s:
                    continue
                ins.deps.append(d)
                d.signal = True
            self.prog[s].append(ins)

    def wait_for(self, stream, instrs):
        ins = Instr(stream, None)
        for d in instrs:
            ins.deps.append(d)
            d.signal = True
        self.prog[stream].append(ins)

    def dma(self, out, in_, q="sp", join=False, **kw):
        o, i = out.ap, in_.ap
        return self._add(q, lambda e: e.dma_start(out=o, in_=i, **kw), [in_], [out], dma=True, join=join)

    def mm(self, out, lhsT, rhs, start=True, stop=True, xr=()):
        o, l, r = out.ap, lhsT.ap, rhs.ap
        return self._add("pe", lambda e: e.matmul(o, l, r, start=start, stop=stop), [lhsT, rhs], [out], xr=xr)

    def tr(self, out, in_, ident):
        o, i, d = out.ap, in_.ap, ident.ap
        return self._add("pe", lambda e: e.transpose(o, i, d), [in_, ident], [out])

    def act(self, out, in_, func, bias=None, scale=None, accum=None, xr=(), xw=()):
        o, i = out.ap, in_.ap
        kw = {}
        reads = [in_]
        writes = [out]
        if bias is not None:
            if isinstance(bias, V):
                kw["bias"] = bias.ap
                reads.append(bias)
            else:
                kw["bias"] = bias
        if scale is not None:
            if isinstance(scale, V):
                kw["scale"] = scale.ap
                reads.append(scale)
            else:
                kw["scale"] = scale
        if accum is not None:
            kw["accum_out"] = accum.ap
            writes.append(accum)
        return self._add("act", lambda e: e.activation(o, i, func, **kw), reads, writes, xr=xr, xw=xw)

    def _veng(self, eng):
        return eng

    def tt(self, eng, out, in0, in1, op):
        o, a, b = out.ap, in0.ap, in1.ap
        return self._add(eng, lambda e: e.tensor_tensor(o, a, b, op), [in0, in1], [out])

    def ts(self, eng, out, in0, s1, op0, s2=None, op1=None, accum=None):
        o, a = out.ap, in0.ap
        reads = [in0]
        writes = [out]
        if isinstance(s1, V):
            reads.append(s1)
            s1 = s1.ap
        if isinstance(s2, V):
            reads.append(s2)
            s2 = s2.ap
        kw = {}
        if op1 is not None:
            kw["op1"] = op1
        if accum is not None:
            kw["accum_out"] = accum.ap
            writes.append(accum)
        return self._add(eng, lambda e: e.tensor_scalar(o, a, s1, s2, op0, **kw), reads, writes)

    def stt(self, eng, out, in0, scalar, in1, op0, op1):
        o, a, b = out.ap, in0.ap, in1.ap
        reads = [in0, in1]
        if isinstance(scalar, V):
            reads.append(scalar)
            scalar = scalar.ap
        return self._add(eng, lambda e: e.scalar_tensor_tensor(o, a, scalar, b, op0, op1), reads, [out])

    def copy(self, eng, out, in_):
        o, i = out.ap, in_.ap
        if eng == "act":
            return self._add("act", lambda e: e.copy(o, i), [in_], [out])
        return self._add(eng, lambda e: e.tensor_copy(o, i), [in_], [out])

    def reduce(self, eng, out, in_, op, axis=AX.X):
        o, i = out.ap, in_.ap
        return self._add(eng, lambda e: e.tensor_reduce(o, i, axis, op), [in_], [out])

    def recip(self, out, in_):
        o, i = out.ap, in_.ap
        return self._add("dve", lambda e: e.reciprocal(o, i), [in_], [out])

    def memset(self, eng, out, val):
        o = out.ap
        return self._add(eng, lambda e: e.memset(o, val), [], [out])

    def emit(self):
        nc = self.nc
        from contextlib import ExitStack
        with ExitStack() as es:
            esem = {s: es.enter_context(nc.semaphore("s_" + s)) for s in self.STREAMS if s != "sp" or True}
            dsem = [es.enter_context(nc.semaphore("d%d" % k)) for k in range(self.n_dsem)]
            self.barrier()
            for s in self.STREAMS:
                c = 0
                for ins in self.prog[s]:
                    if ins.dma or ins.fn is None:
                        continue
                    if ins.signal:
                        c += 1
                        ins.semval = c
            plan = {}
            nwaits = 0
            for s in self.STREAMS:
                known = {}
                acts = []
                for ins in self.prog[s]:
                    for d in ins.deps:
                        if d.dma:
                            key, val, sem = ("d", d.dsem), d.dval, dsem[d.dsem]
                        else:
                            key, val, sem = ("e", d.stream), d.semval, esem[d.stream]
                        assert val is not None
                        if known.get(key, 0) >= val:
                            continue
                        known[key] = val
                        acts.append(("w", sem, val))
                        nwaits += 1
                    if ins.fn is not None:
                        acts.append(("i", ins))
                plan[s] = acts

            def run(eng, acts, s):
                for a in acts:
                    if a[0] == "w":
                        eng.wait_ge(a[1], a[2])
                    else:
                        ins = a[1]
                        bi = ins.fn(eng)
                        if ins.dma:
                            bi.then_inc(dsem[ins.dsem], 16)
                        elif ins.signal:
                            bi.then_inc(esem[s], 1)

            with nc.Block() as block:
                @block.sync
                def _(e):
                    run(e, plan["sp"], "sp")

                @block.tensor
                def _(e):
                    run(e, plan["pe"], "pe")

                @block.scalar
                def _(e):
                    run(e, plan["act"], "act")

                @block.vector
                def _(e):
                    run(e, plan["dve"], "dve")

                @block.gpsimd
                def _(e):
                    run(e, plan["pool"], "pool")
            self.stats = {s: len(self.prog[s]) for s in self.STREAMS}
            self.stats["waits"] = nwaits
from concourse.bass_utils import run_bass_kernel_spmd

import math
import ml_dtypes
from contextlib import ExitStack

S = 4096
D = 1024
NT = S // 128
NS = S // 512
NMEM = 256
DFF = 2816
NFC = DFF // 128
EPS = 1e-6
IN_COLS = 7696
OFF_QA, OFF_KA, OFF_VA, OFF_RA, OFF_ZA, OFF_QB, OFF_KB, OFF_VB = 0, 512, 1024, 2048, 3072, 3088, 4624, 6160
DILS = (1, 4, 16)
PI = math.pi

BIG_W = {
    "w_in": (1024, IN_COLS), "w_br_gla": (1024, 1024), "w_br_dil": (512, 1024),
    "w_merge_gate": (1024, 2048), "w_mix_out": (1024, 1024), "w_xq": (1024, 1024),
    "w_xkv": (1024, 2048), "w_xo": (1024, 1024), "w_ffn_up": (1024, 2 * DFF),
    "w_ffn_down": (DFF, 1024),
}
SMALL = {
    "norm_mix": (1024,), "w_gla_gate": (16, 512), "b_gla_gate": (512,), "gla_out_norm": (256,),
    "dil_q_norm": (64,), "dil_k_norm": (64,), "b_merge_gate": (2048,), "norm_x": (1024,),
    "norm_mem": (1024,), "x_q_norm": (256,), "x_k_norm": (256,), "norm_ffn": (1024,),
    "w_ffn_conv": (3, DFF), "b_ffn_conv": (DFF,),
}


class Rot:
    def __init__(self, items):
        self.items = items
        self.i = 0

    def next(self):
        it = self.items[self.i % len(self.items)]
        self.i += 1
        return it


def make_consts():
    c = np.zeros((8, 128, 128), np.float32)
    i = np.arange(128)
    c[0] = np.eye(128)
    c[1] = (i[:, None] <= i[None, :])
    c[2] = (i[:, None] > i[None, :])
    c[3] = (i[:, None] >= i[None, :])
    c[6] = 1.0
    inv_freq = (500000.0 ** (-np.arange(0, 16, 2, dtype=np.float32) / 16)).astype(np.float32)
    c[7, :, 0:8] = inv_freq[None, :]
    return c


def build(dbg=None, stop_after=None):
    nc = bass.Bass("TRN2", target_bir_lowering=False)
    fw = FW(nc)
    es = ExitStack()

    def din(name, shape, dt=F32):
        return Buf(nc.dram_tensor(name, list(shape), dt, kind="ExternalInput").ap(), name)

    def dscr(name, shape, dt):
        return Buf(nc.dram_tensor(name, list(shape), dt, kind="Internal").ap(), name)

    def dout(name, shape, dt=F32):
        return Buf(nc.dram_tensor(name, list(shape), dt, kind="ExternalOutput").ap(), name)

    x = din("x", (S, D))
    mem = din("mem", (NMEM, D))
    pos = din("positions", (S,), I32)
    consts = din("consts", (8, 128, 128))
    W32 = {k: din(k, v) for k, v in BIG_W.items()}
    SM = {k: din(k, v) for k, v in SMALL.items()}
    out = dout("out", (S, D))
    dbg_out = None
    if dbg is not None:
        dbg_out = dout("dbg", dbg[1], dbg[2])

    WB = {k: dscr(k + "_bf", v, BF16) for k, v in BIG_W.items()}
    oaT_d = dscr("oaT_d", (1024, S), BF16)
    obT_d = dscr("obT_d", (512, S), BF16)
    h1_d = dscr("h1_d", (S, D), F32)
    h2_d = dscr("h2_d", (S, D), F32)

    def sb(stack, name, shape, dt):
        return Buf(stack.enter_context(nc.sbuf_tensor(name, list(shape), dt))[:], name)

    def ps(stack, name, shape, dt):
        return Buf(stack.enter_context(nc.psum_tensor(name, list(shape), dt))[:], name)

    def dram_bc(buf, n, parts=128, off=0):
        a = buf.t
        return V(buf, bass.AP(a.tensor, a.offset + off, [[0, parts], [1, n]]))

    xstack = ExitStack()

    def finish():
        fw.emit()
        stB1w.close()
        xstack.close()
        es.close()
        return nc, fw

    win_gla_b = WB["w_in"].sub("w_in_gla")
    win_dil_b = WB["w_in"].sub("w_in_dil")
    top = es
    cst = sb(top, "cst", (128, 8, 128), F32)
    cstb = sb(top, "cstb", (128, 8, 128), BF16)
    epsb = sb(top, "epsb", (128, 2), F32)
    gall = sb(top, "gall", (128, 32), F32)
    gcol = {k: V(gall, gall.t[:, 8 * j:8 * j + 8]) for j, k in enumerate(("norm_mix", "norm_x", "norm_mem", "norm_ffn"))}

    fw.dma(cst[:, :, :], consts.full_v().rearrange("c p f -> p c f"))
    fw.copy("dve", cstb[:, :, :], cst[:, :, :])
    fw.memset("dve", epsb[:, 0:1], EPS)
    fw.memset("dve", epsb[:, 1:2], 1.0)
    with ExitStack() as st0:
        grow = sb(st0, "grow", (32, 128), F32)
        gps = ps(st0, "gps", (128, 512), F32)
        for j, k in enumerate(("norm_mix", "norm_x", "norm_mem", "norm_ffn")):
            fw.dma(grow[8 * j:8 * j + 8, :], SM[k].full_v().rearrange("(c p) -> c p", p=128))
        fw.tr(gps[:, 0:32], grow[:, :], cst[0:32, 0, 0:32])
        fw.copy("dve", gall[:, :], gps[:, 0:32])
        fw.barrier()
    ident_b = cstb[:, 0, :]
    ones_row_b = cstb[0:1, 6, :]
    eps_c = epsb[:, 0:1]
    one_c = epsb[:, 1:2]

    xn_t = sb(xstack, "xnT", (128, 8, S), BF16)
    xn_sub = [xn_t.sub("xn%d" % j) for j in range(NS)]
    xn_ro = xn_t.sub("xn_ro")

    def xn_w(i):
        return V(xn_sub[i // 4], xn_t.t[:, :, i * 128:(i + 1) * 128])

    def norm_a(nb, xt):
        junk, ssr, xsr, ptr = nb
        jk = junk.next()
        ss = ssr.next()
        fw.act(jk[:, :], xt, AF.Square, accum=ss[:, 0:1])
        fw.act(ss[:, 1:2], ss[:, 0:1], AF.Ln, scale=1.0 / 1024, bias=eps_c)
        fw.act(ss[:, 2:3], ss[:, 1:2], AF.Exp, scale=-0.5)
        xs = xsr.next()
        fw.ts("dve", xs[:, :], xt, ss[:, 2:3], ALU.mult)
        return xs

    def norm_b(nb, xs, g, dst_view):
        junk, ssr, xsr, ptr = nb
        pt = ptr.next()
        ptb = pt[:, :].bitcast(BF16)
        for c in range(8):
            fw.tr(ptb[:, c * 128:(c + 1) * 128], xs[:, c * 128:(c + 1) * 128], ident_b)
        fw.tt("dve", dst_view, ptb.rearrange("p (c t) -> p c t", c=8),
              g[:, :].raw(0, [[1, 8], [0, 128]]), ALU.mult)

    def norm_T(nb, xt, g, dst_view):
        norm_b(nb, norm_a(nb, xt), g, dst_view)

    class HeadNorm:
        def __init__(self, st, tag, n=3):
            self.slots = []
            for i in range(n):
                ss = sb(st, "hn_ss%s%d" % (tag, i), (128, 12), F32)
                jk = sb(st, "hn_jk%s%d" % (tag, i), (128, 4, 256), BF16)
                self.slots.append((ss, [ss.sub() for _ in range(4)], jk, [jk.sub() for _ in range(4)]))
            self.i = 0

        def stats(self, srcs):
            ss, ssub, jk, jsub = self.slots[self.i % len(self.slots)]
            self.i += 1
            for h in range(4):
                fw.act(V(jsub[h], jk.t[:, h, :]), srcs[h], AF.Square, accum=V(ssub[h], ss.t[:, h:h + 1]))
            fw.act(ss[:, 4:8], ss[:, 0:4], AF.Ln, scale=1.0 / 256, bias=eps_c,
                   xr=[V(ssub[h], ss.t[:, h:h + 1]) for h in range(4)])
            fw.act(ss[:, 8:12], ss[:, 4:8], AF.Exp, scale=-0.5)
            return ss

    def norm_bufs(st, tag, nj=2, nxs=2):
        return (Rot([sb(st, "junk%s%d" % (tag, i), (128, 1024), BF16) for i in range(nj)]),
                Rot([sb(st, "ss%s%d" % (tag, i), (128, 4), F32) for i in range(max(4, nxs + 1))]),
                Rot([sb(st, "xs%s%d" % (tag, i), (128, 1024), BF16) for i in range(nxs)]))

    win = V(win_gla_b, WB["w_in"].t.rearrange("(c p) n -> p c n", p=128))
    win_dil = V(win_dil_b, WB["w_in"].t.rearrange("(c p) n -> p c n", p=128))
    stB1w = ExitStack()
    wqk = sb(stB1w, "wqk", (128, 8, 1024), BF16)
    wv = sb(stB1w, "wv", (128, 8, 1024), BF16)
    wr = sb(stB1w, "wr", (128, 8, 1024), BF16)
    wz = sb(stB1w, "wz", (128, 8, 16), BF16)
    wgg = sb(stB1w, "wgg", (16, 512), BF16)
    bgg = sb(stB1w, "bgg", (1, 512), BF16)
    gon = sb(stB1w, "gon", (128, 256), F32)
    fw.dma(gon[:, :], dram_bc(SM["gla_out_norm"], 256))
    fw.dma(wgg[:, :], SM["w_gla_gate"].full_v(), q="pool")
    fw.dma(bgg[:, :], SM["b_gla_gate"].full_v().rearrange("(o n) -> o n", o=1), q="pool")
    b1_loads = []
    with ExitStack() as st:
        xts = Rot([sb(st, "xt%d" % i, (128, 1024), F32) for i in range(4)])
        nb = norm_bufs(st, "A") + (Rot([ps(st, "ptA%d" % i, (128, 512), F32) for i in range(2)]),)
        xloads = []
        for i in range(NT):
            xt = xts.next()
            xloads.append(fw.dma(xt[:, :], x[i * 128:(i + 1) * 128, :]))
            if i == 3:
                fw.wait_for("pool", xloads)
                for (c0, c1, bsub) in ((0, 1544, win_gla_b), (1544, 3088, win_gla_b)):
                    for r0 in (0, 512):
                        fw.dma(V(bsub, WB["w_in"].t[r0:r0 + 512, c0:c1]),
                               V(W32["w_in"], W32["w_in"].t[r0:r0 + 512, c0:c1]), q="pool", join=True)
            if i == NT - 1:
                b1_loads.append(fw.dma(wqk[:, :, :], win[:, :, 0:1024]))
                b1_loads.append(fw.dma(wv[:, :, :], win[:, :, 1024:2048]))
                b1_loads.append(fw.dma(wr[:, :, :], win[:, :, 2048:3072]))
                b1_loads.append(fw.dma(wz[:, :, :], win[:, :, 3072:3088], allow_slow_non_contiguous=True))
            norm_T(nb, xt[:, :], gcol["norm_mix"], xn_w(i))
        fw.barrier()
    fw.wait_for("pool", b1_loads)
    deferred_conv = []
    for (c0, c1, bsub) in ((3088, 4624, win_dil_b), (4624, 6160, win_dil_b), (6160, 7696, win_dil_b)):
        for r0 in (0, 512):
            deferred_conv.append((V(bsub, WB["w_in"].t[r0:r0 + 512, c0:c1]),
                                  V(W32["w_in"], W32["w_in"].t[r0:r0 + 512, c0:c1])))
    for k, shp in BIG_W.items():
        if k == "w_in":
            continue
        n = shp[0] * shp[1]
        rows = n // 2048
        src = W32[k].full_v().rearrange("a b -> (a b)").rearrange("(r c) -> r c", c=2048)
        dst = WB[k].full_v().rearrange("a b -> (a b)").rearrange("(r c) -> r c", c=2048)
        r0 = 0
        while r0 < rows:
            r1 = min(rows, r0 + 512)
            deferred_conv.append((dst[r0:r1, :], src[r0:r1, :]))
            r0 = r1

    def emit_conv(after=None, n=1):
        for _ in range(n):
            if not deferred_conv:
                return
            if after is not None:
                fw.wait_for("pool", [after])
            d_, s_ = deferred_conv.pop(0)
            fw.dma(d_, s_, q="pool", join=True)

    emit_conv(None, 2)

    DKS = 128 ** -0.5
    oaT_v = oaT_d.full_v().rearrange("(c p) t -> p c t", p=128)
    with ExitStack() as st:
        S32 = [sb(st, "S32_%d" % h, (128, 256), F32) for h in range(4)]
        Sbf = [sb(st, "Sbf_%d" % h, (128, 256), BF16) for h in range(4)]
        for h in range(4):
            fw.memset("pool", S32[h][:, :], 0.0)
            fw.memset("pool", Sbf[h][:, :], 0.0)
        PB = Rot([ps(st, "bP%d" % i, (128, 512), F32) for i in range(8)])
        zar = Rot([sb(st, "za%d" % i, (16, 512), BF16) for i in range(2)])
        e1r = Rot([sb(st, "e1_%d" % i, (128, 512), F32) for i in range(1)])
        spb = [sb(st, "sp_%d" % i, (128, 512), F32) for i in range(4)]
        eTpr = Rot([sb(st, "eTp%d" % i, (128, 512), F32) for i in range(2)])
        eTnr = Rot([sb(st, "eTn%d" % i, (128, 512), F32) for i in range(2)])
        ervr = Rot([sb(st, "erv%d" % i, (128, 512), F32) for i in range(2)])
        elr = Rot([sb(st, "elast%d" % i, (128, 16), F32) for i in range(2)])
        qes = [[sb(st, "qe%d_%d" % (i, h), (128, 512), BF16) for h in range(4)] for i in range(2)]
        kes = [[sb(st, "ke%d_%d" % (i, h), (128, 512), BF16) for h in range(4)] for i in range(2)]
        kls = [[sb(st, "kl%d_%d" % (i, c4), (128, 512), BF16) for c4 in range(4)] for i in range(2)]
        vsr = Rot([sb(st, "vs%d" % i, (128, 1024), BF16) for i in range(2)])
        atr = Rot([sb(st, "at%d" % i, (128, 512), BF16) for i in range(3)])
        Gr = Rot([sb(st, "G%d" % i, (128, 1024), F32) for i in range(2)])
        onr = Rot([sb(st, "on%d" % i, (128, 1024), BF16) for i in range(2)])
        oTr = Rot([sb(st, "oT%d" % i, (128, 8, 128), BF16) for i in range(2)])
        hn = HeadNorm(st, "B", 2)
        pst = {}
        cst_ = {}

        def xs_(kc, s_):
            return V(xn_ro, xn_t.t[:, kc, s_ * 512:(s_ + 1) * 512])

        def xc_(kc, c):
            return V(xn_ro, xn_t.t[:, kc, c * 128:(c + 1) * 128])

        def P_a(s_):
            d = pst.setdefault(s_, {})
            d["set"] = s_ % 2
            pza = PB.next()
            for kc in range(8):
                fw.mm(pza[0:16, :], wz[:, kc, :], xs_(kc, s_), start=(kc == 0), stop=(kc == 7))
            za = zar.next()
            fw.copy("act", za[:, :], pza[0:16, :])
            for c4 in range(4):
                pz = PB.next()
                fw.mm(pz[:, :], za[:, c4 * 128:(c4 + 1) * 128], wgg[:, :], start=True, stop=False)
                fw.mm(pz[:, :], ones_row_b, bgg[:, :], start=False, stop=True)
                e1 = e1r.next()
                fw.act(e1[:, :], pz[:, :], AF.Exp, scale=-1.0)
                fw.act(spb[c4][:, :], e1[:, :], AF.Ln, bias=one_c, scale=1.0)
            d["el"] = elr.next()

        def P_b(s_, hs):
            d = pst[s_]
            el = d["el"]
            for h in hs:
                pc = PB.next()
                for c4 in range(4):
                    fw.mm(pc[:, c4 * 128:(c4 + 1) * 128], spb[c4][:, h * 128:(h + 1) * 128], cst[:, 1, :])
                eTp = eTpr.next()
                eTn = eTnr.next()
                fw.act(eTp[:, :], pc[:, :], AF.Exp, scale=-1.0 / 16)
                fw.act(eTn[:, :], pc[:, :], AF.Exp, scale=1.0 / 16)
                fw.copy("pool", el[:, h * 4:(h + 1) * 4], eTp[:, :].raw(127, [[128, 4]]))
                for which in range(2):
                    p_ = PB.next()
                    co = (0 if which == 0 else 512) + h * 128
                    for kc in range(8):
                        fw.mm(p_[:, :], wqk[:, kc, co:co + 128], xs_(kc, s_), start=(kc == 0), stop=(kc == 7))
                    if which == 0:
                        fw.stt("dve", qes[d["set"]][h][:, :], p_[:, :], DKS, eTp[:, :], ALU.mult, ALU.mult)
                    else:
                        fw.tt("dve", kes[d["set"]][h][:, :], p_[:, :], eTn[:, :], ALU.mult)

        def P_c(s_, c4):
            d = pst[s_]
            c = s_ * 4 + c4
            pr = PB.next()
            fw.mm(pr[:, :], cst[:, 2, :], spb[c4][:, :])
            erv = ervr.next()
            fw.act(erv[:, :], pr[:, :], AF.Exp, scale=-1.0 / 16)
            pkt = PB.next()
            for kc in range(8):
                fw.mm(pkt[:, :], xc_(kc, c), wqk[:, kc, 512:1024], start=(kc == 0), stop=(kc == 7))
            fw.tt("dve", kls[d["set"]][c4][:, :], pkt[:, :], erv[:, :], ALU.mult)

        def R1(c):
            s_, c4 = c // 4, c % 4
            d = pst[s_]
            cd = cst_.setdefault(c, {})
            qe, ke = qes[d["set"]], kes[d["set"]]
            tsl = slice(c4 * 128, (c4 + 1) * 128)
            pa = PB.next()
            for h in range(4):
                fw.mm(pa[:, h * 128:(h + 1) * 128], ke[h][:, tsl], qe[h][:, tsl])
            atm = atr.next()
            fw.tt("dve", atm[:, :].rearrange("p (h i) -> p h i", h=4), pa[:, :].rearrange("p (h i) -> p h i", h=4),
                  cst[:, 1, :].raw(0, [[0, 4], [1, 128]]), ALU.mult)
            cd["atm"] = atm
            vs = vsr.next()
            for half in range(2):
                pv = PB.next()
                for kc in range(8):
                    fw.mm(pv[:, :], xc_(kc, c), wv[:, kc, half * 512:(half + 1) * 512], start=(kc == 0), stop=(kc == 7))
                fw.copy("act", vs[:, half * 512:(half + 1) * 512], pv[:, :])
            cd["vs"] = vs
            G = Gr.next()
            for half in range(2):
                prr = PB.next()
                for kc in range(8):
                    fw.mm(prr[:, :], xc_(kc, c), wr[:, kc, half * 512:(half + 1) * 512], start=(kc == 0), stop=(kc == 7))
                fw.act(G[:, half * 512:(half + 1) * 512], prr[:, :], AF.Silu)
            fw.tt("pool", G[:, :].rearrange("p (h v) -> p h v", h=4), G[:, :].rearrange("p (h v) -> p h v", h=4),
                  gon[:, :].raw(0, [[0, 4], [1, 256]]), ALU.mult)
            cd["G"] = G

        def R2(c):
            s_, c4 = c // 4, c % 4
            d = pst[s_]
            cd = cst_[c]
            qe = qes[d["set"]]
            klw = kls[d["set"]][c4]
            vs = cd["vs"]
            el = d["el"]
            atm, G = cd["atm"], cd["G"]
            tsl = slice(c4 * 128, (c4 + 1) * 128)
            pos = []
            for hb in range(2):
                pob = PB.next()
                for hl in range(2):
                    h = hb * 2 + hl
                    fw.mm(pob[:, hl * 256:(hl + 1) * 256], atm[:, h * 128:(h + 1) * 128], vs[:, h * 256:(h + 1) * 256],
                          start=True, stop=False)
                    fw.mm(pob[:, hl * 256:(hl + 1) * 256], qe[h][:, tsl], Sbf[h][:, :], start=False, stop=True)
                pos.append(pob)
            for hb in range(2):
                pdb = PB.next()
                for hl in range(2):
                    h = hb * 2 + hl
                    fw.mm(pdb[:, hl * 256:(hl + 1) * 256], klw[:, h * 128:(h + 1) * 128], vs[:, h * 256:(h + 1) * 256])
                for hl in range(2):
                    h = hb * 2 + hl
                    fw.stt("dve", S32[h][:, :], S32[h][:, :], el[:, h * 4 + c4:h * 4 + c4 + 1],
                           pdb[:, hl * 256:(hl + 1) * 256], ALU.mult, ALU.add)
                    fw.copy("pool", Sbf[h][:, :], S32[h][:, :])
            srcs = [pos[h // 2][:, (h % 2) * 256:(h % 2 + 1) * 256] for h in range(4)]
            ss = hn.stats(srcs)
            on = onr.next()
            for h in range(4):
                fw.stt("dve", on[:, h * 256:(h + 1) * 256], srcs[h], ss[:, 8 + h:9 + h],
                       G[:, h * 256:(h + 1) * 256], ALU.mult, ALU.mult)
            cd["on"] = on

        def R3(c):
            cd = cst_[c]
            on = cd["on"]
            pt = PB.next()
            ptb = pt[:, :].bitcast(BF16)
            for cc in range(8):
                fw.tr(ptb[:, cc * 128:(cc + 1) * 128], on[:, cc * 128:(cc + 1) * 128], ident_b)
            oT = oTr.next()
            fw.copy("act", oT[:, :, :], ptb.rearrange("p (c t) -> p c t", c=8))
            st_ins = fw.dma(oaT_v[:, :, c * 128:(c + 1) * 128], oT[:, :, :], join=True)
            emit_conv(st_ins, 1)

        def P_quarter(s_, qi):
            if qi == 0:
                P_a(s_)
                P_b(s_, (0,))
            elif qi == 1:
                P_b(s_, (1, 2))
            elif qi == 2:
                P_b(s_, (3,))
                P_c(s_, 0)
                P_c(s_, 1)
            else:
                P_c(s_, 2)
                P_c(s_, 3)

        for qi in range(4):
            P_quarter(0, qi)
        R1(0)
        for c in range(NT):
            s_, c4 = c // 4, c % 4
            R2(c)
            if c > 0:
                R3(c - 1)
            if s_ + 1 < NS:
                P_quarter(s_ + 1, c4)
            if c + 1 < NT:
                R1(c + 1)
        R3(NT - 1)
        emit_conv(None, 1000)
        fw.barrier()

    if dbg is not None and dbg[0] == "B1":
        with ExitStack() as st:
            tmp = sb(st, "dbgt", (128, 8, 512), BF16)
            dv = dbg_out.full_v().rearrange("(c p) t -> p c t", p=128)
            for j in range(8):
                fw.dma(tmp[:, :, :], oaT_v[:, :, j * 512:(j + 1) * 512])
                fw.dma(dv[:, :, j * 512:(j + 1) * 512], tmp[:, :, :])
            fw.barrier()
        return finish()

    def run_skewed(n_items, stages):
        K = len(stages)
        ctx = [dict() for _ in range(n_items)]
        for step in range(n_items + K - 1):
            for k in range(K):
                i = step - k
                if 0 <= i < n_items and stages[k] is not None:
                    stages[k](i, ctx[i])

    qkT_d = [dscr("qkT_d%d" % g, (4, 128, 2, S), BF16) for g in range(3)]
    vaug_d = [dscr("vaug_d%d" % g, (128, 32, 8, 65), BF16) for g in range(3)]

    def phase_c0():
        with ExitStack() as st:
            pqk = Rot([ps(st, "pqk%d" % i, (128, 1024), F32) for i in range(2)])
            pvv = Rot([ps(st, "pvv%d" % i, (128, 512), F32) for i in range(2)])
            ptt = Rot([ps(st, "ptt%d" % i, (128, 512), F32) for i in range(2)])
            invf = cst[:, 7, 0:8]
            csn = [sb(st, "cs%d" % g, (128, 32, 16), F32) for g in range(3)]
            snn = [sb(st, "sn%d" % g, (128, 32, 16), F32) for g in range(3)]
            gqk = sb(st, "gqk", (128, 16, 64), F32)
            for a_ in range(2):
                nm = "dil_q_norm" if a_ == 0 else "dil_k_norm"
                a = SM[nm].t
                fw.dma(gqk[:, a_ * 8:(a_ + 1) * 8, :], V(SM[nm], bass.AP(a.tensor, a.offset, [[0, 128], [0, 8], [1, 64]])))
            if True:
                prow_b = sb(st, "prow", (1, S), F32)
                prow = prow_b[:, :]
                ang = sb(st, "ang", (128, 256), F32)
                tA = sb(st, "tA", (128, 256), F32)
                tB = sb(st, "tB", (128, 256), F32)
                tI = sb(st, "tI", (128, 256), I32)
                TWO_PI = 2.0 * PI
                fw.dma(prow, pos.full_v().rearrange("(o n) -> o n", o=1), q="pool")
                aps = [ptt.items[1]]
                for g, dil in enumerate(DILS):
                    nb_ = 32 // dil
                    ap_ = aps[0]
                    for n in range(32):
                        r, bb = n // nb_, n % nb_
                        t0 = dil * 128 * bb + r
                        fw.mm(ap_[:, n * 8:(n + 1) * 8], prow[0:1, t0:t0 + 127 * dil + 1:dil], cst[0:1, 7, 0:8])
                    fw.copy("dve", ang[:, :], ap_[:, 0:256])
                    for which in range(2):
                        if which == 1:
                            fw.ts("dve", ang[:, :], ang[:, :], PI / 2, ALU.add)
                        fw.ts("dve", tA[:, :], ang[:, :], 1.0 / TWO_PI, ALU.mult)
                        fw.copy("dve", tI[:, :], tA[:, :])
                        fw.copy("dve", tA[:, :], tI[:, :])
                        fw.stt("dve", tB[:, :], tA[:, :], -TWO_PI, ang[:, :], ALU.mult, ALU.add)
                        fw.ts("dve", tA[:, :], tB[:, :], PI, ALU.is_gt)
                        fw.stt("dve", tB[:, :], tA[:, :], -TWO_PI, tB[:, :], ALU.mult, ALU.add)
                        fw.ts("dve", tA[:, :], tB[:, :], -PI, ALU.is_lt)
                        fw.stt("dve", tB[:, :], tA[:, :], TWO_PI, tB[:, :], ALU.mult, ALU.add)
                        fw.ts("dve", tB[:, :], tB[:, :], 3.141592, ALU.min, -3.141592, ALU.max)
                        tb3 = tB[:, :].rearrange("p (n e) -> p n e", e=8)
                        if which == 0:
                            fw.act(snn[g][:, :, 8:16], tb3, AF.Sin)
                            fw.ts("dve", snn[g][:, :, 0:8], snn[g][:, :, 8:16], -1.0, ALU.mult)
                        else:
                            fw.act(csn[g][:, :, 0:8], tb3, AF.Sin)
                            fw.copy("dve", csn[g][:, :, 8:16], csn[g][:, :, 0:8])

            wdr = Rot([sb(st, "wd%d" % i, (128, 8, 3, 512), BF16) for i in range(1)])
            sqr = Rot([sb(st, "sq%d" % i, (128, 1024), F32) for i in range(2)])
            s16r = Rot([sb(st, "s16_%d" % i, (128, 48), F32) for i in range(3)])
            qk1r = Rot([sb(st, "qk1_%d" % i, (128, 16, 64), F32) for i in range(3)])
            qk2r = Rot([sb(st, "qk2_%d" % i, (128, 16, 64), F32) for i in range(2)])
            Ar = Rot([sb(st, "ropeA%d" % i, (128, 16, 16), F32) for i in range(2)])
            Br = Rot([sb(st, "ropeB%d" % i, (128, 16, 16), F32) for i in range(2)])
            qkbr = Rot([sb(st, "qkb%d" % i, (128, 16, 64), BF16) for i in range(3)])
            oT4r = Rot([sb(st, "oT4_%d" % i, (128, 8, 512), BF16) for i in range(2)])
            vt4r = Rot([sb(st, "vt4_%d" % i, (128, 4, 8, 65), BF16) for i in range(2)])
            for vt in vt4r.items:
                fw.memset("pool", vt[:, :, :, :], 1.0)
            wds = {}
            c0items = [(g, n) for g in range(3) for n in range(32)]
            hold = {}

            def c0s1(it, c):
                g, n = c0items[it]
                dil = DILS[g]
                nb_ = 32 // dil
                if n == 0:
                    wd = wdr.next()
                    for j, off in enumerate((OFF_QB, OFF_KB, OFF_VB)):
                        fw.dma(wd[:, :, j, :], win_dil[:, :, off + g * 512:off + (g + 1) * 512])
                    wds[g] = wd
                wd = wds[g]
                r, bb = n // nb_, n % nb_
                t0 = dil * 128 * bb + r

                def lhsT(kc):
                    return V(xn_ro, xn_t.t[:, kc, t0:t0 + 127 * dil + 1:dil])
                pq_ = pqk.next()
                pv_ = pvv.next()
                for j in range(2):
                    for kc in range(8):
                        fw.mm(pq_[:, j * 512:(j + 1) * 512], lhsT(kc), wd[:, kc, j, :], start=(kc == 0), stop=(kc == 7))
                for kc in range(8):
                    fw.mm(pv_[:, :], lhsT(kc), wd[:, kc, 2, :], start=(kc == 0), stop=(kc == 7))
                sq = sqr.next()
                fw.act(sq[:, :], pq_[:, :], AF.Square)
                s16 = s16r.next()
                fw.reduce("dve", s16[:, 0:16], sq[:, :].rearrange("p (a d) -> p a d", a=16), ALU.add)
                fw.act(s16[:, 16:32], s16[:, 0:16], AF.Ln, scale=1.0 / 64, bias=eps_c)
                fw.act(s16[:, 32:48], s16[:, 16:32], AF.Exp, scale=-0.5)
                qk1 = qk1r.next()
                fw.tt("dve", qk1[:, :, :], pq_[:, :].rearrange("p (a d) -> p a d", a=16),
                      s16[:, 32:48].raw(0, [[1, 16], [0, 64]]), ALU.mult)
                c["qk1"] = qk1
                m4 = n % 4
                if m4 == 0:
                    hold["vt4"] = vt4r.next()
                vt4 = hold["vt4"]
                fw.copy("dve", vt4[:, m4, :, 0:64], pv_[:, :].rearrange("p (a d) -> p a d", a=8))
                if m4 == 3:
                    fw.dma(vaug_d[g].full_v()[:, n - 3:n + 1, :, :], vt4[:, :, :, :], join=True)

            def c0s2(it, c):
                g, n = c0items[it]
                qk1 = c["qk1"]
                qk2 = qk2r.next()
                fw.tt("pool", qk2[:, :, :], qk1[:, :, :], gqk[:, :, :], ALU.mult)
                A = Ar.next()
                B = Br.next()
                fw.tt("pool", A[:, :, :], qk2[:, :, 0:16], csn[g][:, n, :].raw(0, [[0, 16], [1, 16]]), ALU.mult)
                fw.tt("pool", B[:, :, 0:8], qk2[:, :, 8:16], snn[g][:, n, 0:8].raw(0, [[0, 16], [1, 8]]), ALU.mult)
                fw.tt("pool", B[:, :, 8:16], qk2[:, :, 0:8], snn[g][:, n, 8:16].raw(0, [[0, 16], [1, 8]]), ALU.mult)
                qkb = qkbr.next()
                fw.copy("act", qkb[:, :, 16:64], qk2[:, :, 16:64])
                fw.tt("dve", qkb[:, :, 0:16], A[:, :, :], B[:, :, :], ALU.add)
                c["qkb"] = qkb

            def c0s3(it, c):
                g, n = c0items[it]
                qkb = c["qkb"]
                pt_ = ptt.next()
                ptb = pt_[:, :].bitcast(BF16)
                qkb2 = qkb[:, :, :].rearrange("p a d -> p (a d)")
                for cc in range(8):
                    fw.tr(ptb[:, cc * 128:(cc + 1) * 128], qkb2[:, cc * 128:(cc + 1) * 128], ident_b)
                m4 = n % 4
                if m4 == 0:
                    hold["oT4"] = oT4r.next()
                oT4 = hold["oT4"]
                fw.copy("act", oT4[:, :, m4 * 128:(m4 + 1) * 128], ptb.rearrange("p (c t) -> p c t", c=8))
                if m4 == 3:
                    n0 = n - 3
                    dst = qkT_d[g].full_v().rearrange("h p a t -> p a h t")[:, :, :, n0 * 128:(n0 + 4) * 128]
                    fw.dma(dst, oT4[:, :, :].rearrange("p (a h) t -> p a h t", a=2), join=True)

            run_skewed(len(c0items), [c0s1, c0s2, c0s3])
            fw.barrier()

    def phase_c1():
        with ExitStack() as st:
            mask4 = sb(st, "mask4", (128, 4, 128), BF16)
            for a_ in range(4):
                fw.copy("dve", mask4[:, a_, :], cstb[:, 1 if a_ % 2 == 0 else 3, :])
            mask4f = mask4[:, :, :].rearrange("p a i -> p (a i)")
            obT_v = obT_d.full_v()
            accs = [[sb(st, "acc%d_%d" % (k_, hh), (65, S), F32) for hh in range(2)] for k_ in range(2)]
            norm_jobs = []
            qkTr = Rot([sb(st, "qkT%d" % i, (128, 2, S), BF16) for i in range(2)])
            vaugr = Rot([sb(st, "vaug%d" % i, (128, 32, 2, 65), BF16) for i in range(2)])
            pS = Rot([ps(st, "pS%d" % i, (128, 512), F32) for i in range(3)])
            pO = Rot([ps(st, "pO%d" % i, (128, 512), F32) for i in range(3)])
            pB = Rot([ps(st, "pB%d" % i, (128, 512), F32) for i in range(2)])
            ptr_ = Rot([sb(st, "pTs%d" % i, (128, 512), BF16) for i in range(9)])
            obr = Rot([sb(st, "ob%d" % i, (64, 512), BF16) for i in range(2)])
            rbr = Rot([sb(st, "rb%d" % i, (64, 512), F32) for i in range(2)])

            def loads(hp, g):
                qkT = qkTr.next()
                va = vaugr.next()
                fw.dma(qkT[:, :, :], V(qkT_d[g], qkT_d[g].t[hp]))
                fw.dma(va[:, :, :, :], vaug_d[g].full_v()[:, :, hp * 2:hp * 2 + 2, :])
                return qkT, va

            seq = [(hp, g) for hp in range(4) for g in range(3)]
            bufs = {0: loads(*seq[0])}
            items = [(idx, hh, n0) for idx in range(len(seq)) for hh in range(2) for n0 in range(0, 32, 2)]

            def s1(it, c):
                idx, hh, n0 = items[it]
                hp, g = seq[idx]
                nb_ = 32 // DILS[g]
                qkT, vaug = bufs[idx]
                hs = slice(hh * 64, (hh + 1) * 64)
                b0 = n0 % nb_
                pS_ = pS.next()
                for q_ in range(2):
                    n = n0 + q_
                    qv = qkT[hs, 0, n * 128:(n + 1) * 128]
                    fw.mm(pS_[:, (2 * q_) * 128:(2 * q_ + 1) * 128], qkT[hs, 1, n * 128:(n + 1) * 128], qv)
                    if b0 + q_ > 0:
                        fw.mm(pS_[:, (2 * q_ + 1) * 128:(2 * q_ + 2) * 128], qkT[hs, 1, (n - 1) * 128:n * 128], qv)
                pT_ = ptr_.next()
                meng = "pool" if (it % 3) != 2 else "dve"
                if b0 == 0:
                    fw.act(pT_[:, 0:128], pS_[:, 0:128], AF.Exp, scale=0.125)
                    fw.act(pT_[:, 256:512], pS_[:, 256:512], AF.Exp, scale=0.125)
                    fw.tt(meng, pT_[:, 0:128], pT_[:, 0:128], mask4f[:, 0:128], ALU.mult)
                    fw.tt(meng, pT_[:, 256:512], pT_[:, 256:512], mask4f[:, 256:512], ALU.mult)
                else:
                    fw.act(pT_[:, :], pS_[:, :], AF.Exp, scale=0.125)
                    fw.tt(meng, pT_[:, :], pT_[:, :], mask4f, ALU.mult)
                c["pT"] = pT_

            def do_norm(hp_, h2_, blk):
                acc_ = accs[hp_ % 2]
                pB_ = pB.next()
                fw.mm(pB_[0:64, :], cst[64:65, 6, 0:64], acc_[h2_][64:65, blk * 512:(blk + 1) * 512])
                ob = obr.next()
                rb = rbr.next()
                fw.recip(rb[:, :], pB_[0:64, :])
                fw.tt("dve", ob[:, :], acc_[h2_][0:64, blk * 512:(blk + 1) * 512], rb[:, :], ALU.mult)
                hg = hp_ * 2 + h2_
                fw.dma(obT_v[hg * 64:(hg + 1) * 64, blk * 512:(blk + 1) * 512], ob[:, :], join=True)

            def s2(it, c):
                idx, hh, n0 = items[it]
                hp, g = seq[idx]
                dil = DILS[g]
                nb_ = 32 // dil
                qkT, vaug = bufs[idx]
                r, b0 = n0 // nb_, n0 % nb_
                pT_ = c["pT"]
                if hh == 0 and n0 == 0 and idx + 1 < len(seq):
                    bufs[idx + 1] = loads(*seq[idx + 1])
                pO_ = pO.next()
                for q_ in range(2):
                    n = n0 + q_
                    bb = b0 + q_
                    fw.mm(pO_[0:65, q_ * 128:(q_ + 1) * 128], vaug[:, n, hh, :],
                          pT_[:, (2 * q_) * 128:(2 * q_ + 1) * 128], start=True, stop=(bb == 0))
                    if bb > 0:
                        fw.mm(pO_[0:65, q_ * 128:(q_ + 1) * 128], vaug[:, n - 1, hh, :],
                              pT_[:, (2 * q_ + 1) * 128:(2 * q_ + 2) * 128], start=False, stop=True)
                a0 = dil * 128 * b0 + r
                acc = accs[hp % 2]
                av = acc[hh][:, :].raw(a0, [[dil, 256]])
                if g == 0:
                    fw.copy("dve", av, pO_[0:65, 0:256])
                else:
                    fw.tt("dve", av, av, pO_[0:65, 0:256], ALU.add)
                if norm_jobs:
                    do_norm(*norm_jobs.pop(0))
                if g == 2 and hh == 1 and n0 == 30:
                    for h2_ in range(2):
                        for blk in range(8):
                            norm_jobs.append((hp, h2_, blk))

            run_skewed(len(items), [s1, None, None, None, None, s2])
            while norm_jobs:
                do_norm(*norm_jobs.pop(0))
            fw.barrier()

    stB1w.close()
    phase_c0()
    phase_c1()
    if dbg is not None and dbg[0] == "C1":
        with ExitStack() as st:
            tmp = sb(st, "dbgt", (128, 4, 512), BF16)
            sv = obT_d.full_v().rearrange("(c p) t -> p c t", p=128)
            dv = dbg_out.full_v().rearrange("(c p) t -> p c t", p=128)
            for j in range(8):
                fw.dma(tmp[:, :, :], sv[:, :, j * 512:(j + 1) * 512])
                fw.dma(dv[:, :, j * 512:(j + 1) * 512], tmp[:, :, :])
            fw.barrier()
        return finish()

    def phase_t1a():
        with ExitStack() as st:
            wbr = sb(st, "wbr", (128, 8, 1024), BF16)
            wbd = sb(st, "wbd", (128, 4, 1024), BF16)
            wmg = sb(st, "wmg", (128, 8, 2048), BF16)
            wmo = sb(st, "wmo", (128, 8, 1024), BF16)
            fw.dma(wbr[:, :, :], WB["w_br_gla"].full_v().rearrange("(c p) n -> p c n", p=128))
            fw.dma(wbd[:, :, :], WB["w_br_dil"].full_v().rearrange("(c p) n -> p c n", p=128))
            fw.dma(wmg[:, :, :], WB["w_merge_gate"].full_v().rearrange("(c p) n -> p c n", p=128))
            fw.dma(wmo[:, :, :], WB["w_mix_out"].full_v().rearrange("(c p) n -> p c n", p=128))
            bmg = sb(st, "bmg", (1, 2048), BF16)
            fw.dma(bmg[:, :], SM["b_merge_gate"].full_v().rearrange("(o n) -> o n", o=1), q="pool")
            P = Rot([ps(st, "qP%d" % i, (128, 512), F32) for i in range(8)])
            nb = norm_bufs(st, "T") + (P,)
            oatr = Rot([sb(st, "oat%d" % i, (128, 8, 128), BF16) for i in range(2)])
            obtr = Rot([sb(st, "obt%d" % i, (128, 4, 128), BF16) for i in range(2)])
            xtr = Rot([sb(st, "xtT%d" % i, (128, 1024), F32) for i in range(3)])
            gar = Rot([sb(st, "ga%d" % i, (128, 512), F32) for i in range(3)])
            mar = Rot([sb(st, "ma%d" % i, (128, 1024), F32) for i in range(3)])
            mrr = Rot([sb(st, "mr%d" % i, (128, 1024), BF16) for i in range(2)])
            mTr = Rot([sb(st, "mT%d" % i, (128, 8, 128), BF16) for i in range(2)])
            h1r = Rot([sb(st, "h1t%d" % i, (128, 1024), F32) for i in range(2)])
            oaT_v = oaT_d.full_v().rearrange("(c p) t -> p c t", p=128)
            obT_v = obT_d.full_v().rearrange("(c p) t -> p c t", p=128)

            def sL(i, c):
                c["oat"] = oatr.next()
                c["obt"] = obtr.next()
                c["xt"] = xtr.next()
                fw.dma(c["oat"][:, :, :], oaT_v[:, :, i * 128:(i + 1) * 128])
                fw.dma(c["obt"][:, :, :], obT_v[:, :, i * 128:(i + 1) * 128])
                fw.dma(c["xt"][:, :], x[i * 128:(i + 1) * 128, :])

            def s1(i, c):
                oat, obt = c["oat"], c["obt"]
                xnv = V(xn_sub[i // 4], xn_t.t[:, :, i * 128:(i + 1) * 128])
                ms = []
                for br in range(2):
                    ma = mar.next()
                    for half in range(2):
                        cs_ = slice(half * 512, (half + 1) * 512)
                        gs_ = slice(br * 1024 + half * 512, br * 1024 + (half + 1) * 512)
                        py = P.next()
                        if br == 0:
                            for kc in range(8):
                                fw.mm(py[:, :], oat[:, kc, :], wbr[:, kc, cs_], start=(kc == 0), stop=(kc == 7))
                        else:
                            for kc in range(4):
                                fw.mm(py[:, :], obt[:, kc, :], wbd[:, kc, cs_], start=(kc == 0), stop=(kc == 3))
                        pg = P.next()
                        for kc in range(8):
                            fw.mm(pg[:, :], xnv[:, kc, :], wmg[:, kc, gs_], start=(kc == 0), stop=False)
                        fw.mm(pg[:, :], ones_row_b, bmg[0:1, gs_], start=False, stop=True)
                        ga = gar.next()
                        fw.act(ga[:, :], pg[:, :], AF.Sigmoid)
                        fw.tt("dve", ma[:, cs_], py[:, :], ga[:, :], ALU.mult)
                    ms.append(ma)
                mr = mrr.next()
                fw.tt("pool", mr[:, :], ms[0][:, :], ms[1][:, :], ALU.add)
                c["mr"] = mr

            def s2(i, c):
                mr, xt = c["mr"], c["xt"]
                pt = P.next()
                ptb = pt[:, :].bitcast(BF16)
                for cc in range(8):
                    fw.tr(ptb[:, cc * 128:(cc + 1) * 128], mr[:, cc * 128:(cc + 1) * 128], ident_b)
                mT = mTr.next()
                fw.copy("act", mT[:, :, :], ptb.rearrange("p (c t) -> p c t", c=8))
                h1t = h1r.next()
                for half in range(2):
                    cs_ = slice(half * 512, (half + 1) * 512)
                    pm = P.next()
                    for kc in range(8):
                        fw.mm(pm[:, :], mT[:, kc, :], wmo[:, kc, cs_], start=(kc == 0), stop=(kc == 7))
                    fw.tt("dve", h1t[:, cs_], pm[:, :], xt[:, cs_], ALU.add)
                fw.dma(h1_d[i * 128:(i + 1) * 128, :], h1t[:, :], join=True)
                c["xs"] = norm_a(nb, h1t[:, :])

            def s3(i, c):
                xnv = V(xn_sub[i // 4], xn_t.t[:, :, i * 128:(i + 1) * 128])
                norm_b(nb, c["xs"], gcol["norm_x"], xnv)

            sL(0, None) if False else None
            ctxs = [dict() for _ in range(NT)]
            sL(0, ctxs[0])
            for step in range(NT + 2):
                if step + 1 < NT:
                    sL(step + 1, ctxs[step + 1])
                if step < NT:
                    s1(step, ctxs[step])
                if 0 <= step - 1 < NT:
                    s2(step - 1, ctxs[step - 1])
                if 0 <= step - 2 < NT:
                    s3(step - 2, ctxs[step - 2])
            fw.barrier()

    def phase_t1b():
        with ExitStack() as st:
            wxq = sb(st, "wxq", (128, 8, 1024), BF16)
            wxo = sb(st, "wxo", (128, 8, 1024), BF16)
            fw.dma(wxq[:, :, :], WB["w_xq"].full_v().rearrange("(c p) n -> p c n", p=128))
            fw.dma(wxo[:, :, :], WB["w_xo"].full_v().rearrange("(c p) n -> p c n", p=128))
            gq = sb(st, "gqx", (128, 256), F32)
            gk = sb(st, "gkx", (128, 256), F32)
            fw.dma(gq[:, :], dram_bc(SM["x_q_norm"], 256))
            fw.dma(gk[:, :], dram_bc(SM["x_k_norm"], 256))
            kmT = sb(st, "kmT", (128, 8, 256), BF16)
            Vm = sb(st, "Vm", (128, 2, 1024), BF16)
            PB = Rot([ps(st, "rP%d" % i, (128, 512), F32) for i in range(8)])
            p1 = PB
            nb = norm_bufs(st, "X") + (PB,)
            ssr = Rot([sb(st, "ssXh%d" % i, (128, 12), F32) for i in range(3)])
            jkr = Rot([sb(st, "jkXh%d" % i, (128, 256), BF16) for i in range(2)])
            qnr = Rot([sb(st, "qn%d" % i, (128, 1024), BF16) for i in range(2)])

            def head_norm(psrc, gain, dst):
                ss = ssr.next()
                jk = jkr.next()
                for h in range(4):
                    fw.act(jk[:, :], psrc[:, h * 256:(h + 1) * 256], AF.Square, accum=ss[:, h:h + 1])
                fw.act(ss[:, 4:8], ss[:, 0:4], AF.Ln, scale=1.0 / 256, bias=eps_c)
                fw.act(ss[:, 8:12], ss[:, 4:8], AF.Exp, scale=-0.5)
                for h in range(4):
                    fw.stt("dve", dst[:, h * 256:(h + 1) * 256], psrc[:, h * 256:(h + 1) * 256], ss[:, 8 + h:9 + h],
                           gain[:, :], ALU.mult, ALU.mult)

            with ExitStack() as st2:
                wkv = sb(st2, "wkv", (128, 8, 2048), BF16)
                fw.dma(wkv[:, :, :], WB["w_xkv"].full_v().rearrange("(c p) n -> p c n", p=128))
                mnT = sb(st2, "mnT", (128, 8, 256), BF16)
                mtr = Rot([sb(st2, "mt%d" % i, (128, 1024), F32) for i in range(2)])
                for m in range(2):
                    mt = mtr.next()
                    fw.dma(mt[:, :], mem[m * 128:(m + 1) * 128, :])
                    norm_T(nb, mt[:, :], gcol["norm_mem"], mnT[:, :, m * 128:(m + 1) * 128])
                for m in range(2):
                    pk = []
                    for half in range(2):
                        cs_ = slice(half * 512, (half + 1) * 512)
                        p_ = PB.next()
                        for kc in range(8):
                            fw.mm(p_[:, :], mnT[:, kc, m * 128:(m + 1) * 128], wkv[:, kc, cs_],
                                  start=(kc == 0), stop=(kc == 7))
                        pk.append(p_)
                        pv_ = PB.next()
                        for kc in range(8):
                            fw.mm(pv_[:, :], mnT[:, kc, m * 128:(m + 1) * 128],
                                  wkv[:, kc, 1024 + half * 512:1024 + (half + 1) * 512], start=(kc == 0), stop=(kc == 7))
                        fw.copy("act", Vm[:, m, cs_], pv_[:, :])
                    srcs = [pk[h // 2][:, (h % 2) * 256:(h % 2 + 1) * 256] for h in range(4)]
                    hnk = HeadNorm(st2, "K%d" % m, 1)
                    ssk = hnk.stats(srcs)
                    kn = qnr.next()
                    for h in range(4):
                        fw.stt("dve", kn[:, h * 256:(h + 1) * 256], srcs[h], ssk[:, 8 + h:9 + h], gk[:, :], ALU.mult, ALU.mult)
                    pt = PB.next()
                    ptb = pt[:, :].bitcast(BF16)
                    for cc in range(8):
                        fw.tr(ptb[:, cc * 128:(cc + 1) * 128], kn[:, cc * 128:(cc + 1) * 128], ident_b)
                    fw.copy("act", kmT[:, :, m * 128:(m + 1) * 128], ptb.rearrange("p (c t) -> p c t", c=8))
                fw.barrier()

            h1r = Rot([sb(st, "h1x%d" % i, (128, 1024), F32) for i in range(5)])
            qTr = Rot([sb(st, "qTx%d" % i, (128, 8, 128), BF16) for i in range(2)])
            ptsr = Rot([sb(st, "ptsx%d" % i, (128, 1024), BF16) for i in range(2)])
            rdr = Rot([sb(st, "rdx%d" % i, (128, 512), F32) for i in range(2)])
            oxr = Rot([sb(st, "oxT%d" % i, (128, 8, 128), BF16) for i in range(2)])
            h2r = Rot([sb(st, "h2x%d" % i, (128, 1024), F32) for i in range(2)])
            qn2r = Rot([sb(st, "qn2_%d" % i, (128, 1024), BF16) for i in range(2)])
            ones_b = cstb[:, 6, :]
            hn = HeadNorm(st, "X")

            def sL(i, c):
                c["h1"] = h1r.next()
                fw.dma(c["h1"][:, :], h1_d[i * 128:(i + 1) * 128, :])

            def s1(i, c):
                xnv = V(xn_ro, xn_t.t[:, :, i * 128:(i + 1) * 128])
                pq = []
                for half in range(2):
                    cs_ = slice(half * 512, (half + 1) * 512)
                    p_ = PB.next()
                    for kc in range(8):
                        fw.mm(p_[:, :], xnv[:, kc, :], wxq[:, kc, cs_], start=(kc == 0), stop=(kc == 7))
                    pq.append(p_)
                srcs = [pq[h // 2][:, (h % 2) * 256:(h % 2 + 1) * 256] for h in range(4)]
                ss = hn.stats(srcs)
                qn = qn2r.next()
                for h in range(4):
                    fw.stt("dve", qn[:, h * 256:(h + 1) * 256], srcs[h], ss[:, 8 + h:9 + h], gq[:, :], ALU.mult, ALU.mult)
                c["qn"] = qn

            def s2(i, c):
                qn = c["qn"]
                pt = PB.next()
                ptb = pt[:, :].bitcast(BF16)
                for cc in range(8):
                    fw.tr(ptb[:, cc * 128:(cc + 1) * 128], qn[:, cc * 128:(cc + 1) * 128], ident_b)
                qT = qTr.next()
                fw.copy("act", qT[:, :, :], ptb.rearrange("p (c t) -> p c t", c=8))
                pts = ptsr.next()
                for hb in range(2):
                    pS2 = PB.next()
                    for hl in range(2):
                        h = hb * 2 + hl
                        for mb in range(2):
                            sl = slice((hl * 2 + mb) * 128, (hl * 2 + mb + 1) * 128)
                            for dc in range(2):
                                fw.mm(pS2[:, sl], kmT[:, h * 2 + dc, mb * 128:(mb + 1) * 128], qT[:, h * 2 + dc, :],
                                      start=(dc == 0), stop=(dc == 1))
                    fw.act(pts[:, hb * 512:(hb + 1) * 512], pS2[:, :], AF.Exp, scale=1.0 / 16)
                c["pts"] = pts

            def s3(i, c):
                pts = c["pts"]
                pD = PB.next()
                for h in range(4):
                    for mb in range(2):
                        fw.mm(pD[:, h * 128:(h + 1) * 128], ones_b, pts[:, (h * 2 + mb) * 128:(h * 2 + mb + 1) * 128],
                              start=(mb == 0), stop=(mb == 1))
                rd = rdr.next()
                fw.recip(rd[:, :], pD[:, :])
                ox = oxr.next()
                for hb in range(2):
                    pO2 = PB.next()
                    for hl in range(2):
                        h = hb * 2 + hl
                        for dc in range(2):
                            sl = slice((hl * 2 + dc) * 128, (hl * 2 + dc + 1) * 128)
                            for mb in range(2):
                                fw.mm(pO2[:, sl], Vm[:, mb, h * 256 + dc * 128:h * 256 + (dc + 1) * 128],
                                      pts[:, (h * 2 + mb) * 128:(h * 2 + mb + 1) * 128], start=(mb == 0), stop=(mb == 1))
                    fw.tt("dve", ox[:, hb * 4:(hb + 1) * 4, :].rearrange("p (h c) t -> p h c t", h=2),
                          pO2[:, :].rearrange("p (h c t) -> p h c t", h=2, c=2),
                          rd[:, hb * 256:(hb + 1) * 256].raw(0, [[128, 2], [0, 2], [1, 128]]), ALU.mult)
                c["ox"] = ox

            def s4(i, c):
                ox, h1t = c["ox"], c["h1"]
                h2t = h2r.next()
                for half in range(2):
                    cs_ = slice(half * 512, (half + 1) * 512)
                    px = PB.next()
                    for kc in range(8):
                        fw.mm(px[:, :], ox[:, kc, :], wxo[:, kc, cs_], start=(kc == 0), stop=(kc == 7))
                    fw.tt("dve", h2t[:, cs_], px[:, :], h1t[:, cs_], ALU.add)
                fw.dma(h2_d[i * 128:(i + 1) * 128, :], h2t[:, :], join=True)

            ctxs = [dict() for _ in range(NT)]
            sL(0, ctxs[0])
            for step in range(NT + 3):
                if step + 1 < NT:
                    sL(step + 1, ctxs[step + 1])
                for k, fn in enumerate((s1, s2, s3, s4)):
                    i = step - k
                    if 0 <= i < NT:
                        fn(i, ctxs[i])
            fw.barrier()

    def phase_t2():
        with ExitStack() as st:
            wup = sb(st, "wup", (128, 8, 2 * DFF), BF16)
            wdn = sb(st, "wdn", (128, NFC, 1024), BF16)
            wcb = sb(st, "wcb", (128, 4 * NFC), F32)
            crow = sb(st, "crow", (4 * NFC, 128), F32)
            fw.dma(crow[0:3 * NFC, :], SM["w_ffn_conv"].full_v().rearrange("t (c p) -> (t c) p", p=128))
            fw.dma(crow[3 * NFC:4 * NFC, :], SM["b_ffn_conv"].full_v().rearrange("(c p) -> c p", p=128))
            hist = sb(st, "hist", (128, NFC, 2), F32)
            hsub = [hist.sub("hist%d" % fc) for fc in range(NFC)]
            for fc in range(NFC):
                fw.memset("pool", V(hsub[fc], hist.t[:, fc, :]), 0.0)
            p1 = Rot([ps(st, "s1_%d" % i, (128, 512), F32) for i in range(4)])
            p2 = Rot([ps(st, "s2_%d" % i, (128, 1024), F32) for i in range(2)])
            nb = norm_bufs(st, "F", 1, 4) + (p1,)
            cps = p1.next()
            fw.tr(cps[:, 0:4 * NFC], crow[:, :], cst[0:4 * NFC, 0, 0:4 * NFC])
            fw.copy("dve", wcb[:, :], cps[:, 0:4 * NFC])
            h2r = Rot([sb(st, "h2f%d" % i, (128, 1024), F32) for i in range(2)])
            x3r = Rot([sb(st, "x3T%d" % i, (128, 8, 512), BF16) for i in range(1)])
            abr = Rot([sb(st, "abt%d" % i, (128, 514), F32) for i in range(2)])
            c1r = Rot([sb(st, "cv1_%d" % i, (128, 512), F32) for i in range(2)])
            c2r = Rot([sb(st, "cv2_%d" % i, (128, 512), F32) for i in range(2)])
            glr = Rot([sb(st, "gl%d" % i, (128, 512), F32) for i in range(2)])
            yT = sb(st, "yT", (128, NFC, 512), BF16)
            otr = Rot([sb(st, "ot%d" % i, (128, 1024), F32) for i in range(1)])

            def stage_na(s_):
                xsl = []
                for t_ in range(4):
                    i = s_ * 4 + t_
                    h2t = h2r.next()
                    fw.dma(h2t[:, :], h2_d[i * 128:(i + 1) * 128, :])
                    xsl.append(norm_a(nb, h2t[:, :]))
                return xsl

            def stage_nb(xsl):
                x3 = x3r.next()
                for t_ in range(4):
                    norm_b(nb, xsl[t_], gcol["norm_ffn"], x3[:, :, t_ * 128:(t_ + 1) * 128])
                return x3

            xsl0 = stage_na(0)
            wupv = WB["w_ffn_up"].full_v().rearrange("(c p) n -> p c n", p=128)
            for cb in range(4):
                for base in (0, DFF):
                    c0_ = base + cb * 704
                    fw.dma(wup[:, :, c0_:c0_ + 704], wupv[:, :, c0_:c0_ + 704])
            wdnv = WB["w_ffn_down"].full_v().rearrange("(c p) n -> p c n", p=128)
            for c0 in range(0, NFC, 6):
                c1_ = min(NFC, c0 + 6)
                fw.dma(wdn[:, c0:c1_, :], wdnv[:, c0:c1_, :])
            x3n = stage_nb(xsl0)
            for s_ in range(NS):
                x3 = x3n
                xsl_next = [] if s_ + 1 < NS else None
                pend = {}
                for fc in range(NFC):
                    if xsl_next is not None and fc % 5 == 0 and fc // 5 < 4:
                        t_ = fc // 5
                        i_ = (s_ + 1) * 4 + t_
                        h2n = h2r.next()
                        fw.dma(h2n[:, :], h2_d[i_ * 128:(i_ + 1) * 128, :])
                        pend[t_] = h2n
                    if xsl_next is not None and fc % 5 == 3 and fc // 5 < 4:
                        xsl_next.append(norm_a(nb, pend[fc // 5][:, :]))
                    pa = p1.next()
                    for kc in range(8):
                        fw.mm(pa[:, :], wup[:, kc, fc * 128:(fc + 1) * 128], x3[:, kc, :], start=(kc == 0), stop=(kc == 7))
                    pu = p1.next()
                    for kc in range(8):
                        fw.mm(pu[:, :], wup[:, kc, DFF + fc * 128:DFF + (fc + 1) * 128], x3[:, kc, :],
                              start=(kc == 0), stop=(kc == 7))
                    ab = abr.next()
                    hv = V(hsub[fc], hist.t[:, fc, :])
                    fw.copy("pool", ab[:, 0:2], hv)
                    fw.copy("act", ab[:, 2:514], pa[:, :])
                    c1 = c1r.next()
                    c2 = c2r.next()
                    fw.ts("dve", c1[:, :], ab[:, 2:514], wcb[:, 2 * NFC + fc:2 * NFC + fc + 1], ALU.mult, wcb[:, 3 * NFC + fc:3 * NFC + fc + 1], ALU.add)
                    fw.stt("dve", c2[:, :], ab[:, 1:513], wcb[:, NFC + fc:NFC + fc + 1], c1[:, :], ALU.mult, ALU.add)
                    fw.stt("dve", c1[:, :], ab[:, 0:512], wcb[:, fc:fc + 1], c2[:, :], ALU.mult, ALU.add)
                    fw.copy("pool", hv, ab[:, 512:514])
                    gl = glr.next()
                    fw.act(gl[:, :], c1[:, :], AF.Gelu)
                    fw.tt("dve", yT[:, fc, :], gl[:, :], pu[:, :], ALU.mult)
                if xsl_next is not None:
                    x3n = stage_nb(xsl_next)
                for t_ in range(4):
                    i = s_ * 4 + t_
                    h2t = h2r.next()
                    fw.dma(h2t[:, :], h2_d[i * 128:(i + 1) * 128, :])
                    pdn = p2.next()
                    for half in range(2):
                        cs_ = slice(half * 512, (half + 1) * 512)
                        for fc in range(NFC):
                            fw.mm(pdn[:, cs_], yT[:, fc, t_ * 128:(t_ + 1) * 128], wdn[:, fc, cs_],
                                  start=(fc == 0), stop=(fc == NFC - 1))
                    ot = otr.next()
                    fw.tt("dve", ot[:, :], pdn[:, :], h2t[:, :], ALU.add)
                    fw.dma(out[i * 128:(i + 1) * 128, :], ot[:, :], join=True)
            fw.barrier()

    def dump_f32(src_d):
        with ExitStack() as st:
            tmp = Rot([sb(st, "dbgf%d" % i, (128, 1024), F32) for i in range(2)])
            for i in range(NT):
                t_ = tmp.next()
                fw.dma(t_[:, :], src_d[i * 128:(i + 1) * 128, :])
                fw.dma(dbg_out[i * 128:(i + 1) * 128, :], t_[:, :])
            fw.barrier()

    phase_t1a()
    if dbg is not None and dbg[0] == "T1a":
        dump_f32(h1_d)
        return finish()
    phase_t1b()
    if dbg is not None and dbg[0] == "T1b":
        dump_f32(h2_d)
        return finish()
    xstack.close()
    phase_t2()
    return finish()


def make_in_maps(inputs):
    consts = make_consts()
    maps = []
    for b in range(8):
        m = {"x": np.ascontiguousarray(inputs["x"][b]), "mem": np.ascontiguousarray(inputs["mem"][b]),
             "positions": np.ascontiguousarray(inputs["positions"][b]).astype(np.int32), "consts": consts}
        for k in BIG_W:
            m[k] = np.ascontiguousarray(np.asarray(inputs[k])[0])
        for k in SMALL:
            m[k] = np.ascontiguousarray(np.asarray(inputs[k])[0])
        maps.append(m)
    return maps


def kernel(**inputs):
    inputs = {k: np.asarray(v) for k, v in inputs.items()}
    nc, fw = build()
    res = run_bass_kernel_spmd(nc, make_in_maps(inputs), core_ids=list(range(8)))
    return np.stack([np.asarray(r["out"]) for r in res.results], axis=0).astype(np.float32)
```

```python
import numpy as np
import concourse.bass as bass
import concourse.mybir as mybir

F32 = mybir.dt.float32
BF16 = mybir.dt.bfloat16
I32 = mybir.dt.int32
AF = mybir.ActivationFunctionType
ALU = mybir.AluOpType
AX = mybir.AxisListType

SAME_ENGINE_SYNC = True
N_DMA_SEMS = 24


class Buf:
    __slots__ = ("t", "w", "r", "name")

    def __init__(self, t, name=""):
        self.t = t
        self.w = []
        self.r = {}
        self.name = name

    def sub(self, name=""):
        return Buf(self.t, name)

    def __getitem__(self, idx):
        return V(self, self.t[idx])

    def full_v(self):
        return V(self, self.t)


class V:
    __slots__ = ("b", "ap")

    def __init__(self, b, ap):
        self.b = b
        self.ap = ap

    def __getitem__(self, idx):
        return V(self.b, self.ap[idx])

    def rearrange(self, s, **kw):
        return V(self.b, self.ap.rearrange(s, **kw))

    def bcast(self, shape):
        return V(self.b, self.ap.to_broadcast(shape))

    def bitcast(self, dt):
        return V(self.b, self.ap.bitcast(dt))

    def raw(self, extra_off, dims):
        a = self.ap
        return V(self.b, bass.AP(a.tensor, a.offset + extra_off, [list(a.ap[0])] + [list(d) for d in dims]))

    @property
    def shape(self):
        return self.ap.shape


class Instr:
    __slots__ = ("stream", "fn", "deps", "signal", "semval", "dma", "dsem", "dval", "idx")

    def __init__(self, stream, fn, dma=False):
        self.stream = stream
        self.fn = fn
        self.deps = []
        self.signal = False
        self.semval = None
        self.dma = dma
        self.dsem = None
        self.dval = None


class FW:
    STREAMS = ("pe", "act", "dve", "pool", "sp")

    def __init__(self, nc):
        self.nc = nc
        self.prog = {s: [] for s in self.STREAMS}
        self.dma_q = {"sp": (0, N_DMA_SEMS), "pool": (N_DMA_SEMS, 8), "act": (N_DMA_SEMS + 8, 8)}
        self.n_dsem = N_DMA_SEMS + 16
        self.dma_rr = {q: 0 for q in self.dma_q}
        self.dma_last = [None] * self.n_dsem
        self.dma_cnt = [0] * self.n_dsem
        self.all_dmas = []
        self.order = []

    def _add(self, stream, fn, reads, writes, dma=False, join=False, xr=(), xw=()):
        ins = Instr(stream, fn, dma)
        deps = []
        reads = list(reads) + list(xr)
        writes = list(writes) + list(xw)
        wb = set(id(v.b) for v in writes)
        for v in reads:
            b = v.b
            if id(b) in wb:
                continue
            for w in b.w:
                deps.append((w, 0))
        for v in writes:
            b = v.b
            if not join:
                for w in b.w:
                    deps.append((w, 0 if any(v2.b is b for v2 in reads) else 1))
            for r in b.r.values():
                deps.append((r, 1))
        if dma:
            base, cnt = self.dma_q[stream]
            k = base + self.dma_rr[stream]
            self.dma_rr[stream] = (self.dma_rr[stream] + 1) % cnt
            if self.dma_last[k] is not None:
                deps.append((self.dma_last[k], 0))
            self.dma_last[k] = ins
            self.dma_cnt[k] += 16
            ins.dsem = k
            ins.dval = self.dma_cnt[k]
            self.all_dmas.append(ins)
        seen = set()
        for d, kind in deps:
            if d is ins or id(d) in seen:
                continue
            if (not d.dma) and (not dma) and d.stream == stream:
                if stream == "pe" or not SAME_ENGINE_SYNC:
                    continue
            seen.add(id(d))
            ins.deps.append(d)
            d.signal = True
        rkey = ("dma", ins.dsem) if dma else stream
        for v in reads:
            if id(v.b) not in wb:
                v.b.r[rkey] = ins
        for v in writes:
            if join:
                v.b.w = v.b.w + [ins]
            else:
                v.b.w = [ins]
                v.b.r = {}
        self.prog[stream].append(ins)
        return ins

    def barrier(self):
        lasts = []
        for s in self.STREAMS:
            for ins in reversed(self.prog[s]):
                if not ins.dma and ins.fn is not None:
                    lasts.append(ins)
                    break
        lasts = lasts + [d for d in self.dma_last if d is not None]
        for s in self.STREAMS:
            ins = Instr(s, None)
            for d in lasts:
                if (not d.dma) and d.stream == s:
                    continue
                ins.deps.append(d)
                d.signal = True
            self.prog[s].append(ins)

    def wait_for(self, stream, instrs):
        ins = Instr(stream, None)
        for d in instrs:
            ins.deps.append(d)
            d.signal = True
        self.prog[stream].append(ins)

    def dma(self, out, in_, q="sp", join=False, **kw):
        o, i = out.ap, in_.ap
        return self._add(q, lambda e: e.dma_start(out=o, in_=i, **kw), [in_], [out], dma=True, join=join)

    def mm(self, out, lhsT, rhs, start=True, stop=True, xr=()):
        o, l, r = out.ap, lhsT.ap, rhs.ap
        return self._add("pe", lambda e: e.matmul(o, l, r, start=start, stop=stop), [lhsT, rhs], [out], xr=xr)

    def tr(self, out, in_, ident):
        o, i, d = out.ap, in_.ap, ident.ap
        return self._add("pe", lambda e: e.transpose(o, i, d), [in_, ident], [out])

    def act(self, out, in_, func, bias=None, scale=None, accum=None, xr=(), xw=()):
        o, i = out.ap, in_.ap
        kw = {}
        reads = [in_]
        writes = [out]
        if bias is not None:
            if isinstance(bias, V):
                kw["bias"] = bias.ap
                reads.append(bias)
            else:
                kw["bias"] = bias
        if scale is not None:
            if isinstance(scale, V):
                kw["scale"] = scale.ap
                reads.append(scale)
            else:
                kw["scale"] = scale
        if accum is not None:
            kw["accum_out"] = accum.ap
            writes.append(accum)
        return self._add("act", lambda e: e.activation(o, i, func, **kw), reads, writes, xr=xr, xw=xw)

    def _veng(self, eng):
        return eng

    def tt(self, eng, out, in0, in1, op):
        o, a, b = out.ap, in0.ap, in1.ap
        return self._add(eng, lambda e: e.tensor_tensor(o, a, b, op), [in0, in1], [out])

    def ts(self, eng, out, in0, s1, op0, s2=None, op1=None, accum=None):
        o, a = out.ap, in0.ap
        reads = [in0]
        writes = [out]
        if isinstance(s1, V):
            reads.append(s1)
            s1 = s1.ap
        if isinstance(s2, V):
            reads.append(s2)
            s2 = s2.ap
        kw = {}
        if op1 is not None:
            kw["op1"] = op1
        if accum is not None:
            kw["accum_out"] = accum.ap
            writes.append(accum)
        return self._add(eng, lambda e: e.tensor_scalar(o, a, s1, s2, op0, **kw), reads, writes)

    def stt(self, eng, out, in0, scalar, in1, op0, op1):
        o, a, b = out.ap, in0.ap, in1.ap
        reads = [in0, in1]
        if isinstance(scalar, V):
            reads.append(scalar)
            scalar = scalar.ap
        return self._add(eng, lambda e: e.scalar_tensor_tensor(o, a, scalar, b, op0, op1), reads, [out])

    def copy(self, eng, out, in_):
        o, i = out.ap, in_.ap
        if eng == "act":
            return self._add("act", lambda e: e.copy(o, i), [in_], [out])
        return self._add(eng, lambda e: e.tensor_copy(o, i), [in_], [out])

    def reduce(self, eng, out, in_, op, axis=AX.X):
        o, i = out.ap, in_.ap
        return self._add(eng, lambda e: e.tensor_reduce(o, i, axis, op), [in_], [out])

    def recip(self, out, in_):
        o, i = out.ap, in_.ap
        return self._add("dve", lambda e: e.reciprocal(o, i), [in_], [out])

    def memset(self, eng, out, val):
        o = out.ap
        return self._add(eng, lambda e: e.memset(o, val), [], [out])

    def emit(self):
        nc = self.nc
        from contextlib import ExitStack
        with ExitStack() as es:
            esem = {s: es.enter_context(nc.semaphore("s_" + s)) for s in self.STREAMS if s != "sp" or True}
            dsem = [es.enter_context(nc.semaphore("d%d" % k)) for k in range(self.n_dsem)]
            self.barrier()
            for s in self.STREAMS:
                c = 0
                for ins in self.prog[s]:
                    if ins.dma or ins.fn is None:
                        continue
                    if ins.signal:
                        c += 1
                        ins.semval = c
            plan = {}
            nwaits = 0
            for s in self.STREAMS:
                known = {}
                acts = []
                for ins in self.prog[s]:
                    for d in ins.deps:
                        if d.dma:
                            key, val, sem = ("d", d.dsem), d.dval, dsem[d.dsem]
                        else:
                            key, val, sem = ("e", d.stream), d.semval, esem[d.stream]
                        assert val is not None
                        if known.get(key, 0) >= val:
                            continue
                        known[key] = val
                        acts.append(("w", sem, val))
                        nwaits += 1
                    if ins.fn is not None:
                        acts.append(("i", ins))
                plan[s] = acts

            def run(eng, acts, s):
                for a in acts:
                    if a[0] == "w":
                        eng.wait_ge(a[1], a[2])
                    else:
                        ins = a[1]
                        bi = ins.fn(eng)
                        if ins.dma:
                            bi.then_inc(dsem[ins.dsem], 16)
                        elif ins.signal:
                            bi.then_inc(esem[s], 1)

            with nc.Block() as block:
                @block.sync
                def _(e):
                    run(e, plan["sp"], "sp")

                @block.tensor
                def _(e):
                    run(e, plan["pe"], "pe")

                @block.scalar
                def _(e):
                    run(e, plan["act"], "act")

                @block.vector
                def _(e):
                    run(e, plan["dve"], "dve")

                @block.gpsimd
                def _(e):
                    run(e, plan["pool"], "pool")
            self.stats = {s: len(self.prog[s]) for s in self.STREAMS}
            self.stats["waits"] = nwaits
from concourse.bass_utils import run_bass_kernel_spmd

import math
import ml_dtypes
from contextlib import ExitStack

S = 4096
D = 1024
NT = S // 128
NS = S // 512
NMEM = 256
DFF = 2816
NFC = DFF // 128
EPS = 1e-6
IN_COLS = 7696
OFF_QA, OFF_KA, OFF_VA, OFF_RA, OFF_ZA, OFF_QB, OFF_KB, OFF_VB = 0, 512, 1024, 2048, 3072, 3088, 4624, 6160
DILS = (1, 4, 16)
PI = math.pi

BIG_W = {
    "w_in": (1024, IN_COLS), "w_br_gla": (1024, 1024), "w_br_dil": (512, 1024),
    "w_merge_gate": (1024, 2048), "w_mix_out": (1024, 1024), "w_xq": (1024, 1024),
    "w_xkv": (1024, 2048), "w_xo": (1024, 1024), "w_ffn_up": (1024, 2 * DFF),
    "w_ffn_down": (DFF, 1024),
}
SMALL = {
    "norm_mix": (1024,), "w_gla_gate": (16, 512), "b_gla_gate": (512,), "gla_out_norm": (256,),
    "dil_q_norm": (64,), "dil_k_norm": (64,), "b_merge_gate": (2048,), "norm_x": (1024,),
    "norm_mem": (1024,), "x_q_norm": (256,), "x_k_norm": (256,), "norm_ffn": (1024,),
    "w_ffn_conv": (3, DFF), "b_ffn_conv": (DFF,),
}


class Rot:
    def __init__(self, items):
        self.items = items
        self.i = 0

    def next(self):
        it = self.items[self.i % len(self.items)]
        self.i += 1
        return it


def make_consts():
    c = np.zeros((8, 128, 128), np.float32)
    i = np.arange(128)
    c[0] = np.eye(128)
    c[1] = (i[:, None] <= i[None, :])
    c[2] = (i[:, None] > i[None, :])
    c[3] = (i[:, None] >= i[None, :])
    c[6] = 1.0
    inv_freq = (500000.0 ** (-np.arange(0, 16, 2, dtype=np.float32) / 16)).astype(np.float32)
    c[7, :, 0:8] = inv_freq[None, :]
    return c


def build(dbg=None, stop_after=None):
    nc = bass.Bass("TRN2", target_bir_lowering=False)
    fw = FW(nc)
    es = ExitStack()

    def din(name, shape, dt=F32):
        return Buf(nc.dram_tensor(name, list(shape), dt, kind="ExternalInput").ap(), name)

    def dscr(name, shape, dt):
        return Buf(nc.dram_tensor(name, list(shape), dt, kind="Internal").ap(), name)

    def dout(name, shape, dt=F32):
        return Buf(nc.dram_tensor(name, list(shape), dt, kind="ExternalOutput").ap(), name)

    x = din("x", (S, D))
    mem = din("mem", (NMEM, D))
    pos = din("positions", (S,), I32)
    consts = din("consts", (8, 128, 128))
    W32 = {k: din(k, v) for k, v in BIG_W.items()}
    SM = {k: din(k, v) for k, v in SMALL.items()}
    out = dout("out", (S, D))
    dbg_out = None
    if dbg is not None:
        dbg_out = dout("dbg", dbg[1], dbg[2])

    WB = {k: dscr(k + "_bf", v, BF16) for k, v in BIG_W.items()}
    oaT_d = dscr("oaT_d", (1024, S), BF16)
    obT_d = dscr("obT_d", (512, S), BF16)
    h1_d = dscr("h1_d", (S, D), F32)
    h2_d = dscr("h2_d", (S, D), F32)

    def sb(stack, name, shape, dt):
        return Buf(stack.enter_context(nc.sbuf_tensor(name, list(shape), dt))[:], name)

    def ps(stack, name, shape, dt):
        return Buf(stack.enter_context(nc.psum_tensor(name, list(shape), dt))[:], name)

    def dram_bc(buf, n, parts=128, off=0):
        a = buf.t
        return V(buf, bass.AP(a.tensor, a.offset + off, [[0, parts], [1, n]]))

    xstack = ExitStack()

    def finish():
        fw.emit()
        stB1w.close()
        xstack.close()
        es.close()
        return nc, fw

    win_gla_b = WB["w_in"].sub("w_in_gla")
    win_dil_b = WB["w_in"].sub("w_in_dil")
    top = es
    cst = sb(top, "cst", (128, 8, 128), F32)
    cstb = sb(top, "cstb", (128, 8, 128), BF16)
    epsb = sb(top, "epsb", (128, 2), F32)
    gall = sb(top, "gall", (128, 32), F32)
    gcol = {k: V(gall, gall.t[:, 8 * j:8 * j + 8]) for j, k in enumerate(("norm_mix", "norm_x", "norm_mem", "norm_ffn"))}

    fw.dma(cst[:, :, :], consts.full_v().rearrange("c p f -> p c f"))
    fw.copy("dve", cstb[:, :, :], cst[:, :, :])
    fw.memset("dve", epsb[:, 0:1], EPS)
    fw.memset("dve", epsb[:, 1:2], 1.0)
    with ExitStack() as st0:
        grow = sb(st0, "grow", (32, 128), F32)
        gps = ps(st0, "gps", (128, 512), F32)
        for j, k in enumerate(("norm_mix", "norm_x", "norm_mem", "norm_ffn")):
            fw.dma(grow[8 * j:8 * j + 8, :], SM[k].full_v().rearrange("(c p) -> c p", p=128))
        fw.tr(gps[:, 0:32], grow[:, :], cst[0:32, 0, 0:32])
        fw.copy("dve", gall[:, :], gps[:, 0:32])
        fw.barrier()
    ident_b = cstb[:, 0, :]
    ones_row_b = cstb[0:1, 6, :]
    eps_c = epsb[:, 0:1]
    one_c = epsb[:, 1:2]

    xn_t = sb(xstack, "xnT", (128, 8, S), BF16)
    xn_sub = [xn_t.sub("xn%d" % j) for j in range(NS)]
    xn_ro = xn_t.sub("xn_ro")

    def xn_w(i):
        return V(xn_sub[i // 4], xn_t.t[:, :, i * 128:(i + 1) * 128])

    def norm_a(nb, xt):
        junk, ssr, xsr, ptr = nb
        jk = junk.next()
        ss = ssr.next()
        fw.act(jk[:, :], xt, AF.Square, accum=ss[:, 0:1])
        fw.act(ss[:, 1:2], ss[:, 0:1], AF.Ln, scale=1.0 / 1024, bias=eps_c)
        fw.act(ss[:, 2:3], ss[:, 1:2], AF.Exp, scale=-0.5)
        xs = xsr.next()
        fw.ts("dve", xs[:, :], xt, ss[:, 2:3], ALU.mult)
        return xs

    def norm_b(nb, xs, g, dst_view):
        junk, ssr, xsr, ptr = nb
        pt = ptr.next()
        ptb = pt[:, :].bitcast(BF16)
        for c in range(8):
            fw.tr(ptb[:, c * 128:(c + 1) * 128], xs[:, c * 128:(c + 1) * 128], ident_b)
        fw.tt("dve", dst_view, ptb.rearrange("p (c t) -> p c t", c=8),
              g[:, :].raw(0, [[1, 8], [0, 128]]), ALU.mult)

    def norm_T(nb, xt, g, dst_view):
        norm_b(nb, norm_a(nb, xt), g, dst_view)

    class HeadNorm:
        def __init__(self, st, tag, n=3):
            self.slots = []
            for i in range(n):
                ss = sb(st, "hn_ss%s%d" % (tag, i), (128, 12), F32)
                jk = sb(st, "hn_jk%s%d" % (tag, i), (128, 4, 256), BF16)
                self.slots.append((ss, [ss.sub() for _ in range(4)], jk, [jk.sub() for _ in range(4)]))
            self.i = 0

        def stats(self, srcs):
            ss, ssub, jk, jsub = self.slots[self.i % len(self.slots)]
            self.i += 1
            for h in range(4):
                fw.act(V(jsub[h], jk.t[:, h, :]), srcs[h], AF.Square, accum=V(ssub[h], ss.t[:, h:h + 1]))
            fw.act(ss[:, 4:8], ss[:, 0:4], AF.Ln, scale=1.0 / 256, bias=eps_c,
                   xr=[V(ssub[h], ss.t[:, h:h + 1]) for h in range(4)])
            fw.act(ss[:, 8:12], ss[:, 4:8], AF.Exp, scale=-0.5)
            return ss

    def norm_bufs(st, tag, nj=2, nxs=2):
        return (Rot([sb(st, "junk%s%d" % (tag, i), (128, 1024), BF16) for i in range(nj)]),
                Rot([sb(st, "ss%s%d" % (tag, i), (128, 4), F32) for i in range(max(4, nxs + 1))]),
                Rot([sb(st, "xs%s%d" % (tag, i), (128, 1024), BF16) for i in range(nxs)]))

    win = V(win_gla_b, WB["w_in"].t.rearrange("(c p) n -> p c n", p=128))
    win_dil = V(win_dil_b, WB["w_in"].t.rearrange("(c p) n -> p c n", p=128))
    stB1w = ExitStack()
    wqk = sb(stB1w, "wqk", (128, 8, 1024), BF16)
    wv = sb(stB1w, "wv", (128, 8, 1024), BF16)
    wr = sb(stB1w, "wr", (128, 8, 1024), BF16)
    wz = sb(stB1w, "wz", (128, 8, 16), BF16)
    wgg = sb(stB1w, "wgg", (16, 512), BF16)
    bgg = sb(stB1w, "bgg", (1, 512), BF16)
    gon = sb(stB1w, "gon", (128, 256), F32)
    fw.dma(gon[:, :], dram_bc(SM["gla_out_norm"], 256))
    fw.dma(wgg[:, :], SM["w_gla_gate"].full_v(), q="pool")
    fw.dma(bgg[:, :], SM["b_gla_gate"].full_v().rearrange("(o n) -> o n", o=1), q="pool")
    b1_loads = []
    with ExitStack() as st:
        xts = Rot([sb(st, "xt%d" % i, (128, 1024), F32) for i in range(4)])
        nb = norm_bufs(st, "A") + (Rot([ps(st, "ptA%d" % i, (128, 512), F32) for i in range(2)]),)
        xloads = []
        for i in range(NT):
            xt = xts.next()
            xloads.append(fw.dma(xt[:, :], x[i * 128:(i + 1) * 128, :]))
            if i == 3:
                fw.wait_for("pool", xloads)
                for (c0, c1, bsub) in ((0, 1544, win_gla_b), (1544, 3088, win_gla_b)):
                    for r0 in (0, 512):
                        fw.dma(V(bsub, WB["w_in"].t[r0:r0 + 512, c0:c1]),
                               V(W32["w_in"], W32["w_in"].t[r0:r0 + 512, c0:c1]), q="pool", join=True)
            if i == NT - 1:
                b1_loads.append(fw.dma(wqk[:, :, :], win[:, :, 0:1024]))
                b1_loads.append(fw.dma(wv[:, :, :], win[:, :, 1024:2048]))
                b1_loads.append(fw.dma(wr[:, :, :], win[:, :, 2048:3072]))
                b1_loads.append(fw.dma(wz[:, :, :], win[:, :, 3072:3088], allow_slow_non_contiguous=True))
            norm_T(nb, xt[:, :], gcol["norm_mix"], xn_w(i))
        fw.barrier()
    fw.wait_for("pool", b1_loads)
    deferred_conv = []
    for (c0, c1, bsub) in ((3088, 4624, win_dil_b), (4624, 6160, win_dil_b), (6160, 7696, win_dil_b)):
        for r0 in (0, 512):
            deferred_conv.append((V(bsub, WB["w_in"].t[r0:r0 + 512, c0:c1]),
                                  V(W32["w_in"], W32["w_in"].t[r0:r0 + 512, c0:c1])))
    for k, shp in BIG_W.items():
        if k == "w_in":
            continue
        n = shp[0] * shp[1]
        rows = n // 2048
        src = W32[k].full_v().rearrange("a b -> (a b)").rearrange("(r c) -> r c", c=2048)
        dst = WB[k].full_v().rearrange("a b -> (a b)").rearrange("(r c) -> r c", c=2048)
        r0 = 0
        while r0 < rows:
            r1 = min(rows, r0 + 512)
            deferred_conv.append((dst[r0:r1, :], src[r0:r1, :]))
            r0 = r1

    def emit_conv(after=None, n=1):
        for _ in range(n):
            if not deferred_conv:
                return
            if after is not None:
                fw.wait_for("pool", [after])
            d_, s_ = deferred_conv.pop(0)
            fw.dma(d_, s_, q="pool", join=True)

    emit_conv(None, 2)

    DKS = 128 ** -0.5
    oaT_v = oaT_d.full_v().rearrange("(c p) t -> p c t", p=128)
    with ExitStack() as st:
        S32 = [sb(st, "S32_%d" % h, (128, 256), F32) for h in range(4)]
        Sbf = [sb(st, "Sbf_%d" % h, (128, 256), BF16) for h in range(4)]
        for h in range(4):
            fw.memset("pool", S32[h][:, :], 0.0)
            fw.memset("pool", Sbf[h][:, :], 0.0)
        PB = Rot([ps(st, "bP%d" % i, (128, 512), F32) for i in range(8)])
        zar = Rot([sb(st, "za%d" % i, (16, 512), BF16) for i in range(2)])
        e1r = Rot([sb(st, "e1_%d" % i, (128, 512), F32) for i in range(1)])
        spb = [sb(st, "sp_%d" % i, (128, 512), F32) for i in range(4)]
        eTpr = Rot([sb(st, "eTp%d" % i, (128, 512), F32) for i in range(2)])
        eTnr = Rot([sb(st, "eTn%d" % i, (128, 512), F32) for i in range(2)])
        ervr = Rot([sb(st, "erv%d" % i, (128, 512), F32) for i in range(2)])
        elr = Rot([sb(st, "elast%d" % i, (128, 16), F32) for i in range(2)])
        qes = [[sb(st, "qe%d_%d" % (i, h), (128, 512), BF16) for h in range(4)] for i in range(2)]
        kes = [[sb(st, "ke%d_%d" % (i, h), (128, 512), BF16) for h in range(4)] for i in range(2)]
        kls = [[sb(st, "kl%d_%d" % (i, c4), (128, 512), BF16) for c4 in range(4)] for i in range(2)]
        vsr = Rot([sb(st, "vs%d" % i, (128, 1024), BF16) for i in range(2)])
        atr = Rot([sb(st, "at%d" % i, (128, 512), BF16) for i in range(3)])
        Gr = Rot([sb(st, "G%d" % i, (128, 1024), F32) for i in range(2)])
        onr = Rot([sb(st, "on%d" % i, (128, 1024), BF16) for i in range(2)])
        oTr = Rot([sb(st, "oT%d" % i, (128, 8, 128), BF16) for i in range(2)])
        hn = HeadNorm(st, "B", 2)
        pst = {}
        cst_ = {}

        def xs_(kc, s_):
            return V(xn_ro, xn_t.t[:, kc, s_ * 512:(s_ + 1) * 512])

        def xc_(kc, c):
            return V(xn_ro, xn_t.t[:, kc, c * 128:(c + 1) * 128])

        def P_a(s_):
            d = pst.setdefault(s_, {})
            d["set"] = s_ % 2
            pza = PB.next()
            for kc in range(8):
                fw.mm(pza[0:16, :], wz[:, kc, :], xs_(kc, s_), start=(kc == 0), stop=(kc == 7))
            za = zar.next()
            fw.copy("act", za[:, :], pza[0:16, :])
            for c4 in range(4):
                pz = PB.next()
                fw.mm(pz[:, :], za[:, c4 * 128:(c4 + 1) * 128], wgg[:, :], start=True, stop=False)
                fw.mm(pz[:, :], ones_row_b, bgg[:, :], start=False, stop=True)
                e1 = e1r.next()
                fw.act(e1[:, :], pz[:, :], AF.Exp, scale=-1.0)
                fw.act(spb[c4][:, :], e1[:, :], AF.Ln, bias=one_c, scale=1.0)
            d["el"] = elr.next()

        def P_b(s_, hs):
            d = pst[s_]
            el = d["el"]
            for h in hs:
                pc = PB.next()
                for c4 in range(4):
                    fw.mm(pc[:, c4 * 128:(c4 + 1) * 128], spb[c4][:, h * 128:(h + 1) * 128], cst[:, 1, :])
                eTp = eTpr.next()
                eTn = eTnr.next()
                fw.act(eTp[:, :], pc[:, :], AF.Exp, scale=-1.0 / 16)
                fw.act(eTn[:, :], pc[:, :], AF.Exp, scale=1.0 / 16)
                fw.copy("pool", el[:, h * 4:(h + 1) * 4], eTp[:, :].raw(127, [[128, 4]]))
                for which in range(2):
                    p_ = PB.next()
                    co = (0 if which == 0 else 512) + h * 128
                    for kc in range(8):
                        fw.mm(p_[:, :], wqk[:, kc, co:co + 128], xs_(kc, s_), start=(kc == 0), stop=(kc == 7))
                    if which == 0:
                        fw.stt("dve", qes[d["set"]][h][:, :], p_[:, :], DKS, eTp[:, :], ALU.mult, ALU.mult)
                    else:
                        fw.tt("dve", kes[d["set"]][h][:, :], p_[:, :], eTn[:, :], ALU.mult)

        def P_c(s_, c4):
            d = pst[s_]
            c = s_ * 4 + c4
            pr = PB.next()
            fw.mm(pr[:, :], cst[:, 2, :], spb[c4][:, :])
            erv = ervr.next()
            fw.act(erv[:, :], pr[:, :], AF.Exp, scale=-1.0 / 16)
            pkt = PB.next()
            for kc in range(8):
                fw.mm(pkt[:, :], xc_(kc, c), wqk[:, kc, 512:1024], start=(kc == 0), stop=(kc == 7))
            fw.tt("dve", kls[d["set"]][c4][:, :], pkt[:, :], erv[:, :], ALU.mult)

        def R1(c):
            s_, c4 = c // 4, c % 4
            d = pst[s_]
            cd = cst_.setdefault(c, {})
            qe, ke = qes[d["set"]], kes[d["set"]]
            tsl = slice(c4 * 128, (c4 + 1) * 128)
            pa = PB.next()
            for h in range(4):
                fw.mm(pa[:, h * 128:(h + 1) * 128], ke[h][:, tsl], qe[h][:, tsl])
            atm = atr.next()
            fw.tt("dve", atm[:, :].rearrange("p (h i) -> p h i", h=4), pa[:, :].rearrange("p (h i) -> p h i", h=4),
                  cst[:, 1, :].raw(0, [[0, 4], [1, 128]]), ALU.mult)
            cd["atm"] = atm
            vs = vsr.next()
            for half in range(2):
                pv = PB.next()
                for kc in range(8):
                    fw.mm(pv[:, :], xc_(kc, c), wv[:, kc, half * 512:(half + 1) * 512], start=(kc == 0), stop=(kc == 7))
                fw.copy("act", vs[:, half * 512:(half + 1) * 512], pv[:, :])
            cd["vs"] = vs
            G = Gr.next()
            for half in range(2):
                prr = PB.next()
                for kc in range(8):
                    fw.mm(prr[:, :], xc_(kc, c), wr[:, kc, half * 512:(half + 1) * 512], start=(kc == 0), stop=(kc == 7))
                fw.act(G[:, half * 512:(half + 1) * 512], prr[:, :], AF.Silu)
            fw.tt("pool", G[:, :].rearrange("p (h v) -> p h v", h=4), G[:, :].rearrange("p (h v) -> p h v", h=4),
                  gon[:, :].raw(0, [[0, 4], [1, 256]]), ALU.mult)
            cd["G"] = G

        def R2(c):
            s_, c4 = c // 4, c % 4
            d = pst[s_]
            cd = cst_[c]
            qe = qes[d["set"]]
            klw = kls[d["set"]][c4]
            vs = cd["vs"]
            el = d["el"]
            atm, G = cd["atm"], cd["G"]
            tsl = slice(c4 * 128, (c4 + 1) * 128)
            pos = []
            for hb in range(2):
                pob = PB.next()
                for hl in range(2):
                    h = hb * 2 + hl
                    fw.mm(pob[:, hl * 256:(hl + 1) * 256], atm[:, h * 128:(h + 1) * 128], vs[:, h * 256:(h + 1) * 256],
                          start=True, stop=False)
                    fw.mm(pob[:, hl * 256:(hl + 1) * 256], qe[h][:, tsl], Sbf[h][:, :], start=False, stop=True)
                pos.append(pob)
            for hb in range(2):
                pdb = PB.next()
                for hl in range(2):
                    h = hb * 2 + hl
                    fw.mm(pdb[:, hl * 256:(hl + 1) * 256], klw[:, h * 128:(h + 1) * 128], vs[:, h * 256:(h + 1) * 256])
                for hl in range(2):
                    h = hb * 2 + hl
                    fw.stt("dve", S32[h][:, :], S32[h][:, :], el[:, h * 4 + c4:h * 4 + c4 + 1],
                           pdb[:, hl * 256:(hl + 1) * 256], ALU.mult, ALU.add)
                    fw.copy("pool", Sbf[h][:, :], S32[h][:, :])
            srcs = [pos[h // 2][:, (h % 2) * 256:(h % 2 + 1) * 256] for h in range(4)]
            ss = hn.stats(srcs)
            on = onr.next()
            for h in range(4):
                fw.stt("dve", on[:, h * 256:(h + 1) * 256], srcs[h], ss[:, 8 + h:9 + h],
                       G[:, h * 256:(h + 1) * 256], ALU.mult, ALU.mult)
            cd["on"] = on

        def R3(c):
            cd = cst_[c]
            on = cd["on"]
            pt = PB.next()
            ptb = pt[:, :].bitcast(BF16)
            for cc in range(8):
                fw.tr(ptb[:, cc * 128:(cc + 1) * 128], on[:, cc * 128:(cc + 1) * 128], ident_b)
            oT = oTr.next()
            fw.copy("act", oT[:, :, :], ptb.rearrange("p (c t) -> p c t", c=8))
            st_ins = fw.dma(oaT_v[:, :, c * 128:(c + 1) * 128], oT[:, :, :], join=True)
            emit_conv(st_ins, 1)

        def P_quarter(s_, qi):
            if qi == 0:
                P_a(s_)
                P_b(s_, (0,))
            elif qi == 1:
                P_b(s_, (1, 2))
            elif qi == 2:
                P_b(s_, (3,))
                P_c(s_, 0)
                P_c(s_, 1)
            else:
                P_c(s_, 2)
                P_c(s_, 3)

        for qi in range(4):
            P_quarter(0, qi)
        R1(0)
        for c in range(NT):
            s_, c4 = c // 4, c % 4
            R2(c)
            if c > 0:
                R3(c - 1)
            if s_ + 1 < NS:
                P_quarter(s_ + 1, c4)
            if c + 1 < NT:
                R1(c + 1)
        R3(NT - 1)
        emit_conv(None, 1000)
        fw.barrier()

    if dbg is not None and dbg[0] == "B1":
        with ExitStack() as st:
            tmp = sb(st, "dbgt", (128, 8, 512), BF16)
            dv = dbg_out.full_v().rearrange("(c p) t -> p c t", p=128)
            for j in range(8):
                fw.dma(tmp[:, :, :], oaT_v[:, :, j * 512:(j + 1) * 512])
                fw.dma(dv[:, :, j * 512:(j + 1) * 512], tmp[:, :, :])
            fw.barrier()
        return finish()

    def run_skewed(n_items, stages):
        K = len(stages)
        ctx = [dict() for _ in range(n_items)]
        for step in range(n_items + K - 1):
            for k in range(K):
                i = step - k
                if 0 <= i < n_items and stages[k] is not None:
                    stages[k](i, ctx[i])

    qkT_d = [dscr("qkT_d%d" % g, (4, 128, 2, S), BF16) for g in range(3)]
    vaug_d = [dscr("vaug_d%d" % g, (128, 32, 8, 65), BF16) for g in range(3)]

    def phase_c0():
        with ExitStack() as st:
            pqk = Rot([ps(st, "pqk%d" % i, (128, 1024), F32) for i in range(2)])
            pvv = Rot([ps(st, "pvv%d" % i, (128, 512), F32) for i in range(2)])
            ptt = Rot([ps(st, "ptt%d" % i, (128, 512), F32) for i in range(2)])
            invf = cst[:, 7, 0:8]
            csn = [sb(st, "cs%d" % g, (128, 32, 16), F32) for g in range(3)]
            snn = [sb(st, "sn%d" % g, (128, 32, 16), F32) for g in range(3)]
            gqk = sb(st, "gqk", (128, 16, 64), F32)
            for a_ in range(2):
                nm = "dil_q_norm" if a_ == 0 else "dil_k_norm"
                a = SM[nm].t
                fw.dma(gqk[:, a_ * 8:(a_ + 1) * 8, :], V(SM[nm], bass.AP(a.tensor, a.offset, [[0, 128], [0, 8], [1, 64]])))
            if True:
                prow_b = sb(st, "prow", (1, S), F32)
                prow = prow_b[:, :]
                ang = sb(st, "ang", (128, 256), F32)
                tA = sb(st, "tA", (128, 256), F32)
                tB = sb(st, "tB", (128, 256), F32)
                tI = sb(st, "tI", (128, 256), I32)
                TWO_PI = 2.0 * PI
                fw.dma(prow, pos.full_v().rearrange("(o n) -> o n", o=1), q="pool")
                aps = [ptt.items[1]]
                for g, dil in enumerate(DILS):
                    nb_ = 32 // dil
                    ap_ = aps[0]
                    for n in range(32):
                        r, bb = n // nb_, n % nb_
                        t0 = dil * 128 * bb + r
                        fw.mm(ap_[:, n * 8:(n + 1) * 8], prow[0:1, t0:t0 + 127 * dil + 1:dil], cst[0:1, 7, 0:8])
                    fw.copy("dve", ang[:, :], ap_[:, 0:256])
                    for which in range(2):
                        if which == 1:
                            fw.ts("dve", ang[:, :], ang[:, :], PI / 2, ALU.add)
                        fw.ts("dve", tA[:, :], ang[:, :], 1.0 / TWO_PI, ALU.mult)
                        fw.copy("dve", tI[:, :], tA[:, :])
                        fw.copy("dve", tA[:, :], tI[:, :])
                        fw.stt("dve", tB[:, :], tA[:, :], -TWO_PI, ang[:, :], ALU.mult, ALU.add)
                        fw.ts("dve", tA[:, :], tB[:, :], PI, ALU.is_gt)
                        fw.stt("dve", tB[:, :], tA[:, :], -TWO_PI, tB[:, :], ALU.mult, ALU.add)
                        fw.ts("dve", tA[:, :], tB[:, :], -PI, ALU.is_lt)
                        fw.stt("dve", tB[:, :], tA[:, :], TWO_PI, tB[:, :], ALU.mult, ALU.add)
                        fw.ts("dve", tB[:, :], tB[:, :], 3.141592, ALU.min, -3.141592, ALU.max)
                        tb3 = tB[:, :].rearrange("p (n e) -> p n e", e=8)
                        if which == 0:
                            fw.act(snn[g][:, :, 8:16], tb3, AF.Sin)
                            fw.ts("dve", snn[g][:, :, 0:8], snn[g][:, :, 8:16], -1.0, ALU.mult)
                        else:
                            fw.act(csn[g][:, :, 0:8], tb3, AF.Sin)
                            fw.copy("dve", csn[g][:, :, 8:16], csn[g][:, :, 0:8])

            wdr = Rot([sb(st, "wd%d" % i, (128, 8, 3, 512), BF16) for i in range(1)])
            sqr = Rot([sb(st, "sq%d" % i, (128, 1024), F32) for i in range(2)])
            s16r = Rot([sb(st, "s16_%d" % i, (128, 48), F32) for i in range(3)])
            qk1r = Rot([sb(st, "qk1_%d" % i, (128, 16, 64), F32) for i in range(3)])
            qk2r = Rot([sb(st, "qk2_%d" % i, (128, 16, 64), F32) for i in range(2)])
            Ar = Rot([sb(st, "ropeA%d" % i, (128, 16, 16), F32) for i in range(2)])
            Br = Rot([sb(st, "ropeB%d" % i, (128, 16, 16), F32) for i in range(2)])
            qkbr = Rot([sb(st, "qkb%d" % i, (128, 16, 64), BF16) for i in range(3)])
            oT4r = Rot([sb(st, "oT4_%d" % i, (128, 8, 512), BF16) for i in range(2)])
            vt4r = Rot([sb(st, "vt4_%d" % i, (128, 4, 8, 65), BF16) for i in range(2)])
            for vt in vt4r.items:
                fw.memset("pool", vt[:, :, :, :], 1.0)
            wds = {}
            c0items = [(g, n) for g in range(3) for n in range(32)]
            hold = {}

            def c0s1(it, c):
                g, n = c0items[it]
                dil = DILS[g]
                nb_ = 32 // dil
                if n == 0:
                    wd = wdr.next()
                    for j, off in enumerate((OFF_QB, OFF_KB, OFF_VB)):
                        fw.dma(wd[:, :, j, :], win_dil[:, :, off + g * 512:off + (g + 1) * 512])
                    wds[g] = wd
                wd = wds[g]
                r, bb = n // nb_, n % nb_
                t0 = dil * 128 * bb + r

                def lhsT(kc):
                    return V(xn_ro, xn_t.t[:, kc, t0:t0 + 127 * dil + 1:dil])
                pq_ = pqk.next()
                pv_ = pvv.next()
                for j in range(2):
                    for kc in range(8):
                        fw.mm(pq_[:, j * 512:(j + 1) * 512], lhsT(kc), wd[:, kc, j, :], start=(kc == 0), stop=(kc == 7))
                for kc in range(8):
                    fw.mm(pv_[:, :], lhsT(kc), wd[:, kc, 2, :], start=(kc == 0), stop=(kc == 7))
                sq = sqr.next()
                fw.act(sq[:, :], pq_[:, :], AF.Square)
                s16 = s16r.next()
                fw.reduce("dve", s16[:, 0:16], sq[:, :].rearrange("p (a d) -> p a d", a=16), ALU.add)
                fw.act(s16[:, 16:32], s16[:, 0:16], AF.Ln, scale=1.0 / 64, bias=eps_c)
                fw.act(s16[:, 32:48], s16[:, 16:32], AF.Exp, scale=-0.5)
                qk1 = qk1r.next()
                fw.tt("dve", qk1[:, :, :], pq_[:, :].rearrange("p (a d) -> p a d", a=16),
                      s16[:, 32:48].raw(0, [[1, 16], [0, 64]]), ALU.mult)
                c["qk1"] = qk1
                m4 = n % 4
                if m4 == 0:
                    hold["vt4"] = vt4r.next()
                vt4 = hold["vt4"]
                fw.copy("dve", vt4[:, m4, :, 0:64], pv_[:, :].rearrange("p (a d) -> p a d", a=8))
                if m4 == 3:
                    fw.dma(vaug_d[g].full_v()[:, n - 3:n + 1, :, :], vt4[:, :, :, :], join=True)

            def c0s2(it, c):
                g, n = c0items[it]
                qk1 = c["qk1"]
                qk2 = qk2r.next()
                fw.tt("pool", qk2[:, :, :], qk1[:, :, :], gqk[:, :, :], ALU.mult)
                A = Ar.next()
                B = Br.next()
                fw.tt("pool", A[:, :, :], qk2[:, :, 0:16], csn[g][:, n, :].raw(0, [[0, 16], [1, 16]]), ALU.mult)
                fw.tt("pool", B[:, :, 0:8], qk2[:, :, 8:16], snn[g][:, n, 0:8].raw(0, [[0, 16], [1, 8]]), ALU.mult)
                fw.tt("pool", B[:, :, 8:16], qk2[:, :, 0:8], snn[g][:, n, 8:16].raw(0, [[0, 16], [1, 8]]), ALU.mult)
                qkb = qkbr.next()
                fw.copy("act", qkb[:, :, 16:64], qk2[:, :, 16:64])
                fw.tt("dve", qkb[:, :, 0:16], A[:, :, :], B[:, :, :], ALU.add)
                c["qkb"] = qkb

            def c0s3(it, c):
                g, n = c0items[it]
                qkb = c["qkb"]
                pt_ = ptt.next()
                ptb = pt_[:, :].bitcast(BF16)
                qkb2 = qkb[:, :, :].rearrange("p a d -> p (a d)")
                for cc in range(8):
                    fw.tr(ptb[:, cc * 128:(cc + 1) * 128], qkb2[:, cc * 128:(cc + 1) * 128], ident_b)
                m4 = n % 4
                if m4 == 0:
                    hold["oT4"] = oT4r.next()
                oT4 = hold["oT4"]
                fw.copy("act", oT4[:, :, m4 * 128:(m4 + 1) * 128], ptb.rearrange("p (c t) -> p c t", c=8))
                if m4 == 3:
                    n0 = n - 3
                    dst = qkT_d[g].full_v().rearrange("h p a t -> p a h t")[:, :, :, n0 * 128:(n0 + 4) * 128]
                    fw.dma(dst, oT4[:, :, :].rearrange("p (a h) t -> p a h t", a=2), join=True)

            run_skewed(len(c0items), [c0s1, c0s2, c0s3])
            fw.barrier()

    def phase_c1():
        with ExitStack() as st:
            mask4 = sb(st, "mask4", (128, 4, 128), BF16)
            for a_ in range(4):
                fw.copy("dve", mask4[:, a_, :], cstb[:, 1 if a_ % 2 == 0 else 3, :])
            mask4f = mask4[:, :, :].rearrange("p a i -> p (a i)")
            obT_v = obT_d.full_v()
            accs = [[sb(st, "acc%d_%d" % (k_, hh), (65, S), F32) for hh in range(2)] for k_ in range(2)]
            norm_jobs = []
            qkTr = Rot([sb(st, "qkT%d" % i, (128, 2, S), BF16) for i in range(2)])
            vaugr = Rot([sb(st, "vaug%d" % i, (128, 32, 2, 65), BF16) for i in range(2)])
            pS = Rot([ps(st, "pS%d" % i, (128, 512), F32) for i in range(3)])
            pO = Rot([ps(st, "pO%d" % i, (128, 512), F32) for i in range(3)])
            pB = Rot([ps(st, "pB%d" % i, (128, 512), F32) for i in range(2)])
            ptr_ = Rot([sb(st, "pTs%d" % i, (128, 512), BF16) for i in range(9)])
            obr = Rot([sb(st, "ob%d" % i, (64, 512), BF16) for i in range(2)])
            rbr = Rot([sb(st, "rb%d" % i, (64, 512), F32) for i in range(2)])

            def loads(hp, g):
                qkT = qkTr.next()
                va = vaugr.next()
                fw.dma(qkT[:, :, :], V(qkT_d[g], qkT_d[g].t[hp]))
                fw.dma(va[:, :, :, :], vaug_d[g].full_v()[:, :, hp * 2:hp * 2 + 2, :])
                return qkT, va

            seq = [(hp, g) for hp in range(4) for g in range(3)]
            bufs = {0: loads(*seq[0])}
            items = [(idx, hh, n0) for idx in range(len(seq)) for hh in range(2) for n0 in range(0, 32, 2)]

            def s1(it, c):
                idx, hh, n0 = items[it]
                hp, g = seq[idx]
                nb_ = 32 // DILS[g]
                qkT, vaug = bufs[idx]
                hs = slice(hh * 64, (hh + 1) * 64)
                b0 = n0 % nb_
                pS_ = pS.next()
                for q_ in range(2):
                    n = n0 + q_
                    qv = qkT[hs, 0, n * 128:(n + 1) * 128]
                    fw.mm(pS_[:, (2 * q_) * 128:(2 * q_ + 1) * 128], qkT[hs, 1, n * 128:(n + 1) * 128], qv)
                    if b0 + q_ > 0:
                        fw.mm(pS_[:, (2 * q_ + 1) * 128:(2 * q_ + 2) * 128], qkT[hs, 1, (n - 1) * 128:n * 128], qv)
                pT_ = ptr_.next()
                meng = "pool" if (it % 3) != 2 else "dve"
                if b0 == 0:
                    fw.act(pT_[:, 0:128], pS_[:, 0:128], AF.Exp, scale=0.125)
                    fw.act(pT_[:, 256:512], pS_[:, 256:512], AF.Exp, scale=0.125)
                    fw.tt(meng, pT_[:, 0:128], pT_[:, 0:128], mask4f[:, 0:128], ALU.mult)
                    fw.tt(meng, pT_[:, 256:512], pT_[:, 256:512], mask4f[:, 256:512], ALU.mult)
                else:
                    fw.act(pT_[:, :], pS_[:, :], AF.Exp, scale=0.125)
                    fw.tt(meng, pT_[:, :], pT_[:, :], mask4f, ALU.mult)
                c["pT"] = pT_

            def do_norm(hp_, h2_, blk):
                acc_ = accs[hp_ % 2]
                pB_ = pB.next()
                fw.mm(pB_[0:64, :], cst[64:65, 6, 0:64], acc_[h2_][64:65, blk * 512:(blk + 1) * 512])
                ob = obr.next()
                rb = rbr.next()
                fw.recip(rb[:, :], pB_[0:64, :])
                fw.tt("dve", ob[:, :], acc_[h2_][0:64, blk * 512:(blk + 1) * 512], rb[:, :], ALU.mult)
                hg = hp_ * 2 + h2_
                fw.dma(obT_v[hg * 64:(hg + 1) * 64, blk * 512:(blk + 1) * 512], ob[:, :], join=True)

            def s2(it, c):
                idx, hh, n0 = items[it]
                hp, g = seq[idx]
                dil = DILS[g]
                nb_ = 32 // dil
                qkT, vaug = bufs[idx]
                r, b0 = n0 // nb_, n0 % nb_
                pT_ = c["pT"]
                if hh == 0 and n0 == 0 and idx + 1 < len(seq):
                    bufs[idx + 1] = loads(*seq[idx + 1])
                pO_ = pO.next()
                for q_ in range(2):
                    n = n0 + q_
                    bb = b0 + q_
                    fw.mm(pO_[0:65, q_ * 128:(q_ + 1) * 128], vaug[:, n, hh, :],
                          pT_[:, (2 * q_) * 128:(2 * q_ + 1) * 128], start=True, stop=(bb == 0))
                    if bb > 0:
                        fw.mm(pO_[0:65, q_ * 128:(q_ + 1) * 128], vaug[:, n - 1, hh, :],
                              pT_[:, (2 * q_ + 1) * 128:(2 * q_ + 2) * 128], start=False, stop=True)
                a0 = dil * 128 * b0 + r
                acc = accs[hp % 2]
                av = acc[hh][:, :].raw(a0, [[dil, 256]])
                if g == 0:
                    fw.copy("dve", av, pO_[0:65, 0:256])
                else:
                    fw.tt("dve", av, av, pO_[0:65, 0:256], ALU.add)
                if norm_jobs:
                    do_norm(*norm_jobs.pop(0))
                if g == 2 and hh == 1 and n0 == 30:
                    for h2_ in range(2):
                        for blk in range(8):
                            norm_jobs.append((hp, h2_, blk))

            run_skewed(len(items), [s1, None, None, None, None, s2])
            while norm_jobs:
                do_norm(*norm_jobs.pop(0))
            fw.barrier()

    stB1w.close()
    phase_c0()
    phase_c1()
    if dbg is not None and dbg[0] == "C1":
        with ExitStack() as st:
            tmp = sb(st, "dbgt", (128, 4, 512), BF16)
            sv = obT_d.full_v().rearrange("(c p) t -> p c t", p=128)
            dv = dbg_out.full_v().rearrange("(c p) t -> p c t", p=128)
            for j in range(8):
                fw.dma(tmp[:, :, :], sv[:, :, j * 512:(j + 1) * 512])
                fw.dma(dv[:, :, j * 512:(j + 1) * 512], tmp[:, :, :])
            fw.barrier()
        return finish()

    def phase_t1a():
        with ExitStack() as st:
            wbr = sb(st, "wbr", (128, 8, 1024), BF16)
            wbd = sb(st, "wbd", (128, 4, 1024), BF16)
            wmg = sb(st, "wmg", (128, 8, 2048), BF16)
            wmo = sb(st, "wmo", (128, 8, 1024), BF16)
            fw.dma(wbr[:, :, :], WB["w_br_gla"].full_v().rearrange("(c p) n -> p c n", p=128))
            fw.dma(wbd[:, :, :], WB["w_br_dil"].full_v().rearrange("(c p) n -> p c n", p=128))
            fw.dma(wmg[:, :, :], WB["w_merge_gate"].full_v().rearrange("(c p) n -> p c n", p=128))
            fw.dma(wmo[:, :, :], WB["w_mix_out"].full_v().rearrange("(c p) n -> p c n", p=128))
            bmg = sb(st, "bmg", (1, 2048), BF16)
            fw.dma(bmg[:, :], SM["b_merge_gate"].full_v().rearrange("(o n) -> o n", o=1), q="pool")
            P = Rot([ps(st, "qP%d" % i, (128, 512), F32) for i in range(8)])
            nb = norm_bufs(st, "T") + (P,)
            oatr = Rot([sb(st, "oat%d" % i, (128, 8, 128), BF16) for i in range(2)])
            obtr = Rot([sb(st, "obt%d" % i, (128, 4, 128), BF16) for i in range(2)])
            xtr = Rot([sb(st, "xtT%d" % i, (128, 1024), F32) for i in range(4)])
            gar = Rot([sb(st, "ga%d" % i, (128, 512), F32) for i in range(3)])
            mar = Rot([sb(st, "ma%d" % i, (128, 1024), F32) for i in range(2)])
            mrr = Rot([sb(st, "mr%d" % i, (128, 1024), BF16) for i in range(2)])
            mTr = Rot([sb(st, "mT%d" % i, (128, 8, 128), BF16) for i in range(2)])
            h1r = Rot([sb(st, "h1t%d" % i, (128, 1024), F32) for i in range(2)])
            oaT_v = oaT_d.full_v().rearrange("(c p) t -> p c t", p=128)
            obT_v = obT_d.full_v().rearrange("(c p) t -> p c t", p=128)

            def sL(i, c):
                c["oat"] = oatr.next()
                c["obt"] = obtr.next()
                c["xt"] = xtr.next()
                fw.dma(c["oat"][:, :, :], oaT_v[:, :, i * 128:(i + 1) * 128])
                fw.dma(c["obt"][:, :, :], obT_v[:, :, i * 128:(i + 1) * 128])
                fw.dma(c["xt"][:, :], x[i * 128:(i + 1) * 128, :])

            def s1(i, c):
                oat, obt = c["oat"], c["obt"]
                xnv = V(xn_sub[i // 4], xn_t.t[:, :, i * 128:(i + 1) * 128])
                ms = []
                for br in range(2):
                    ma = mar.next()
                    for half in range(2):
                        cs_ = slice(half * 512, (half + 1) * 512)
                        gs_ = slice(br * 1024 + half * 512, br * 1024 + (half + 1) * 512)
                        py = P.next()
                        if br == 0:
                            for kc in range(8):
                                fw.mm(py[:, :], oat[:, kc, :], wbr[:, kc, cs_], start=(kc == 0), stop=(kc == 7))
                        else:
                            for kc in range(4):
                                fw.mm(py[:, :], obt[:, kc, :], wbd[:, kc, cs_], start=(kc == 0), stop=(kc == 3))
                        pg = P.next()
                        for kc in range(8):
                            fw.mm(pg[:, :], xnv[:, kc, :], wmg[:, kc, gs_], start=(kc == 0), stop=False)
                        fw.mm(pg[:, :], ones_row_b, bmg[0:1, gs_], start=False, stop=True)
                        ga = gar.next()
                        fw.act(ga[:, :], pg[:, :], AF.Sigmoid)
                        fw.tt("dve", ma[:, cs_], py[:, :], ga[:, :], ALU.mult)
                    ms.append(ma)
                mr = mrr.next()
                fw.tt("pool", mr[:, :], ms[0][:, :], ms[1][:, :], ALU.add)
                c["mr"] = mr

            def s2(i, c):
                mr, xt = c["mr"], c["xt"]
                pt = P.next()
                ptb = pt[:, :].bitcast(BF16)
                for cc in range(8):
                    fw.tr(ptb[:, cc * 128:(cc + 1) * 128], mr[:, cc * 128:(cc + 1) * 128], ident_b)
                mT = mTr.next()
                fw.copy("act", mT[:, :, :], ptb.rearrange("p (c t) -> p c t", c=8))
                c["mT"] = mT

            def s2b(i, c):
                mT, xt = c["mT"], c["xt"]
                h1t = h1r.next()
                for half in range(2):
                    cs_ = slice(half * 512, (half + 1) * 512)
                    pm = P.next()
                    for kc in range(8):
                        fw.mm(pm[:, :], mT[:, kc, :], wmo[:, kc, cs_], start=(kc == 0), stop=(kc == 7))
                    fw.tt("dve", h1t[:, cs_], pm[:, :], xt[:, cs_], ALU.add)
                fw.dma(h1_d[i * 128:(i + 1) * 128, :], h1t[:, :], join=True)
                c["xs"] = norm_a(nb, h1t[:, :])

            def s3(i, c):
                xnv = V(xn_sub[i // 4], xn_t.t[:, :, i * 128:(i + 1) * 128])
                norm_b(nb, c["xs"], gcol["norm_x"], xnv)

            sL(0, None) if False else None
            ctxs = [dict() for _ in range(NT)]
            sL(0, ctxs[0])
            for step in range(NT + 3):
                if step + 1 < NT:
                    sL(step + 1, ctxs[step + 1])
                if step < NT:
                    s1(step, ctxs[step])
                if 0 <= step - 1 < NT:
                    s2(step - 1, ctxs[step - 1])
                if 0 <= step - 2 < NT:
                    s2b(step - 2, ctxs[step - 2])
                if 0 <= step - 3 < NT:
                    s3(step - 3, ctxs[step - 3])
            fw.barrier()

    def phase_t1b():
        with ExitStack() as st:
            wxq = sb(st, "wxq", (128, 8, 1024), BF16)
            wxo = sb(st, "wxo", (128, 8, 1024), BF16)
            fw.dma(wxq[:, :, :], WB["w_xq"].full_v().rearrange("(c p) n -> p c n", p=128))
            fw.dma(wxo[:, :, :], WB["w_xo"].full_v().rearrange("(c p) n -> p c n", p=128))
            gq = sb(st, "gqx", (128, 256), F32)
            gk = sb(st, "gkx", (128, 256), F32)
            fw.dma(gq[:, :], dram_bc(SM["x_q_norm"], 256))
            fw.dma(gk[:, :], dram_bc(SM["x_k_norm"], 256))
            kmT = sb(st, "kmT", (128, 8, 256), BF16)
            Vm = sb(st, "Vm", (128, 2, 1024), BF16)
            PB = Rot([ps(st, "rP%d" % i, (128, 512), F32) for i in range(8)])
            p1 = PB
            nb = norm_bufs(st, "X") + (PB,)
            ssr = Rot([sb(st, "ssXh%d" % i, (128, 12), F32) for i in range(3)])
            jkr = Rot([sb(st, "jkXh%d" % i, (128, 256), BF16) for i in range(2)])
            qnr = Rot([sb(st, "qn%d" % i, (128, 1024), BF16) for i in range(2)])

            def head_norm(psrc, gain, dst):
                ss = ssr.next()
                jk = jkr.next()
                for h in range(4):
                    fw.act(jk[:, :], psrc[:, h * 256:(h + 1) * 256], AF.Square, accum=ss[:, h:h + 1])
                fw.act(ss[:, 4:8], ss[:, 0:4], AF.Ln, scale=1.0 / 256, bias=eps_c)
                fw.act(ss[:, 8:12], ss[:, 4:8], AF.Exp, scale=-0.5)
                for h in range(4):
                    fw.stt("dve", dst[:, h * 256:(h + 1) * 256], psrc[:, h * 256:(h + 1) * 256], ss[:, 8 + h:9 + h],
                           gain[:, :], ALU.mult, ALU.mult)

            with ExitStack() as st2:
                wkv = sb(st2, "wkv", (128, 8, 2048), BF16)
                fw.dma(wkv[:, :, :], WB["w_xkv"].full_v().rearrange("(c p) n -> p c n", p=128))
                mnT = sb(st2, "mnT", (128, 8, 256), BF16)
                mtr = Rot([sb(st2, "mt%d" % i, (128, 1024), F32) for i in range(2)])
                for m in range(2):
                    mt = mtr.next()
                    fw.dma(mt[:, :], mem[m * 128:(m + 1) * 128, :])
                    norm_T(nb, mt[:, :], gcol["norm_mem"], mnT[:, :, m * 128:(m + 1) * 128])
                for m in range(2):
                    pk = []
                    for half in range(2):
                        cs_ = slice(half * 512, (half + 1) * 512)
                        p_ = PB.next()
                        for kc in range(8):
                            fw.mm(p_[:, :], mnT[:, kc, m * 128:(m + 1) * 128], wkv[:, kc, cs_],
                                  start=(kc == 0), stop=(kc == 7))
                        pk.append(p_)
                        pv_ = PB.next()
                        for kc in range(8):
                            fw.mm(pv_[:, :], mnT[:, kc, m * 128:(m + 1) * 128],
                                  wkv[:, kc, 1024 + half * 512:1024 + (half + 1) * 512], start=(kc == 0), stop=(kc == 7))
                        fw.copy("act", Vm[:, m, cs_], pv_[:, :])
                    srcs = [pk[h // 2][:, (h % 2) * 256:(h % 2 + 1) * 256] for h in range(4)]
                    hnk = HeadNorm(st2, "K%d" % m, 1)
                    ssk = hnk.stats(srcs)
                    kn = qnr.next()
                    for h in range(4):
                        fw.stt("dve", kn[:, h * 256:(h + 1) * 256], srcs[h], ssk[:, 8 + h:9 + h], gk[:, :], ALU.mult, ALU.mult)
                    pt = PB.next()
                    ptb = pt[:, :].bitcast(BF16)
                    for cc in range(8):
                        fw.tr(ptb[:, cc * 128:(cc + 1) * 128], kn[:, cc * 128:(cc + 1) * 128], ident_b)
                    fw.copy("act", kmT[:, :, m * 128:(m + 1) * 128], ptb.rearrange("p (c t) -> p c t", c=8))
                fw.barrier()

            h1r = Rot([sb(st, "h1x%d" % i, (128, 1024), F32) for i in range(7)])
            qTr = Rot([sb(st, "qTx%d" % i, (128, 8, 128), BF16) for i in range(2)])
            ptsr = Rot([sb(st, "ptsx%d" % i, (128, 1024), BF16) for i in range(2)])
            rdr = Rot([sb(st, "rdx%d" % i, (128, 512), F32) for i in range(2)])
            oxr = Rot([sb(st, "oxT%d" % i, (128, 8, 128), BF16) for i in range(2)])
            h2r = Rot([sb(st, "h2x%d" % i, (128, 1024), F32) for i in range(2)])
            qn2r = Rot([sb(st, "qn2_%d" % i, (128, 1024), BF16) for i in range(2)])
            ones_b = cstb[:, 6, :]
            hn = HeadNorm(st, "X")

            def sL(i, c):
                c["h1"] = h1r.next()
                fw.dma(c["h1"][:, :], h1_d[i * 128:(i + 1) * 128, :])

            def s1(i, c):
                xnv = V(xn_ro, xn_t.t[:, :, i * 128:(i + 1) * 128])
                pq = []
                for half in range(2):
                    cs_ = slice(half * 512, (half + 1) * 512)
                    p_ = PB.next()
                    for kc in range(8):
                        fw.mm(p_[:, :], xnv[:, kc, :], wxq[:, kc, cs_], start=(kc == 0), stop=(kc == 7))
                    pq.append(p_)
                srcs = [pq[h // 2][:, (h % 2) * 256:(h % 2 + 1) * 256] for h in range(4)]
                ss = hn.stats(srcs)
                qn = qn2r.next()
                for h in range(4):
                    fw.stt("dve", qn[:, h * 256:(h + 1) * 256], srcs[h], ss[:, 8 + h:9 + h], gq[:, :], ALU.mult, ALU.mult)
                c["qn"] = qn

            def s2(i, c):
                qn = c["qn"]
                pt = PB.next()
                ptb = pt[:, :].bitcast(BF16)
                for cc in range(8):
                    fw.tr(ptb[:, cc * 128:(cc + 1) * 128], qn[:, cc * 128:(cc + 1) * 128], ident_b)
                qT = qTr.next()
                fw.copy("act", qT[:, :, :], ptb.rearrange("p (c t) -> p c t", c=8))
                c["qT"] = qT

            def s2b(i, c):
                qT = c["qT"]
                pts = ptsr.next()
                for hb in range(2):
                    pS2 = PB.next()
                    for hl in range(2):
                        h = hb * 2 + hl
                        for mb in range(2):
                            sl = slice((hl * 2 + mb) * 128, (hl * 2 + mb + 1) * 128)
                            for dc in range(2):
                                fw.mm(pS2[:, sl], kmT[:, h * 2 + dc, mb * 128:(mb + 1) * 128], qT[:, h * 2 + dc, :],
                                      start=(dc == 0), stop=(dc == 1))
                    fw.act(pts[:, hb * 512:(hb + 1) * 512], pS2[:, :], AF.Exp, scale=1.0 / 16)
                c["pts"] = pts

            def s3(i, c):
                pts = c["pts"]
                pD = PB.next()
                for h in range(4):
                    for mb in range(2):
                        fw.mm(pD[:, h * 128:(h + 1) * 128], ones_b, pts[:, (h * 2 + mb) * 128:(h * 2 + mb + 1) * 128],
                              start=(mb == 0), stop=(mb == 1))
                rd = rdr.next()
                fw.recip(rd[:, :], pD[:, :])
                ox = oxr.next()
                for hb in range(2):
                    pO2 = PB.next()
                    for hl in range(2):
                        h = hb * 2 + hl
                        for dc in range(2):
                            sl = slice((hl * 2 + dc) * 128, (hl * 2 + dc + 1) * 128)
                            for mb in range(2):
                                fw.mm(pO2[:, sl], Vm[:, mb, h * 256 + dc * 128:h * 256 + (dc + 1) * 128],
                                      pts[:, (h * 2 + mb) * 128:(h * 2 + mb + 1) * 128], start=(mb == 0), stop=(mb == 1))
                    fw.tt("dve", ox[:, hb * 4:(hb + 1) * 4, :].rearrange("p (h c) t -> p h c t", h=2),
                          pO2[:, :].rearrange("p (h c t) -> p h c t", h=2, c=2),
                          rd[:, hb * 256:(hb + 1) * 256].raw(0, [[128, 2], [0, 2], [1, 128]]), ALU.mult)
                c["ox"] = ox

            def s4(i, c):
                ox, h1t = c["ox"], c["h1"]
                h2t = h2r.next()
                for half in range(2):
                    cs_ = slice(half * 512, (half + 1) * 512)
                    px = PB.next()
                    for kc in range(8):
                        fw.mm(px[:, :], ox[:, kc, :], wxo[:, kc, cs_], start=(kc == 0), stop=(kc == 7))
                    fw.tt("dve", h2t[:, cs_], px[:, :], h1t[:, cs_], ALU.add)
                fw.dma(h2_d[i * 128:(i + 1) * 128, :], h2t[:, :], join=True)

            ctxs = [dict() for _ in range(NT)]
            sL(0, ctxs[0])
            for step in range(NT + 4):
                if step + 1 < NT:
                    sL(step + 1, ctxs[step + 1])
                for k, fn in enumerate((s1, s2, s2b, s3, s4)):
                    i = step - k
                    if 0 <= i < NT:
                        fn(i, ctxs[i])
            fw.barrier()

    def phase_t2():
        with ExitStack() as st:
            wup = sb(st, "wup", (128, 8, 2 * DFF), BF16)
            wdn = sb(st, "wdn", (128, NFC, 1024), BF16)
            wcb = sb(st, "wcb", (128, 4 * NFC), F32)
            crow = sb(st, "crow", (4 * NFC, 128), F32)
            fw.dma(crow[0:3 * NFC, :], SM["w_ffn_conv"].full_v().rearrange("t (c p) -> (t c) p", p=128))
            fw.dma(crow[3 * NFC:4 * NFC, :], SM["b_ffn_conv"].full_v().rearrange("(c p) -> c p", p=128))
            hist = sb(st, "hist", (128, NFC, 2), F32)
            hsub = [hist.sub("hist%d" % fc) for fc in range(NFC)]
            for fc in range(NFC):
                fw.memset("pool", V(hsub[fc], hist.t[:, fc, :]), 0.0)
            p1 = Rot([ps(st, "s1_%d" % i, (128, 512), F32) for i in range(4)])
            p2 = Rot([ps(st, "s2_%d" % i, (128, 1024), F32) for i in range(2)])
            nb = norm_bufs(st, "F", 1, 4) + (p1,)
            cps = p1.next()
            fw.tr(cps[:, 0:4 * NFC], crow[:, :], cst[0:4 * NFC, 0, 0:4 * NFC])
            fw.copy("dve", wcb[:, :], cps[:, 0:4 * NFC])
            h2r = Rot([sb(st, "h2f%d" % i, (128, 1024), F32) for i in range(2)])
            x3r = Rot([sb(st, "x3T%d" % i, (128, 8, 512), BF16) for i in range(1)])
            abr = Rot([sb(st, "abt%d" % i, (128, 514), F32) for i in range(2)])
            c1r = Rot([sb(st, "cv1_%d" % i, (128, 512), F32) for i in range(2)])
            c2r = Rot([sb(st, "cv2_%d" % i, (128, 512), F32) for i in range(2)])
            glr = Rot([sb(st, "gl%d" % i, (128, 512), F32) for i in range(2)])
            yT = sb(st, "yT", (128, NFC, 512), BF16)
            otr = Rot([sb(st, "ot%d" % i, (128, 1024), F32) for i in range(1)])

            def stage_na(s_):
                xsl = []
                for t_ in range(4):
                    i = s_ * 4 + t_
                    h2t = h2r.next()
                    fw.dma(h2t[:, :], h2_d[i * 128:(i + 1) * 128, :])
                    xsl.append(norm_a(nb, h2t[:, :]))
                return xsl

            def stage_nb(xsl):
                x3 = x3r.next()
                for t_ in range(4):
                    norm_b(nb, xsl[t_], gcol["norm_ffn"], x3[:, :, t_ * 128:(t_ + 1) * 128])
                return x3

            xsl0 = stage_na(0)
            wupv = WB["w_ffn_up"].full_v().rearrange("(c p) n -> p c n", p=128)
            for cb in range(4):
                for base in (0, DFF):
                    c0_ = base + cb * 704
                    fw.dma(wup[:, :, c0_:c0_ + 704], wupv[:, :, c0_:c0_ + 704])
            wdnv = WB["w_ffn_down"].full_v().rearrange("(c p) n -> p c n", p=128)
            for c0 in range(0, NFC, 6):
                c1_ = min(NFC, c0 + 6)
                fw.dma(wdn[:, c0:c1_, :], wdnv[:, c0:c1_, :])
            x3n = stage_nb(xsl0)
            for s_ in range(NS):
                x3 = x3n
                xsl_next = [] if s_ + 1 < NS else None
                pend = {}
                for fc in range(NFC):
                    if xsl_next is not None and fc % 5 == 0 and fc // 5 < 4:
                        t_ = fc // 5
                        i_ = (s_ + 1) * 4 + t_
                        h2n = h2r.next()
                        fw.dma(h2n[:, :], h2_d[i_ * 128:(i_ + 1) * 128, :])
                        pend[t_] = h2n
                    if xsl_next is not None and fc % 5 == 3 and fc // 5 < 4:
                        xsl_next.append(norm_a(nb, pend[fc // 5][:, :]))
                    pa = p1.next()
                    for kc in range(8):
                        fw.mm(pa[:, :], wup[:, kc, fc * 128:(fc + 1) * 128], x3[:, kc, :], start=(kc == 0), stop=(kc == 7))
                    pu = p1.next()
                    for kc in range(8):
                        fw.mm(pu[:, :], wup[:, kc, DFF + fc * 128:DFF + (fc + 1) * 128], x3[:, kc, :],
                              start=(kc == 0), stop=(kc == 7))
                    ab = abr.next()
                    hv = V(hsub[fc], hist.t[:, fc, :])
                    fw.copy("pool", ab[:, 0:2], hv)
                    fw.copy("act", ab[:, 2:514], pa[:, :])
                    c1 = c1r.next()
                    c2 = c2r.next()
                    fw.ts("dve", c1[:, :], ab[:, 2:514], wcb[:, 2 * NFC + fc:2 * NFC + fc + 1], ALU.mult, wcb[:, 3 * NFC + fc:3 * NFC + fc + 1], ALU.add)
                    fw.stt("dve", c2[:, :], ab[:, 1:513], wcb[:, NFC + fc:NFC + fc + 1], c1[:, :], ALU.mult, ALU.add)
                    fw.stt("dve", c1[:, :], ab[:, 0:512], wcb[:, fc:fc + 1], c2[:, :], ALU.mult, ALU.add)
                    fw.copy("pool", hv, ab[:, 512:514])
                    gl = glr.next()
                    fw.act(gl[:, :], c1[:, :], AF.Gelu)
                    fw.tt("dve", yT[:, fc, :], gl[:, :], pu[:, :], ALU.mult)
                if xsl_next is not None:
                    x3n = stage_nb(xsl_next)
                for t_ in range(4):
                    i = s_ * 4 + t_
                    h2t = h2r.next()
                    fw.dma(h2t[:, :], h2_d[i * 128:(i + 1) * 128, :])
                    pdn = p2.next()
                    for half in range(2):
                        cs_ = slice(half * 512, (half + 1) * 512)
                        for fc in range(NFC):
                            fw.mm(pdn[:, cs_], yT[:, fc, t_ * 128:(t_ + 1) * 128], wdn[:, fc, cs_],
                                  start=(fc == 0), stop=(fc == NFC - 1))
                    ot = otr.next()
                    fw.tt("dve", ot[:, :], pdn[:, :], h2t[:, :], ALU.add)
                    fw.dma(out[i * 128:(i + 1) * 128, :], ot[:, :], join=True)
            fw.barrier()

    def dump_f32(src_d):
        with ExitStack() as st:
            tmp = Rot([sb(st, "dbgf%d" % i, (128, 1024), F32) for i in range(2)])
            for i in range(NT):
                t_ = tmp.next()
                fw.dma(t_[:, :], src_d[i * 128:(i + 1) * 128, :])
                fw.dma(dbg_out[i * 128:(i + 1) * 128, :], t_[:, :])
            fw.barrier()

    phase_t1a()
    if dbg is not None and dbg[0] == "T1a":
        dump_f32(h1_d)
        return finish()
    phase_t1b()
    if dbg is not None and dbg[0] == "T1b":
        dump_f32(h2_d)
        return finish()
    xstack.close()
    phase_t2()
    return finish()


def make_in_maps(inputs):
    consts = make_consts()
    maps = []
    for b in range(8):
        m = {"x": np.ascontiguousarray(inputs["x"][b]), "mem": np.ascontiguousarray(inputs["mem"][b]),
             "positions": np.ascontiguousarray(inputs["positions"][b]).astype(np.int32), "consts": consts}
        for k in BIG_W:
            m[k] = np.ascontiguousarray(np.asarray(inputs[k])[0])
        for k in SMALL:
            m[k] = np.ascontiguousarray(np.asarray(inputs[k])[0])
        maps.append(m)
    return maps


def kernel(**inputs):
    inputs = {k: np.asarray(v) for k, v in inputs.items()}
    nc, fw = build()
    res = run_bass_kernel_spmd(nc, make_in_maps(inputs), core_ids=list(range(8)))
    return np.stack([np.asarray(r["out"]) for r in res.results], axis=0).astype(np.float32)
```

```python
import numpy as np
import concourse.bass as bass
import concourse.mybir as mybir

F32 = mybir.dt.float32
BF16 = mybir.dt.bfloat16
I32 = mybir.dt.int32
AF = mybir.ActivationFunctionType
ALU = mybir.AluOpType
AX = mybir.AxisListType

SAME_ENGINE_SYNC = True
N_DMA_SEMS = 24


class Buf:
    __slots__ = ("t", "w", "r", "name")

    def __init__(self, t, name=""):
        self.t = t
        self.w = []
        self.r = {}
        self.name = name

    def sub(self, name=""):
        return Buf(self.t, name)

    def __getitem__(self, idx):
        return V(self, self.t[idx])

    def full_v(self):
        return V(self, self.t)


class V:
    __slots__ = ("b", "ap")

    def __init__(self, b, ap):
        self.b = b
        self.ap = ap

    def __getitem__(self, idx):
        return V(self.b, self.ap[idx])

    def rearrange(self, s, **kw):
        return V(self.b, self.ap.rearrange(s, **kw))

    def bcast(self, shape):
        return V(self.b, self.ap.to_broadcast(shape))

    def bitcast(self, dt):
        return V(self.b, self.ap.bitcast(dt))

    def raw(self, extra_off, dims):
        a = self.ap
        return V(self.b, bass.AP(a.tensor, a.offset + extra_off, [list(a.ap[0])] + [list(d) for d in dims]))

    @property
    def shape(self):
        return self.ap.shape


class Instr:
    __slots__ = ("stream", "fn", "deps", "signal", "semval", "dma", "dsem", "dval", "idx")

    def __init__(self, stream, fn, dma=False):
        self.stream = stream
        self.fn = fn
        self.deps = []
        self.signal = False
        self.semval = None
        self.dma = dma
        self.dsem = None
        self.dval = None


class FW:
    STREAMS = ("pe", "act", "dve", "pool", "sp")

    def __init__(self, nc):
        self.nc = nc
        self.prog = {s: [] for s in self.STREAMS}
        self.dma_q = {"sp": (0, N_DMA_SEMS), "pool": (N_DMA_SEMS, 8), "act": (N_DMA_SEMS + 8, 8)}
        self.n_dsem = N_DMA_SEMS + 16
        self.dma_rr = {q: 0 for q in self.dma_q}
        self.dma_last = [None] * self.n_dsem
        self.dma_cnt = [0] * self.n_dsem
        self.all_dmas = []
        self.order = []

    def _add(self, stream, fn, reads, writes, dma=False, join=False, xr=(), xw=()):
        ins = Instr(stream, fn, dma)
        deps = []
        reads = list(reads) + list(xr)
        writes = list(writes) + list(xw)
        wb = set(id(v.b) for v in writes)
        for v in reads:
            b = v.b
            if id(b) in wb:
                continue
            for w in b.w:
                deps.append((w, 0))
        for v in writes:
            b = v.b
            if not join:
                for w in b.w:
                    deps.append((w, 0 if any(v2.b is b for v2 in reads) else 1))
            for r in b.r.values():
                deps.append((r, 1))
        if dma:
            base, cnt = self.dma_q[stream]
            k = base + self.dma_rr[stream]
            self.dma_rr[stream] = (self.dma_rr[stream] + 1) % cnt
            if self.dma_last[k] is not None:
                deps.append((self.dma_last[k], 0))
            self.dma_last[k] = ins
            self.dma_cnt[k] += 16
            ins.dsem = k
            ins.dval = self.dma_cnt[k]
            self.all_dmas.append(ins)
        seen = set()
        for d, kind in deps:
            if d is ins or id(d) in seen:
                continue
            if (not d.dma) and (not dma) and d.stream == stream:
                if stream == "pe" or not SAME_ENGINE_SYNC:
                    continue
            seen.add(id(d))
            ins.deps.append(d)
            d.signal = True
        rkey = ("dma", ins.dsem) if dma else stream
        for v in reads:
            if id(v.b) not in wb:
                v.b.r[rkey] = ins
        for v in writes:
            if join:
                v.b.w = v.b.w + [ins]
            else:
                v.b.w = [ins]
                v.b.r = {}
        self.prog[stream].append(ins)
        return ins

    def barrier(self):
        lasts = []
        for s in self.STREAMS:
            for ins in reversed(self.prog[s]):
                if not ins.dma and ins.fn is not None:
                    lasts.append(ins)
                    break
        lasts = lasts + [d for d in self.dma_last if d is not None]
        for s in self.STREAMS:
            ins = Instr(s, None)
            for d in lasts:
                if (not d.dma) and d.stream == s:
                    continue
                ins.deps.append(d)
                d.signal = True
            self.prog[s].append(ins)

    def wait_for(self, stream, instrs):
        ins = Instr(stream, None)
        for d in instrs:
            ins.deps.append(d)
            d.signal = True
        self.prog[stream].append(ins)

    def dma(self, out, in_, q="sp", join=False, **kw):
        o, i = out.ap, in_.ap
        return self._add(q, lambda e: e.dma_start(out=o, in_=i, **kw), [in_], [out], dma=True, join=join)

    def mm(self, out, lhsT, rhs, start=True, stop=True, xr=()):
        o, l, r = out.ap, lhsT.ap, rhs.ap
        return self._add("pe", lambda e: e.matmul(o, l, r, start=start, stop=stop), [lhsT, rhs], [out], xr=xr)

    def tr(self, out, in_, ident):
        o, i, d = out.ap, in_.ap, ident.ap
        return self._add("pe", lambda e: e.transpose(o, i, d), [in_, ident], [out])

    def act(self, out, in_, func, bias=None, scale=None, accum=None, xr=(), xw=()):
        o, i = out.ap, in_.ap
        kw = {}
        reads = [in_]
        writes = [out]
        if bias is not None:
            if isinstance(bias, V):
                kw["bias"] = bias.ap
                reads.append(bias)
            else:
                kw["bias"] = bias
        if scale is not None:
            if isinstance(scale, V):
                kw["scale"] = scale.ap
                reads.append(scale)
            else:
                kw["scale"] = scale
        if accum is not None:
            kw["accum_out"] = accum.ap
            writes.append(accum)
        return self._add("act", lambda e: e.activation(o, i, func, **kw), reads, writes, xr=xr, xw=xw)

    def _veng(self, eng):
        return eng

    def tt(self, eng, out, in0, in1, op):
        o, a, b = out.ap, in0.ap, in1.ap
        return self._add(eng, lambda e: e.tensor_tensor(o, a, b, op), [in0, in1], [out])

    def ts(self, eng, out, in0, s1, op0, s2=None, op1=None, accum=None):
        o, a = out.ap, in0.ap
        reads = [in0]
        writes = [out]
        if isinstance(s1, V):
            reads.append(s1)
            s1 = s1.ap
        if isinstance(s2, V):
            reads.append(s2)
            s2 = s2.ap
        kw = {}
        if op1 is not None:
            kw["op1"] = op1
        if accum is not None:
            kw["accum_out"] = accum.ap
            writes.append(accum)
        return self._add(eng, lambda e: e.tensor_scalar(o, a, s1, s2, op0, **kw), reads, writes)

    def stt(self, eng, out, in0, scalar, in1, op0, op1):
        o, a, b = out.ap, in0.ap, in1.ap
        reads = [in0, in1]
        if isinstance(scalar, V):
            reads.append(scalar)
            scalar = scalar.ap
        return self._add(eng, lambda e: e.scalar_tensor_tensor(o, a, scalar, b, op0, op1), reads, [out])

    def copy(self, eng, out, in_):
        o, i = out.ap, in_.ap
        if eng == "act":
            return self._add("act", lambda e: e.copy(o, i), [in_], [out])
        return self._add(eng, lambda e: e.tensor_copy(o, i), [in_], [out])

    def reduce(self, eng, out, in_, op, axis=AX.X):
        o, i = out.ap, in_.ap
        return self._add(eng, lambda e: e.tensor_reduce(o, i, axis, op), [in_], [out])

    def recip(self, out, in_):
        o, i = out.ap, in_.ap
        return self._add("dve", lambda e: e.reciprocal(o, i), [in_], [out])

    def memset(self, eng, out, val):
        o = out.ap
        return self._add(eng, lambda e: e.memset(o, val), [], [out])

    def emit(self):
        nc = self.nc
        from contextlib import ExitStack
        with ExitStack() as es:
            esem = {s: es.enter_context(nc.semaphore("s_" + s)) for s in self.STREAMS if s != "sp" or True}
            dsem = [es.enter_context(nc.semaphore("d%d" % k)) for k in range(self.n_dsem)]
            self.barrier()
            for s in self.STREAMS:
                c = 0
                for ins in self.prog[s]:
                    if ins.dma or ins.fn is None:
                        continue
                    if ins.signal:
                        c += 1
                        ins.semval = c
            plan = {}
            nwaits = 0
            for s in self.STREAMS:
                known = {}
                acts = []
                for ins in self.prog[s]:
                    for d in ins.deps:
                        if d.dma:
                            key, val, sem = ("d", d.dsem), d.dval, dsem[d.dsem]
                        else:
                            key, val, sem = ("e", d.stream), d.semval, esem[d.stream]
                        assert val is not None
                        if known.get(key, 0) >= val:
                            continue
                        known[key] = val
                        acts.append(("w", sem, val))
                        nwaits += 1
                    if ins.fn is not None:
                        acts.append(("i", ins))
                plan[s] = acts

            def run(eng, acts, s):
                for a in acts:
                    if a[0] == "w":
                        eng.wait_ge(a[1], a[2])
                    else:
                        ins = a[1]
                        bi = ins.fn(eng)
                        if ins.dma:
                            bi.then_inc(dsem[ins.dsem], 16)
                        elif ins.signal:
                            bi.then_inc(esem[s], 1)

            with nc.Block() as block:
                @block.sync
                def _(e):
                    run(e, plan["sp"], "sp")

                @block.tensor
                def _(e):
                    run(e, plan["pe"], "pe")

                @block.scalar
                def _(e):
                    run(e, plan["act"], "act")

                @block.vector
                def _(e):
                    run(e, plan["dve"], "dve")

                @block.gpsimd
                def _(e):
                    run(e, plan["pool"], "pool")
            self.stats = {s: len(self.prog[s]) for s in self.STREAMS}
            self.stats["waits"] = nwaits
from concourse.bass_utils import run_bass_kernel_spmd

import math
import ml_dtypes
from contextlib import ExitStack

S = 4096
D = 1024
NT = S // 128
NS = S // 512
NMEM = 256
DFF = 2816
NFC = DFF // 128
EPS = 1e-6
IN_COLS = 7696
OFF_QA, OFF_KA, OFF_VA, OFF_RA, OFF_ZA, OFF_QB, OFF_KB, OFF_VB = 0, 512, 1024, 2048, 3072, 3088, 4624, 6160
DILS = (1, 4, 16)
PI = math.pi

BIG_W = {
    "w_in": (1024, IN_COLS), "w_br_gla": (1024, 1024), "w_br_dil": (512, 1024),
    "w_merge_gate": (1024, 2048), "w_mix_out": (1024, 1024), "w_xq": (1024, 1024),
    "w_xkv": (1024, 2048), "w_xo": (1024, 1024), "w_ffn_up": (1024, 2 * DFF),
    "w_ffn_down": (DFF, 1024),
}
SMALL = {
    "norm_mix": (1024,), "w_gla_gate": (16, 512), "b_gla_gate": (512,), "gla_out_norm": (256,),
    "dil_q_norm": (64,), "dil_k_norm": (64,), "b_merge_gate": (2048,), "norm_x": (1024,),
    "norm_mem": (1024,), "x_q_norm": (256,), "x_k_norm": (256,), "norm_ffn": (1024,),
    "w_ffn_conv": (3, DFF), "b_ffn_conv": (DFF,),
}


class Rot:
    def __init__(self, items):
        self.items = items
        self.i = 0

    def next(self):
        it = self.items[self.i % len(self.items)]
        self.i += 1
        return it


def make_consts():
    c = np.zeros((8, 128, 128), np.float32)
    i = np.arange(128)
    c[0] = np.eye(128)
    c[1] = (i[:, None] <= i[None, :])
    c[2] = (i[:, None] > i[None, :])
    c[3] = (i[:, None] >= i[None, :])
    c[6] = 1.0
    inv_freq = (500000.0 ** (-np.arange(0, 16, 2, dtype=np.float32) / 16)).astype(np.float32)
    c[7, :, 0:8] = inv_freq[None, :]
    return c


def build(dbg=None, stop_after=None):
    nc = bass.Bass("TRN2", target_bir_lowering=False)
    fw = FW(nc)
    es = ExitStack()

    def din(name, shape, dt=F32):
        return Buf(nc.dram_tensor(name, list(shape), dt, kind="ExternalInput").ap(), name)

    def dscr(name, shape, dt):
        return Buf(nc.dram_tensor(name, list(shape), dt, kind="Internal").ap(), name)

    def dout(name, shape, dt=F32):
        return Buf(nc.dram_tensor(name, list(shape), dt, kind="ExternalOutput").ap(), name)

    x = din("x", (S, D))
    mem = din("mem", (NMEM, D))
    pos = din("positions", (S,), I32)
    consts = din("consts", (8, 128, 128))
    W32 = {k: din(k, v) for k, v in BIG_W.items()}
    SM = {k: din(k, v) for k, v in SMALL.items()}
    out = dout("out", (S, D))
    dbg_out = None
    if dbg is not None:
        dbg_out = dout("dbg", dbg[1], dbg[2])

    WB = {k: dscr(k + "_bf", v, BF16) for k, v in BIG_W.items()}
    oaT_d = dscr("oaT_d", (1024, S), BF16)
    obT_d = dscr("obT_d", (512, S), BF16)
    h1_d = dscr("h1_d", (S, D), F32)
    h2_d = dscr("h2_d", (S, D), F32)

    def sb(stack, name, shape, dt):
        return Buf(stack.enter_context(nc.sbuf_tensor(name, list(shape), dt))[:], name)

    def ps(stack, name, shape, dt):
        return Buf(stack.enter_context(nc.psum_tensor(name, list(shape), dt))[:], name)

    def dram_bc(buf, n, parts=128, off=0):
        a = buf.t
        return V(buf, bass.AP(a.tensor, a.offset + off, [[0, parts], [1, n]]))

    xstack = ExitStack()

    def finish():
        fw.emit()
        stB1w.close()
        xstack.close()
        es.close()
        return nc, fw

    win_gla_b = WB["w_in"].sub("w_in_gla")
    win_dil_b = WB["w_in"].sub("w_in_dil")
    top = es
    cst = sb(top, "cst", (128, 8, 128), F32)
    cstb = sb(top, "cstb", (128, 8, 128), BF16)
    epsb = sb(top, "epsb", (128, 2), F32)
    gall = sb(top, "gall", (128, 32), F32)
    gcol = {k: V(gall, gall.t[:, 8 * j:8 * j + 8]) for j, k in enumerate(("norm_mix", "norm_x", "norm_mem", "norm_ffn"))}

    fw.dma(cst[:, :, :], consts.full_v().rearrange("c p f -> p c f"))
    fw.copy("dve", cstb[:, :, :], cst[:, :, :])
    fw.memset("dve", epsb[:, 0:1], EPS)
    fw.memset("dve", epsb[:, 1:2], 1.0)
    with ExitStack() as st0:
        grow = sb(st0, "grow", (32, 128), F32)
        gps = ps(st0, "gps", (128, 512), F32)
        for j, k in enumerate(("norm_mix", "norm_x", "norm_mem", "norm_ffn")):
            fw.dma(grow[8 * j:8 * j + 8, :], SM[k].full_v().rearrange("(c p) -> c p", p=128))
        fw.tr(gps[:, 0:32], grow[:, :], cst[0:32, 0, 0:32])
        fw.copy("dve", gall[:, :], gps[:, 0:32])
        fw.barrier()
    ident_b = cstb[:, 0, :]
    ones_row_b = cstb[0:1, 6, :]
    eps_c = epsb[:, 0:1]
    one_c = epsb[:, 1:2]

    xn_t = sb(xstack, "xnT", (128, 8, S), BF16)
    xn_sub = [xn_t.sub("xn%d" % j) for j in range(NS)]
    xn_ro = xn_t.sub("xn_ro")

    def xn_w(i):
        return V(xn_sub[i // 4], xn_t.t[:, :, i * 128:(i + 1) * 128])

    def norm_a(nb, xt):
        junk, ssr, xsr, ptr = nb
        jk = junk.next()
        ss = ssr.next()
        fw.act(jk[:, :], xt, AF.Square, accum=ss[:, 0:1])
        fw.act(ss[:, 1:2], ss[:, 0:1], AF.Ln, scale=1.0 / 1024, bias=eps_c)
        fw.act(ss[:, 2:3], ss[:, 1:2], AF.Exp, scale=-0.5)
        xs = xsr.next()
        fw.ts("dve", xs[:, :], xt, ss[:, 2:3], ALU.mult)
        return xs

    def norm_b(nb, xs, g, dst_view):
        junk, ssr, xsr, ptr = nb
        pt = ptr.next()
        ptb = pt[:, :].bitcast(BF16)
        for c in range(8):
            fw.tr(ptb[:, c * 128:(c + 1) * 128], xs[:, c * 128:(c + 1) * 128], ident_b)
        fw.tt("dve", dst_view, ptb.rearrange("p (c t) -> p c t", c=8),
              g[:, :].raw(0, [[1, 8], [0, 128]]), ALU.mult)

    def norm_T(nb, xt, g, dst_view):
        norm_b(nb, norm_a(nb, xt), g, dst_view)

    class HeadNorm:
        def __init__(self, st, tag, n=3):
            self.slots = []
            for i in range(n):
                ss = sb(st, "hn_ss%s%d" % (tag, i), (128, 12), F32)
                jk = sb(st, "hn_jk%s%d" % (tag, i), (128, 4, 256), BF16)
                self.slots.append((ss, [ss.sub() for _ in range(4)], jk, [jk.sub() for _ in range(4)]))
            self.i = 0

        def stats(self, srcs):
            ss, ssub, jk, jsub = self.slots[self.i % len(self.slots)]
            self.i += 1
            for h in range(4):
                fw.act(V(jsub[h], jk.t[:, h, :]), srcs[h], AF.Square, accum=V(ssub[h], ss.t[:, h:h + 1]))
            fw.act(ss[:, 4:8], ss[:, 0:4], AF.Ln, scale=1.0 / 256, bias=eps_c,
                   xr=[V(ssub[h], ss.t[:, h:h + 1]) for h in range(4)])
            fw.act(ss[:, 8:12], ss[:, 4:8], AF.Exp, scale=-0.5)
            return ss

    def norm_bufs(st, tag, nj=2, nxs=2):
        return (Rot([sb(st, "junk%s%d" % (tag, i), (128, 1024), BF16) for i in range(nj)]),
                Rot([sb(st, "ss%s%d" % (tag, i), (128, 4), F32) for i in range(max(4, nxs + 1))]),
                Rot([sb(st, "xs%s%d" % (tag, i), (128, 1024), BF16) for i in range(nxs)]))

    win = V(win_gla_b, WB["w_in"].t.rearrange("(c p) n -> p c n", p=128))
    win_dil = V(win_dil_b, WB["w_in"].t.rearrange("(c p) n -> p c n", p=128))
    stB1w = ExitStack()
    wqk = sb(stB1w, "wqk", (128, 8, 1024), BF16)
    wv = sb(stB1w, "wv", (128, 8, 1024), BF16)
    wr = sb(stB1w, "wr", (128, 8, 1024), BF16)
    wz = sb(stB1w, "wz", (128, 8, 16), BF16)
    wgg = sb(stB1w, "wgg", (16, 512), BF16)
    bgg = sb(stB1w, "bgg", (1, 512), BF16)
    gon = sb(stB1w, "gon", (128, 256), F32)
    fw.dma(gon[:, :], dram_bc(SM["gla_out_norm"], 256))
    fw.dma(wgg[:, :], SM["w_gla_gate"].full_v(), q="pool")
    fw.dma(bgg[:, :], SM["b_gla_gate"].full_v().rearrange("(o n) -> o n", o=1), q="pool")
    b1_loads = []
    with ExitStack() as st:
        xts = Rot([sb(st, "xt%d" % i, (128, 1024), F32) for i in range(4)])
        nb = norm_bufs(st, "A") + (Rot([ps(st, "ptA%d" % i, (128, 512), F32) for i in range(2)]),)
        xloads = []
        for i in range(NT):
            xt = xts.next()
            xloads.append(fw.dma(xt[:, :], x[i * 128:(i + 1) * 128, :]))
            if i == 3:
                fw.wait_for("pool", xloads)
                for (c0, c1, bsub) in ((0, 1544, win_gla_b), (1544, 3088, win_gla_b)):
                    for r0 in (0, 512):
                        fw.dma(V(bsub, WB["w_in"].t[r0:r0 + 512, c0:c1]),
                               V(W32["w_in"], W32["w_in"].t[r0:r0 + 512, c0:c1]), q="pool", join=True)
            if i == NT - 1:
                b1_loads.append(fw.dma(wqk[:, :, :], win[:, :, 0:1024]))
                b1_loads.append(fw.dma(wv[:, :, :], win[:, :, 1024:2048]))
                b1_loads.append(fw.dma(wr[:, :, :], win[:, :, 2048:3072]))
                b1_loads.append(fw.dma(wz[:, :, :], win[:, :, 3072:3088], allow_slow_non_contiguous=True))
            norm_T(nb, xt[:, :], gcol["norm_mix"], xn_w(i))
        fw.barrier()
    fw.wait_for("pool", b1_loads)
    deferred_conv = []
    for (c0, c1, bsub) in ((3088, 4624, win_dil_b), (4624, 6160, win_dil_b), (6160, 7696, win_dil_b)):
        for r0 in (0, 512):
            deferred_conv.append((V(bsub, WB["w_in"].t[r0:r0 + 512, c0:c1]),
                                  V(W32["w_in"], W32["w_in"].t[r0:r0 + 512, c0:c1])))
    for k, shp in BIG_W.items():
        if k == "w_in":
            continue
        n = shp[0] * shp[1]
        rows = n // 2048
        src = W32[k].full_v().rearrange("a b -> (a b)").rearrange("(r c) -> r c", c=2048)
        dst = WB[k].full_v().rearrange("a b -> (a b)").rearrange("(r c) -> r c", c=2048)
        r0 = 0
        while r0 < rows:
            r1 = min(rows, r0 + 512)
            deferred_conv.append((dst[r0:r1, :], src[r0:r1, :]))
            r0 = r1

    def emit_conv(after=None, n=1):
        for _ in range(n):
            if not deferred_conv:
                return
            if after is not None:
                fw.wait_for("pool", [after])
            d_, s_ = deferred_conv.pop(0)
            fw.dma(d_, s_, q="pool", join=True)

    emit_conv(None, 2)

    DKS = 128 ** -0.5
    oaT_v = oaT_d.full_v().rearrange("(c p) t -> p c t", p=128)
    with ExitStack() as st:
        S32 = [sb(st, "S32_%d" % h, (128, 256), F32) for h in range(4)]
        Sbf = [sb(st, "Sbf_%d" % h, (128, 256), BF16) for h in range(4)]
        for h in range(4):
            fw.memset("pool", S32[h][:, :], 0.0)
            fw.memset("pool", Sbf[h][:, :], 0.0)
        PB = Rot([ps(st, "bP%d" % i, (128, 512), F32) for i in range(8)])
        zar = Rot([sb(st, "za%d" % i, (16, 512), BF16) for i in range(2)])
        e1r = Rot([sb(st, "e1_%d" % i, (128, 512), F32) for i in range(1)])
        spb = [sb(st, "sp_%d" % i, (128, 512), F32) for i in range(4)]
        eTpr = Rot([sb(st, "eTp%d" % i, (128, 512), F32) for i in range(2)])
        eTnr = Rot([sb(st, "eTn%d" % i, (128, 512), F32) for i in range(2)])
        ervr = Rot([sb(st, "erv%d" % i, (128, 512), F32) for i in range(2)])
        elr = Rot([sb(st, "elast%d" % i, (128, 16), F32) for i in range(2)])
        qes = [[sb(st, "qe%d_%d" % (i, h), (128, 512), BF16) for h in range(4)] for i in range(2)]
        kes = [[sb(st, "ke%d_%d" % (i, h), (128, 512), BF16) for h in range(4)] for i in range(2)]
        kls = [[sb(st, "kl%d_%d" % (i, c4), (128, 512), BF16) for c4 in range(4)] for i in range(2)]
        vsr = Rot([sb(st, "vs%d" % i, (128, 1024), BF16) for i in range(2)])
        atr = Rot([sb(st, "at%d" % i, (128, 512), BF16) for i in range(3)])
        Gr = Rot([sb(st, "G%d" % i, (128, 1024), F32) for i in range(2)])
        onr = Rot([sb(st, "on%d" % i, (128, 1024), BF16) for i in range(2)])
        oTr = Rot([sb(st, "oT%d" % i, (128, 8, 128), BF16) for i in range(2)])
        hn = HeadNorm(st, "B", 2)
        pst = {}
        cst_ = {}

        def xs_(kc, s_):
            return V(xn_ro, xn_t.t[:, kc, s_ * 512:(s_ + 1) * 512])

        def xc_(kc, c):
            return V(xn_ro, xn_t.t[:, kc, c * 128:(c + 1) * 128])

        def P_a(s_):
            d = pst.setdefault(s_, {})
            d["set"] = s_ % 2
            pza = PB.next()
            for kc in range(8):
                fw.mm(pza[0:16, :], wz[:, kc, :], xs_(kc, s_), start=(kc == 0), stop=(kc == 7))
            za = zar.next()
            fw.copy("act", za[:, :], pza[0:16, :])
            d["za"] = za

        def P_a2(s_):
            d = pst[s_]
            za = d["za"]
            for c4 in range(4):
                pz = PB.next()
                fw.mm(pz[:, :], za[:, c4 * 128:(c4 + 1) * 128], wgg[:, :], start=True, stop=False)
                fw.mm(pz[:, :], ones_row_b, bgg[:, :], start=False, stop=True)
                e1 = e1r.next()
                fw.act(e1[:, :], pz[:, :], AF.Exp, scale=-1.0)
                fw.act(spb[c4][:, :], e1[:, :], AF.Ln, bias=one_c, scale=1.0)
            d["el"] = elr.next()

        def P_b(s_, hs):
            d = pst[s_]
            el = d["el"]
            for h in hs:
                pc = PB.next()
                for c4 in range(4):
                    fw.mm(pc[:, c4 * 128:(c4 + 1) * 128], spb[c4][:, h * 128:(h + 1) * 128], cst[:, 1, :])
                eTp = eTpr.next()
                eTn = eTnr.next()
                fw.act(eTp[:, :], pc[:, :], AF.Exp, scale=-1.0 / 16)
                fw.act(eTn[:, :], pc[:, :], AF.Exp, scale=1.0 / 16)
                fw.copy("pool", el[:, h * 4:(h + 1) * 4], eTp[:, :].raw(127, [[128, 4]]))
                for which in range(2):
                    p_ = PB.next()
                    co = (0 if which == 0 else 512) + h * 128
                    for kc in range(8):
                        fw.mm(p_[:, :], wqk[:, kc, co:co + 128], xs_(kc, s_), start=(kc == 0), stop=(kc == 7))
                    if which == 0:
                        fw.stt("dve", qes[d["set"]][h][:, :], p_[:, :], DKS, eTp[:, :], ALU.mult, ALU.mult)
                    else:
                        fw.tt("dve", kes[d["set"]][h][:, :], p_[:, :], eTn[:, :], ALU.mult)

        def P_c(s_, c4):
            d = pst[s_]
            c = s_ * 4 + c4
            pr = PB.next()
            fw.mm(pr[:, :], cst[:, 2, :], spb[c4][:, :])
            erv = ervr.next()
            fw.act(erv[:, :], pr[:, :], AF.Exp, scale=-1.0 / 16)
            pkt = PB.next()
            for kc in range(8):
                fw.mm(pkt[:, :], xc_(kc, c), wqk[:, kc, 512:1024], start=(kc == 0), stop=(kc == 7))
            fw.tt("dve", kls[d["set"]][c4][:, :], pkt[:, :], erv[:, :], ALU.mult)

        def R1(c):
            s_, c4 = c // 4, c % 4
            d = pst[s_]
            cd = cst_.setdefault(c, {})
            qe, ke = qes[d["set"]], kes[d["set"]]
            tsl = slice(c4 * 128, (c4 + 1) * 128)
            pa = PB.next()
            for h in range(4):
                fw.mm(pa[:, h * 128:(h + 1) * 128], ke[h][:, tsl], qe[h][:, tsl])
            atm = atr.next()
            fw.tt("dve", atm[:, :].rearrange("p (h i) -> p h i", h=4), pa[:, :].rearrange("p (h i) -> p h i", h=4),
                  cst[:, 1, :].raw(0, [[0, 4], [1, 128]]), ALU.mult)
            cd["atm"] = atm
            vs = vsr.next()
            for half in range(2):
                pv = PB.next()
                for kc in range(8):
                    fw.mm(pv[:, :], xc_(kc, c), wv[:, kc, half * 512:(half + 1) * 512], start=(kc == 0), stop=(kc == 7))
                fw.copy("act", vs[:, half * 512:(half + 1) * 512], pv[:, :])
            cd["vs"] = vs
            G = Gr.next()
            for half in range(2):
                prr = PB.next()
                for kc in range(8):
                    fw.mm(prr[:, :], xc_(kc, c), wr[:, kc, half * 512:(half + 1) * 512], start=(kc == 0), stop=(kc == 7))
                fw.act(G[:, half * 512:(half + 1) * 512], prr[:, :], AF.Silu)
            fw.tt("pool", G[:, :].rearrange("p (h v) -> p h v", h=4), G[:, :].rearrange("p (h v) -> p h v", h=4),
                  gon[:, :].raw(0, [[0, 4], [1, 256]]), ALU.mult)
            cd["G"] = G

        def R2(c):
            s_, c4 = c // 4, c % 4
            d = pst[s_]
            cd = cst_[c]
            qe = qes[d["set"]]
            klw = kls[d["set"]][c4]
            vs = cd["vs"]
            el = d["el"]
            atm, G = cd["atm"], cd["G"]
            tsl = slice(c4 * 128, (c4 + 1) * 128)
            pos = []
            for hb in range(2):
                pob = PB.next()
                for hl in range(2):
                    h = hb * 2 + hl
                    fw.mm(pob[:, hl * 256:(hl + 1) * 256], atm[:, h * 128:(h + 1) * 128], vs[:, h * 256:(h + 1) * 256],
                          start=True, stop=False)
                    fw.mm(pob[:, hl * 256:(hl + 1) * 256], qe[h][:, tsl], Sbf[h][:, :], start=False, stop=True)
                pos.append(pob)
            for hb in range(2):
                pdb = PB.next()
                for hl in range(2):
                    h = hb * 2 + hl
                    fw.mm(pdb[:, hl * 256:(hl + 1) * 256], klw[:, h * 128:(h + 1) * 128], vs[:, h * 256:(h + 1) * 256])
                for hl in range(2):
                    h = hb * 2 + hl
                    fw.stt("dve", S32[h][:, :], S32[h][:, :], el[:, h * 4 + c4:h * 4 + c4 + 1],
                           pdb[:, hl * 256:(hl + 1) * 256], ALU.mult, ALU.add)
                    fw.copy("pool", Sbf[h][:, :], S32[h][:, :])
            srcs = [pos[h // 2][:, (h % 2) * 256:(h % 2 + 1) * 256] for h in range(4)]
            ss = hn.stats(srcs)
            on = onr.next()
            for h in range(4):
                fw.stt("dve", on[:, h * 256:(h + 1) * 256], srcs[h], ss[:, 8 + h:9 + h],
                       G[:, h * 256:(h + 1) * 256], ALU.mult, ALU.mult)
            cd["on"] = on

        def R3(c):
            cd = cst_[c]
            on = cd["on"]
            pt = PB.next()
            ptb = pt[:, :].bitcast(BF16)
            for cc in range(8):
                fw.tr(ptb[:, cc * 128:(cc + 1) * 128], on[:, cc * 128:(cc + 1) * 128], ident_b)
            oT = oTr.next()
            fw.copy("act", oT[:, :, :], ptb.rearrange("p (c t) -> p c t", c=8))
            st_ins = fw.dma(oaT_v[:, :, c * 128:(c + 1) * 128], oT[:, :, :], join=True)
            emit_conv(st_ins, 1)

        def P_quarter(s_, qi):
            if qi == 0:
                P_a(s_)
            elif qi == 1:
                P_a2(s_)
            elif qi == 2:
                P_b(s_, (0, 1))
            else:
                P_b(s_, (2, 3))
                for c4 in range(4):
                    P_c(s_, c4)

        for qi in range(4):
            P_quarter(0, qi)
        R1(0)
        for c in range(NT):
            s_, c4 = c // 4, c % 4
            R2(c)
            if c > 0:
                R3(c - 1)
            if s_ + 1 < NS:
                P_quarter(s_ + 1, c4)
            if c + 1 < NT:
                R1(c + 1)
        R3(NT - 1)
        emit_conv(None, 1000)
        fw.barrier()

    if dbg is not None and dbg[0] == "B1":
        with ExitStack() as st:
            tmp = sb(st, "dbgt", (128, 8, 512), BF16)
            dv = dbg_out.full_v().rearrange("(c p) t -> p c t", p=128)
            for j in range(8):
                fw.dma(tmp[:, :, :], oaT_v[:, :, j * 512:(j + 1) * 512])
                fw.dma(dv[:, :, j * 512:(j + 1) * 512], tmp[:, :, :])
            fw.barrier()
        return finish()

    def run_skewed(n_items, stages):
        K = len(stages)
        ctx = [dict() for _ in range(n_items)]
        for step in range(n_items + K - 1):
            for k in range(K):
                i = step - k
                if 0 <= i < n_items and stages[k] is not None:
                    stages[k](i, ctx[i])

    qkT_d = [dscr("qkT_d%d" % g, (4, 128, 2, S), BF16) for g in range(3)]
    vaug_d = [dscr("vaug_d%d" % g, (128, 32, 8, 65), BF16) for g in range(3)]

    def phase_c0():
        with ExitStack() as st:
            pqk = Rot([ps(st, "pqk%d" % i, (128, 1024), F32) for i in range(2)])
            pvv = Rot([ps(st, "pvv%d" % i, (128, 512), F32) for i in range(2)])
            ptt = Rot([ps(st, "ptt%d" % i, (128, 512), F32) for i in range(2)])
            invf = cst[:, 7, 0:8]
            csn = [sb(st, "cs%d" % g, (128, 32, 16), F32) for g in range(3)]
            snn = [sb(st, "sn%d" % g, (128, 32, 16), F32) for g in range(3)]
            gqk = sb(st, "gqk", (128, 16, 64), F32)
            for a_ in range(2):
                nm = "dil_q_norm" if a_ == 0 else "dil_k_norm"
                a = SM[nm].t
                fw.dma(gqk[:, a_ * 8:(a_ + 1) * 8, :], V(SM[nm], bass.AP(a.tensor, a.offset, [[0, 128], [0, 8], [1, 64]])))
            if True:
                prow_b = sb(st, "prow", (1, S), F32)
                prow = prow_b[:, :]
                ang = sb(st, "ang", (128, 256), F32)
                tA = sb(st, "tA", (128, 256), F32)
                tB = sb(st, "tB", (128, 256), F32)
                tI = sb(st, "tI", (128, 256), I32)
                TWO_PI = 2.0 * PI
                fw.dma(prow, pos.full_v().rearrange("(o n) -> o n", o=1), q="pool")
                aps = [ptt.items[1]]
                for g, dil in enumerate(DILS):
                    nb_ = 32 // dil
                    ap_ = aps[0]
                    for n in range(32):
                        r, bb = n // nb_, n % nb_
                        t0 = dil * 128 * bb + r
                        fw.mm(ap_[:, n * 8:(n + 1) * 8], prow[0:1, t0:t0 + 127 * dil + 1:dil], cst[0:1, 7, 0:8])
                    fw.copy("dve", ang[:, :], ap_[:, 0:256])
                    for which in range(2):
                        if which == 1:
                            fw.ts("dve", ang[:, :], ang[:, :], PI / 2, ALU.add)
                        fw.ts("dve", tA[:, :], ang[:, :], 1.0 / TWO_PI, ALU.mult)
                        fw.copy("dve", tI[:, :], tA[:, :])
                        fw.copy("dve", tA[:, :], tI[:, :])
                        fw.stt("dve", tB[:, :], tA[:, :], -TWO_PI, ang[:, :], ALU.mult, ALU.add)
                        fw.ts("dve", tA[:, :], tB[:, :], PI, ALU.is_gt)
                        fw.stt("dve", tB[:, :], tA[:, :], -TWO_PI, tB[:, :], ALU.mult, ALU.add)
                        fw.ts("dve", tA[:, :], tB[:, :], -PI, ALU.is_lt)
                        fw.stt("dve", tB[:, :], tA[:, :], TWO_PI, tB[:, :], ALU.mult, ALU.add)
                        fw.ts("dve", tB[:, :], tB[:, :], 3.141592, ALU.min, -3.141592, ALU.max)
                        tb3 = tB[:, :].rearrange("p (n e) -> p n e", e=8)
                        if which == 0:
                            fw.act(snn[g][:, :, 8:16], tb3, AF.Sin)
                            fw.ts("dve", snn[g][:, :, 0:8], snn[g][:, :, 8:16], -1.0, ALU.mult)
                        else:
                            fw.act(csn[g][:, :, 0:8], tb3, AF.Sin)
                            fw.copy("dve", csn[g][:, :, 8:16], csn[g][:, :, 0:8])

            wdr = Rot([sb(st, "wd%d" % i, (128, 8, 3, 512), BF16) for i in range(1)])
            sqr = Rot([sb(st, "sq%d" % i, (128, 1024), F32) for i in range(2)])
            s16r = Rot([sb(st, "s16_%d" % i, (128, 48), F32) for i in range(3)])
            qk1r = Rot([sb(st, "qk1_%d" % i, (128, 16, 64), F32) for i in range(3)])
            qk2r = Rot([sb(st, "qk2_%d" % i, (128, 16, 64), F32) for i in range(2)])
            Ar = Rot([sb(st, "ropeA%d" % i, (128, 16, 16), F32) for i in range(2)])
            Br = Rot([sb(st, "ropeB%d" % i, (128, 16, 16), F32) for i in range(2)])
            qkbr = Rot([sb(st, "qkb%d" % i, (128, 16, 64), BF16) for i in range(3)])
            oT4r = Rot([sb(st, "oT4_%d" % i, (128, 8, 512), BF16) for i in range(2)])
            vt4r = Rot([sb(st, "vt4_%d" % i, (128, 4, 8, 65), BF16) for i in range(2)])
            for vt in vt4r.items:
                fw.memset("pool", vt[:, :, :, :], 1.0)
            wds = {}
            c0items = [(g, n) for g in range(3) for n in range(32)]
            hold = {}

            def c0s1(it, c):
                g, n = c0items[it]
                dil = DILS[g]
                nb_ = 32 // dil
                if n == 0:
                    wd = wdr.next()
                    for j, off in enumerate((OFF_QB, OFF_KB, OFF_VB)):
                        fw.dma(wd[:, :, j, :], win_dil[:, :, off + g * 512:off + (g + 1) * 512])
                    wds[g] = wd
                wd = wds[g]
                r, bb = n // nb_, n % nb_
                t0 = dil * 128 * bb + r

                def lhsT(kc):
                    return V(xn_ro, xn_t.t[:, kc, t0:t0 + 127 * dil + 1:dil])
                pq_ = pqk.next()
                pv_ = pvv.next()
                for j in range(2):
                    for kc in range(8):
                        fw.mm(pq_[:, j * 512:(j + 1) * 512], lhsT(kc), wd[:, kc, j, :], start=(kc == 0), stop=(kc == 7))
                for kc in range(8):
                    fw.mm(pv_[:, :], lhsT(kc), wd[:, kc, 2, :], start=(kc == 0), stop=(kc == 7))
                sq = sqr.next()
                fw.act(sq[:, :], pq_[:, :], AF.Square)
                s16 = s16r.next()
                fw.reduce("dve", s16[:, 0:16], sq[:, :].rearrange("p (a d) -> p a d", a=16), ALU.add)
                fw.act(s16[:, 16:32], s16[:, 0:16], AF.Ln, scale=1.0 / 64, bias=eps_c)
                fw.act(s16[:, 32:48], s16[:, 16:32], AF.Exp, scale=-0.5)
                qk1 = qk1r.next()
                fw.tt("dve", qk1[:, :, :], pq_[:, :].rearrange("p (a d) -> p a d", a=16),
                      s16[:, 32:48].raw(0, [[1, 16], [0, 64]]), ALU.mult)
                c["qk1"] = qk1
                m4 = n % 4
                if m4 == 0:
                    hold["vt4"] = vt4r.next()
                vt4 = hold["vt4"]
                fw.copy("dve", vt4[:, m4, :, 0:64], pv_[:, :].rearrange("p (a d) -> p a d", a=8))
                if m4 == 3:
                    fw.dma(vaug_d[g].full_v()[:, n - 3:n + 1, :, :], vt4[:, :, :, :], join=True)

            def c0s2(it, c):
                g, n = c0items[it]
                qk1 = c["qk1"]
                qk2 = qk2r.next()
                fw.tt("pool", qk2[:, :, :], qk1[:, :, :], gqk[:, :, :], ALU.mult)
                A = Ar.next()
                B = Br.next()
                fw.tt("pool", A[:, :, :], qk2[:, :, 0:16], csn[g][:, n, :].raw(0, [[0, 16], [1, 16]]), ALU.mult)
                fw.tt("pool", B[:, :, 0:8], qk2[:, :, 8:16], snn[g][:, n, 0:8].raw(0, [[0, 16], [1, 8]]), ALU.mult)
                fw.tt("pool", B[:, :, 8:16], qk2[:, :, 0:8], snn[g][:, n, 8:16].raw(0, [[0, 16], [1, 8]]), ALU.mult)
                qkb = qkbr.next()
                fw.copy("act", qkb[:, :, 16:64], qk2[:, :, 16:64])
                fw.tt("dve", qkb[:, :, 0:16], A[:, :, :], B[:, :, :], ALU.add)
                c["qkb"] = qkb

            def c0s3(it, c):
                g, n = c0items[it]
                qkb = c["qkb"]
                pt_ = ptt.next()
                ptb = pt_[:, :].bitcast(BF16)
                qkb2 = qkb[:, :, :].rearrange("p a d -> p (a d)")
                for cc in range(8):
                    fw.tr(ptb[:, cc * 128:(cc + 1) * 128], qkb2[:, cc * 128:(cc + 1) * 128], ident_b)
                m4 = n % 4
                if m4 == 0:
                    hold["oT4"] = oT4r.next()
                oT4 = hold["oT4"]
                fw.copy("act", oT4[:, :, m4 * 128:(m4 + 1) * 128], ptb.rearrange("p (c t) -> p c t", c=8))
                if m4 == 3:
                    n0 = n - 3
                    dst = qkT_d[g].full_v().rearrange("h p a t -> p a h t")[:, :, :, n0 * 128:(n0 + 4) * 128]
                    fw.dma(dst, oT4[:, :, :].rearrange("p (a h) t -> p a h t", a=2), join=True)

            run_skewed(len(c0items), [c0s1, c0s2, c0s3])
            fw.barrier()

    def phase_c1():
        with ExitStack() as st:
            mask4 = sb(st, "mask4", (128, 4, 128), BF16)
            for a_ in range(4):
                fw.copy("dve", mask4[:, a_, :], cstb[:, 1 if a_ % 2 == 0 else 3, :])
            mask4f = mask4[:, :, :].rearrange("p a i -> p (a i)")
            obT_v = obT_d.full_v()
            accs = [[sb(st, "acc%d_%d" % (k_, hh), (65, S), F32) for hh in range(2)] for k_ in range(2)]
            norm_jobs = []
            qkTr = Rot([sb(st, "qkT%d" % i, (128, 2, S), BF16) for i in range(2)])
            vaugr = Rot([sb(st, "vaug%d" % i, (128, 32, 2, 65), BF16) for i in range(2)])
            pS = Rot([ps(st, "pS%d" % i, (128, 512), F32) for i in range(3)])
            pO = Rot([ps(st, "pO%d" % i, (128, 512), F32) for i in range(3)])
            pB = Rot([ps(st, "pB%d" % i, (128, 512), F32) for i in range(2)])
            ptr_ = Rot([sb(st, "pTs%d" % i, (128, 512), BF16) for i in range(9)])
            obr = Rot([sb(st, "ob%d" % i, (64, 512), BF16) for i in range(2)])
            rbr = Rot([sb(st, "rb%d" % i, (64, 512), F32) for i in range(2)])

            def loads(hp, g):
                qkT = qkTr.next()
                va = vaugr.next()
                fw.dma(qkT[:, :, :], V(qkT_d[g], qkT_d[g].t[hp]))
                fw.dma(va[:, :, :, :], vaug_d[g].full_v()[:, :, hp * 2:hp * 2 + 2, :])
                return qkT, va

            seq = [(hp, g) for hp in range(4) for g in range(3)]
            bufs = {0: loads(*seq[0])}
            items = [(idx, hh, n0) for idx in range(len(seq)) for hh in range(2) for n0 in range(0, 32, 2)]

            def s1(it, c):
                idx, hh, n0 = items[it]
                hp, g = seq[idx]
                nb_ = 32 // DILS[g]
                qkT, vaug = bufs[idx]
                hs = slice(hh * 64, (hh + 1) * 64)
                b0 = n0 % nb_
                pS_ = pS.next()
                for q_ in range(2):
                    n = n0 + q_
                    qv = qkT[hs, 0, n * 128:(n + 1) * 128]
                    fw.mm(pS_[:, (2 * q_) * 128:(2 * q_ + 1) * 128], qkT[hs, 1, n * 128:(n + 1) * 128], qv)
                    if b0 + q_ > 0:
                        fw.mm(pS_[:, (2 * q_ + 1) * 128:(2 * q_ + 2) * 128], qkT[hs, 1, (n - 1) * 128:n * 128], qv)
                pT_ = ptr_.next()
                meng = "pool" if (it % 3) != 2 else "dve"
                if b0 == 0:
                    fw.act(pT_[:, 0:128], pS_[:, 0:128], AF.Exp, scale=0.125)
                    fw.act(pT_[:, 256:512], pS_[:, 256:512], AF.Exp, scale=0.125)
                    fw.tt(meng, pT_[:, 0:128], pT_[:, 0:128], mask4f[:, 0:128], ALU.mult)
                    fw.tt(meng, pT_[:, 256:512], pT_[:, 256:512], mask4f[:, 256:512], ALU.mult)
                else:
                    fw.act(pT_[:, :], pS_[:, :], AF.Exp, scale=0.125)
                    fw.tt(meng, pT_[:, :], pT_[:, :], mask4f, ALU.mult)
                c["pT"] = pT_

            def do_norm(hp_, h2_, blk):
                acc_ = accs[hp_ % 2]
                pB_ = pB.next()
                fw.mm(pB_[0:64, :], cst[64:65, 6, 0:64], acc_[h2_][64:65, blk * 512:(blk + 1) * 512])
                ob = obr.next()
                rb = rbr.next()
                fw.recip(rb[:, :], pB_[0:64, :])
                fw.tt("dve", ob[:, :], acc_[h2_][0:64, blk * 512:(blk + 1) * 512], rb[:, :], ALU.mult)
                hg = hp_ * 2 + h2_
                fw.dma(obT_v[hg * 64:(hg + 1) * 64, blk * 512:(blk + 1) * 512], ob[:, :], join=True)

            def s2(it, c):
                idx, hh, n0 = items[it]
                hp, g = seq[idx]
                dil = DILS[g]
                nb_ = 32 // dil
                qkT, vaug = bufs[idx]
                r, b0 = n0 // nb_, n0 % nb_
                pT_ = c["pT"]
                if hh == 0 and n0 == 0 and idx + 1 < len(seq):
                    bufs[idx + 1] = loads(*seq[idx + 1])
                pO_ = pO.next()
                for q_ in range(2):
                    n = n0 + q_
                    bb = b0 + q_
                    fw.mm(pO_[0:65, q_ * 128:(q_ + 1) * 128], vaug[:, n, hh, :],
                          pT_[:, (2 * q_) * 128:(2 * q_ + 1) * 128], start=True, stop=(bb == 0))
                    if bb > 0:
                        fw.mm(pO_[0:65, q_ * 128:(q_ + 1) * 128], vaug[:, n - 1, hh, :],
                              pT_[:, (2 * q_ + 1) * 128:(2 * q_ + 2) * 128], start=False, stop=True)
                a0 = dil * 128 * b0 + r
                acc = accs[hp % 2]
                av = acc[hh][:, :].raw(a0, [[dil, 256]])
                if g == 0:
                    fw.copy("dve", av, pO_[0:65, 0:256])
                else:
                    fw.tt("dve", av, av, pO_[0:65, 0:256], ALU.add)
                if norm_jobs:
                    do_norm(*norm_jobs.pop(0))
                if g == 2 and hh == 1 and n0 == 30:
                    for h2_ in range(2):
                        for blk in range(8):
                            norm_jobs.append((hp, h2_, blk))

            run_skewed(len(items), [s1, None, None, None, None, s2])
            while norm_jobs:
                do_norm(*norm_jobs.pop(0))
            fw.barrier()

    stB1w.close()
    phase_c0()
    phase_c1()
    if dbg is not None and dbg[0] == "C1":
        with ExitStack() as st:
            tmp = sb(st, "dbgt", (128, 4, 512), BF16)
            sv = obT_d.full_v().rearrange("(c p) t -> p c t", p=128)
            dv = dbg_out.full_v().rearrange("(c p) t -> p c t", p=128)
            for j in range(8):
                fw.dma(tmp[:, :, :], sv[:, :, j * 512:(j + 1) * 512])
                fw.dma(dv[:, :, j * 512:(j + 1) * 512], tmp[:, :, :])
            fw.barrier()
        return finish()

    def phase_t1a():
        with ExitStack() as st:
            wbr = sb(st, "wbr", (128, 8, 1024), BF16)
            wbd = sb(st, "wbd", (128, 4, 1024), BF16)
            wmg = sb(st, "wmg", (128, 8, 2048), BF16)
            wmo = sb(st, "wmo", (128, 8, 1024), BF16)
            fw.dma(wbr[:, :, :], WB["w_br_gla"].full_v().rearrange("(c p) n -> p c n", p=128))
            fw.dma(wbd[:, :, :], WB["w_br_dil"].full_v().rearrange("(c p) n -> p c n", p=128))
            fw.dma(wmg[:, :, :], WB["w_merge_gate"].full_v().rearrange("(c p) n -> p c n", p=128))
            fw.dma(wmo[:, :, :], WB["w_mix_out"].full_v().rearrange("(c p) n -> p c n", p=128))
            bmg = sb(st, "bmg", (1, 2048), BF16)
            fw.dma(bmg[:, :], SM["b_merge_gate"].full_v().rearrange("(o n) -> o n", o=1), q="pool")
            P = Rot([ps(st, "qP%d" % i, (128, 512), F32) for i in range(8)])
            nb = norm_bufs(st, "T") + (P,)
            oatr = Rot([sb(st, "oat%d" % i, (128, 8, 128), BF16) for i in range(2)])
            obtr = Rot([sb(st, "obt%d" % i, (128, 4, 128), BF16) for i in range(2)])
            xtr = Rot([sb(st, "xtT%d" % i, (128, 1024), F32) for i in range(4)])
            gar = Rot([sb(st, "ga%d" % i, (128, 512), F32) for i in range(3)])
            mar = Rot([sb(st, "ma%d" % i, (128, 1024), F32) for i in range(2)])
            mrr = Rot([sb(st, "mr%d" % i, (128, 1024), BF16) for i in range(2)])
            mTr = Rot([sb(st, "mT%d" % i, (128, 8, 128), BF16) for i in range(2)])
            h1r = Rot([sb(st, "h1t%d" % i, (128, 1024), F32) for i in range(2)])
            oaT_v = oaT_d.full_v().rearrange("(c p) t -> p c t", p=128)
            obT_v = obT_d.full_v().rearrange("(c p) t -> p c t", p=128)

            def sL(i, c):
                c["oat"] = oatr.next()
                c["obt"] = obtr.next()
                c["xt"] = xtr.next()
                fw.dma(c["oat"][:, :, :], oaT_v[:, :, i * 128:(i + 1) * 128])
                fw.dma(c["obt"][:, :, :], obT_v[:, :, i * 128:(i + 1) * 128])
                fw.dma(c["xt"][:, :], x[i * 128:(i + 1) * 128, :])

            def s1(i, c):
                oat, obt = c["oat"], c["obt"]
                xnv = V(xn_sub[i // 4], xn_t.t[:, :, i * 128:(i + 1) * 128])
                ms = []
                for br in range(2):
                    ma = mar.next()
                    for half in range(2):
                        cs_ = slice(half * 512, (half + 1) * 512)
                        gs_ = slice(br * 1024 + half * 512, br * 1024 + (half + 1) * 512)
                        py = P.next()
                        if br == 0:
                            for kc in range(8):
                                fw.mm(py[:, :], oat[:, kc, :], wbr[:, kc, cs_], start=(kc == 0), stop=(kc == 7))
                        else:
                            for kc in range(4):
                                fw.mm(py[:, :], obt[:, kc, :], wbd[:, kc, cs_], start=(kc == 0), stop=(kc == 3))
                        pg = P.next()
                        for kc in range(8):
                            fw.mm(pg[:, :], xnv[:, kc, :], wmg[:, kc, gs_], start=(kc == 0), stop=False)
                        fw.mm(pg[:, :], ones_row_b, bmg[0:1, gs_], start=False, stop=True)
                        ga = gar.next()
                        fw.act(ga[:, :], pg[:, :], AF.Sigmoid)
                        fw.tt("dve", ma[:, cs_], py[:, :], ga[:, :], ALU.mult)
                    ms.append(ma)
                mr = mrr.next()
                fw.tt("pool", mr[:, :], ms[0][:, :], ms[1][:, :], ALU.add)
                c["mr"] = mr

            def s2(i, c):
                mr, xt = c["mr"], c["xt"]
                pt = P.next()
                ptb = pt[:, :].bitcast(BF16)
                for cc in range(8):
                    fw.tr(ptb[:, cc * 128:(cc + 1) * 128], mr[:, cc * 128:(cc + 1) * 128], ident_b)
                mT = mTr.next()
                fw.copy("act", mT[:, :, :], ptb.rearrange("p (c t) -> p c t", c=8))
                c["mT"] = mT

            def s2b(i, c):
                mT, xt = c["mT"], c["xt"]
                h1t = h1r.next()
                for half in range(2):
                    cs_ = slice(half * 512, (half + 1) * 512)
                    pm = P.next()
                    for kc in range(8):
                        fw.mm(pm[:, :], mT[:, kc, :], wmo[:, kc, cs_], start=(kc == 0), stop=(kc == 7))
                    fw.tt("dve", h1t[:, cs_], pm[:, :], xt[:, cs_], ALU.add)
                fw.dma(h1_d[i * 128:(i + 1) * 128, :], h1t[:, :], join=True)
                c["xs"] = norm_a(nb, h1t[:, :])

            def s3(i, c):
                xnv = V(xn_sub[i // 4], xn_t.t[:, :, i * 128:(i + 1) * 128])
                norm_b(nb, c["xs"], gcol["norm_x"], xnv)

            sL(0, None) if False else None
            ctxs = [dict() for _ in range(NT)]
            sL(0, ctxs[0])
            for step in range(NT + 3):
                if step + 1 < NT:
                    sL(step + 1, ctxs[step + 1])
                if step < NT:
                    s1(step, ctxs[step])
                if 0 <= step - 1 < NT:
                    s2(step - 1, ctxs[step - 1])
                if 0 <= step - 2 < NT:
                    s2b(step - 2, ctxs[step - 2])
                if 0 <= step - 3 < NT:
                    s3(step - 3, ctxs[step - 3])
            fw.barrier()

    def phase_t1b():
        with ExitStack() as st:
            wxq = sb(st, "wxq", (128, 8, 1024), BF16)
            wxo = sb(st, "wxo", (128, 8, 1024), BF16)
            fw.dma(wxq[:, :, :], WB["w_xq"].full_v().rearrange("(c p) n -> p c n", p=128))
            fw.dma(wxo[:, :, :], WB["w_xo"].full_v().rearrange("(c p) n -> p c n", p=128))
            gq = sb(st, "gqx", (128, 256), F32)
            gk = sb(st, "gkx", (128, 256), F32)
            fw.dma(gq[:, :], dram_bc(SM["x_q_norm"], 256))
            fw.dma(gk[:, :], dram_bc(SM["x_k_norm"], 256))
            kmT = sb(st, "kmT", (128, 8, 256), BF16)
            Vm = sb(st, "Vm", (128, 2, 1024), BF16)
            PB = Rot([ps(st, "rP%d" % i, (128, 512), F32) for i in range(8)])
            p1 = PB
            nb = norm_bufs(st, "X") + (PB,)
            ssr = Rot([sb(st, "ssXh%d" % i, (128, 12), F32) for i in range(3)])
            jkr = Rot([sb(st, "jkXh%d" % i, (128, 256), BF16) for i in range(2)])
            qnr = Rot([sb(st, "qn%d" % i, (128, 1024), BF16) for i in range(2)])

            def head_norm(psrc, gain, dst):
                ss = ssr.next()
                jk = jkr.next()
                for h in range(4):
                    fw.act(jk[:, :], psrc[:, h * 256:(h + 1) * 256], AF.Square, accum=ss[:, h:h + 1])
                fw.act(ss[:, 4:8], ss[:, 0:4], AF.Ln, scale=1.0 / 256, bias=eps_c)
                fw.act(ss[:, 8:12], ss[:, 4:8], AF.Exp, scale=-0.5)
                for h in range(4):
                    fw.stt("dve", dst[:, h * 256:(h + 1) * 256], psrc[:, h * 256:(h + 1) * 256], ss[:, 8 + h:9 + h],
                           gain[:, :], ALU.mult, ALU.mult)

            with ExitStack() as st2:
                wkv = sb(st2, "wkv", (128, 8, 2048), BF16)
                fw.dma(wkv[:, :, :], WB["w_xkv"].full_v().rearrange("(c p) n -> p c n", p=128))
                mnT = sb(st2, "mnT", (128, 8, 256), BF16)
                mtr = Rot([sb(st2, "mt%d" % i, (128, 1024), F32) for i in range(2)])
                for m in range(2):
                    mt = mtr.next()
                    fw.dma(mt[:, :], mem[m * 128:(m + 1) * 128, :])
                    norm_T(nb, mt[:, :], gcol["norm_mem"], mnT[:, :, m * 128:(m + 1) * 128])
                for m in range(2):
                    pk = []
                    for half in range(2):
                        cs_ = slice(half * 512, (half + 1) * 512)
                        p_ = PB.next()
                        for kc in range(8):
                            fw.mm(p_[:, :], mnT[:, kc, m * 128:(m + 1) * 128], wkv[:, kc, cs_],
                                  start=(kc == 0), stop=(kc == 7))
                        pk.append(p_)
                        pv_ = PB.next()
                        for kc in range(8):
                            fw.mm(pv_[:, :], mnT[:, kc, m * 128:(m + 1) * 128],
                                  wkv[:, kc, 1024 + half * 512:1024 + (half + 1) * 512], start=(kc == 0), stop=(kc == 7))
                        fw.copy("act", Vm[:, m, cs_], pv_[:, :])
                    srcs = [pk[h // 2][:, (h % 2) * 256:(h % 2 + 1) * 256] for h in range(4)]
                    hnk = HeadNorm(st2, "K%d" % m, 1)
                    ssk = hnk.stats(srcs)
                    kn = qnr.next()
                    for h in range(4):
                        fw.stt("dve", kn[:, h * 256:(h + 1) * 256], srcs[h], ssk[:, 8 + h:9 + h], gk[:, :], ALU.mult, ALU.mult)
                    pt = PB.next()
                    ptb = pt[:, :].bitcast(BF16)
                    for cc in range(8):
                        fw.tr(ptb[:, cc * 128:(cc + 1) * 128], kn[:, cc * 128:(cc + 1) * 128], ident_b)
                    fw.copy("act", kmT[:, :, m * 128:(m + 1) * 128], ptb.rearrange("p (c t) -> p c t", c=8))
                fw.barrier()

            h1r = Rot([sb(st, "h1x%d" % i, (128, 1024), F32) for i in range(7)])
            qTr = Rot([sb(st, "qTx%d" % i, (128, 8, 128), BF16) for i in range(2)])
            ptsr = Rot([sb(st, "ptsx%d" % i, (128, 1024), BF16) for i in range(2)])
            rdr = Rot([sb(st, "rdx%d" % i, (128, 512), F32) for i in range(2)])
            oxr = Rot([sb(st, "oxT%d" % i, (128, 8, 128), BF16) for i in range(2)])
            h2r = Rot([sb(st, "h2x%d" % i, (128, 1024), F32) for i in range(2)])
            qn2r = Rot([sb(st, "qn2_%d" % i, (128, 1024), BF16) for i in range(2)])
            ones_b = cstb[:, 6, :]
            hn = HeadNorm(st, "X")

            def sL(i, c):
                c["h1"] = h1r.next()
                fw.dma(c["h1"][:, :], h1_d[i * 128:(i + 1) * 128, :])

            def s1(i, c):
                xnv = V(xn_ro, xn_t.t[:, :, i * 128:(i + 1) * 128])
                pq = []
                for half in range(2):
                    cs_ = slice(half * 512, (half + 1) * 512)
                    p_ = PB.next()
                    for kc in range(8):
                        fw.mm(p_[:, :], xnv[:, kc, :], wxq[:, kc, cs_], start=(kc == 0), stop=(kc == 7))
                    pq.append(p_)
                srcs = [pq[h // 2][:, (h % 2) * 256:(h % 2 + 1) * 256] for h in range(4)]
                ss = hn.stats(srcs)
                qn = qn2r.next()
                for h in range(4):
                    fw.stt("dve", qn[:, h * 256:(h + 1) * 256], srcs[h], ss[:, 8 + h:9 + h], gq[:, :], ALU.mult, ALU.mult)
                c["qn"] = qn

            def s2(i, c):
                qn = c["qn"]
                pt = PB.next()
                ptb = pt[:, :].bitcast(BF16)
                for cc in range(8):
                    fw.tr(ptb[:, cc * 128:(cc + 1) * 128], qn[:, cc * 128:(cc + 1) * 128], ident_b)
                qT = qTr.next()
                fw.copy("act", qT[:, :, :], ptb.rearrange("p (c t) -> p c t", c=8))
                c["qT"] = qT

            def s2b(i, c):
                qT = c["qT"]
                pts = ptsr.next()
                for hb in range(2):
                    pS2 = PB.next()
                    for hl in range(2):
                        h = hb * 2 + hl
                        for mb in range(2):
                            sl = slice((hl * 2 + mb) * 128, (hl * 2 + mb + 1) * 128)
                            for dc in range(2):
                                fw.mm(pS2[:, sl], kmT[:, h * 2 + dc, mb * 128:(mb + 1) * 128], qT[:, h * 2 + dc, :],
                                      start=(dc == 0), stop=(dc == 1))
                    fw.act(pts[:, hb * 512:(hb + 1) * 512], pS2[:, :], AF.Exp, scale=1.0 / 16)
                c["pts"] = pts

            def s3(i, c):
                pts = c["pts"]
                pD = PB.next()
                for h in range(4):
                    for mb in range(2):
                        fw.mm(pD[:, h * 128:(h + 1) * 128], ones_b, pts[:, (h * 2 + mb) * 128:(h * 2 + mb + 1) * 128],
                              start=(mb == 0), stop=(mb == 1))
                rd = rdr.next()
                fw.recip(rd[:, :], pD[:, :])
                ox = oxr.next()
                for hb in range(2):
                    pO2 = PB.next()
                    for hl in range(2):
                        h = hb * 2 + hl
                        for dc in range(2):
                            sl = slice((hl * 2 + dc) * 128, (hl * 2 + dc + 1) * 128)
                            for mb in range(2):
                                fw.mm(pO2[:, sl], Vm[:, mb, h * 256 + dc * 128:h * 256 + (dc + 1) * 128],
                                      pts[:, (h * 2 + mb) * 128:(h * 2 + mb + 1) * 128], start=(mb == 0), stop=(mb == 1))
                    fw.tt("dve", ox[:, hb * 4:(hb + 1) * 4, :].rearrange("p (h c) t -> p h c t", h=2),
                          pO2[:, :].rearrange("p (h c t) -> p h c t", h=2, c=2),
                          rd[:, hb * 256:(hb + 1) * 256].raw(0, [[128, 2], [0, 2], [1, 128]]), ALU.mult)
                c["ox"] = ox

            def s4(i, c):
                ox, h1t = c["ox"], c["h1"]
                h2t = h2r.next()
                for half in range(2):
                    cs_ = slice(half * 512, (half + 1) * 512)
                    px = PB.next()
                    for kc in range(8):
                        fw.mm(px[:, :], ox[:, kc, :], wxo[:, kc, cs_], start=(kc == 0), stop=(kc == 7))
                    fw.tt("dve", h2t[:, cs_], px[:, :], h1t[:, cs_], ALU.add)
                fw.dma(h2_d[i * 128:(i + 1) * 128, :], h2t[:, :], join=True)

            ctxs = [dict() for _ in range(NT)]
            sL(0, ctxs[0])
            for step in range(NT + 4):
                if step + 1 < NT:
                    sL(step + 1, ctxs[step + 1])
                for k, fn in enumerate((s1, s2, s2b, s3, s4)):
                    i = step - k
                    if 0 <= i < NT:
                        fn(i, ctxs[i])
            fw.barrier()

    def phase_t2():
        with ExitStack() as st:
            wup = sb(st, "wup", (128, 8, 2 * DFF), BF16)
            wdn = sb(st, "wdn", (128, NFC, 1024), BF16)
            wcb = sb(st, "wcb", (128, 4 * NFC), F32)
            crow = sb(st, "crow", (4 * NFC, 128), F32)
            fw.dma(crow[0:3 * NFC, :], SM["w_ffn_conv"].full_v().rearrange("t (c p) -> (t c) p", p=128))
            fw.dma(crow[3 * NFC:4 * NFC, :], SM["b_ffn_conv"].full_v().rearrange("(c p) -> c p", p=128))
            hist = sb(st, "hist", (128, NFC, 2), F32)
            hsub = [hist.sub("hist%d" % fc) for fc in range(NFC)]
            for fc in range(NFC):
                fw.memset("pool", V(hsub[fc], hist.t[:, fc, :]), 0.0)
            p1 = Rot([ps(st, "s1_%d" % i, (128, 512), F32) for i in range(4)])
            p2 = Rot([ps(st, "s2_%d" % i, (128, 1024), F32) for i in range(2)])
            nb = norm_bufs(st, "F", 1, 4) + (p1,)
            cps = p1.next()
            fw.tr(cps[:, 0:4 * NFC], crow[:, :], cst[0:4 * NFC, 0, 0:4 * NFC])
            fw.copy("dve", wcb[:, :], cps[:, 0:4 * NFC])
            h2r = Rot([sb(st, "h2f%d" % i, (128, 1024), F32) for i in range(2)])
            x3r = Rot([sb(st, "x3T%d" % i, (128, 8, 512), BF16) for i in range(1)])
            abr = Rot([sb(st, "abt%d" % i, (128, 514), F32) for i in range(2)])
            c1r = Rot([sb(st, "cv1_%d" % i, (128, 512), F32) for i in range(2)])
            c2r = Rot([sb(st, "cv2_%d" % i, (128, 512), F32) for i in range(2)])
            glr = Rot([sb(st, "gl%d" % i, (128, 512), F32) for i in range(2)])
            yT = sb(st, "yT", (128, NFC, 512), BF16)
            otr = Rot([sb(st, "ot%d" % i, (128, 1024), F32) for i in range(1)])

            def stage_na(s_):
                xsl = []
                for t_ in range(4):
                    i = s_ * 4 + t_
                    h2t = h2r.next()
                    fw.dma(h2t[:, :], h2_d[i * 128:(i + 1) * 128, :])
                    xsl.append(norm_a(nb, h2t[:, :]))
                return xsl

            def stage_nb(xsl):
                x3 = x3r.next()
                for t_ in range(4):
                    norm_b(nb, xsl[t_], gcol["norm_ffn"], x3[:, :, t_ * 128:(t_ + 1) * 128])
                return x3

            xsl0 = stage_na(0)
            wupv = WB["w_ffn_up"].full_v().rearrange("(c p) n -> p c n", p=128)
            for cb in range(4):
                for base in (0, DFF):
                    c0_ = base + cb * 704
                    fw.dma(wup[:, :, c0_:c0_ + 704], wupv[:, :, c0_:c0_ + 704])
            wdnv = WB["w_ffn_down"].full_v().rearrange("(c p) n -> p c n", p=128)
            for c0 in range(0, NFC, 6):
                c1_ = min(NFC, c0 + 6)
                fw.dma(wdn[:, c0:c1_, :], wdnv[:, c0:c1_, :])
            x3n = stage_nb(xsl0)
            for s_ in range(NS):
                x3 = x3n
                xsl_next = [] if s_ + 1 < NS else None
                pend = {}
                for fc in range(NFC):
                    if xsl_next is not None and fc % 5 == 0 and fc // 5 < 4:
                        t_ = fc // 5
                        i_ = (s_ + 1) * 4 + t_
                        h2n = h2r.next()
                        fw.dma(h2n[:, :], h2_d[i_ * 128:(i_ + 1) * 128, :])
                        pend[t_] = h2n
                    if xsl_next is not None and fc % 5 == 3 and fc // 5 < 4:
                        xsl_next.append(norm_a(nb, pend[fc // 5][:, :]))
                    pa = p1.next()
                    for kc in range(8):
                        fw.mm(pa[:, :], wup[:, kc, fc * 128:(fc + 1) * 128], x3[:, kc, :], start=(kc == 0), stop=(kc == 7))
                    pu = p1.next()
                    for kc in range(8):
                        fw.mm(pu[:, :], wup[:, kc, DFF + fc * 128:DFF + (fc + 1) * 128], x3[:, kc, :],
                              start=(kc == 0), stop=(kc == 7))
                    ab = abr.next()
                    hv = V(hsub[fc], hist.t[:, fc, :])
                    fw.copy("pool", ab[:, 0:2], hv)
                    fw.copy("act", ab[:, 2:514], pa[:, :])
                    c1 = c1r.next()
                    c2 = c2r.next()
                    fw.ts("dve", c1[:, :], ab[:, 2:514], wcb[:, 2 * NFC + fc:2 * NFC + fc + 1], ALU.mult, wcb[:, 3 * NFC + fc:3 * NFC + fc + 1], ALU.add)
                    fw.stt("dve", c2[:, :], ab[:, 1:513], wcb[:, NFC + fc:NFC + fc + 1], c1[:, :], ALU.mult, ALU.add)
                    fw.stt("dve", c1[:, :], ab[:, 0:512], wcb[:, fc:fc + 1], c2[:, :], ALU.mult, ALU.add)
                    fw.copy("pool", hv, ab[:, 512:514])
                    gl = glr.next()
                    fw.act(gl[:, :], c1[:, :], AF.Gelu)
                    fw.tt("dve", yT[:, fc, :], gl[:, :], pu[:, :], ALU.mult)
                if xsl_next is not None:
                    x3n = stage_nb(xsl_next)
                for t_ in range(4):
                    i = s_ * 4 + t_
                    h2t = h2r.next()
                    fw.dma(h2t[:, :], h2_d[i * 128:(i + 1) * 128, :])
                    pdn = p2.next()
                    for half in range(2):
                        cs_ = slice(half * 512, (half + 1) * 512)
                        for fc in range(NFC):
                            fw.mm(pdn[:, cs_], yT[:, fc, t_ * 128:(t_ + 1) * 128], wdn[:, fc, cs_],
                                  start=(fc == 0), stop=(fc == NFC - 1))
                    ot = otr.next()
                    fw.tt("dve", ot[:, :], pdn[:, :], h2t[:, :], ALU.add)
                    fw.dma(out[i * 128:(i + 1) * 128, :], ot[:, :], join=True)
            fw.barrier()

    def dump_f32(src_d):
        with ExitStack() as st:
            tmp = Rot([sb(st, "dbgf%d" % i, (128, 1024), F32) for i in range(2)])
            for i in range(NT):
                t_ = tmp.next()
                fw.dma(t_[:, :], src_d[i * 128:(i + 1) * 128, :])
                fw.dma(dbg_out[i * 128:(i + 1) * 128, :], t_[:, :])
            fw.barrier()

    phase_t1a()
    if dbg is not None and dbg[0] == "T1a":
        dump_f32(h1_d)
        return finish()
    phase_t1b()
    if dbg is not None and dbg[0] == "T1b":
        dump_f32(h2_d)
        return finish()
    xstack.close()
    phase_t2()
    return finish()


def make_in_maps(inputs):
    consts = make_consts()
    maps = []
    for b in range(8):
        m = {"x": np.ascontiguousarray(inputs["x"][b]), "mem": np.ascontiguousarray(inputs["mem"][b]),
             "positions": np.ascontiguousarray(inputs["positions"][b]).astype(np.int32), "consts": consts}
        for k in BIG_W:
            m[k] = np.ascontiguousarray(np.asarray(inputs[k])[0])
        for k in SMALL:
            m[k] = np.ascontiguousarray(np.asarray(inputs[k])[0])
        maps.append(m)
    return maps


def kernel(**inputs):
    inputs = {k: np.asarray(v) for k, v in inputs.items()}
    nc, fw = build()
    res = run_bass_kernel_spmd(nc, make_in_maps(inputs), core_ids=list(range(8)))
    return np.stack([np.asarray(r["out"]) for r in res.results], axis=0).astype(np.float32)
```

```python
import numpy as np
import concourse.bass as bass
import concourse.mybir as mybir

F32 = mybir.dt.float32
BF16 = mybir.dt.bfloat16
I32 = mybir.dt.int32
AF = mybir.ActivationFunctionType
ALU = mybir.AluOpType
AX = mybir.AxisListType

SAME_ENGINE_SYNC = True
N_DMA_SEMS = 24


class Buf:
    __slots__ = ("t", "w", "r", "name")

    def __init__(self, t, name=""):
        self.t = t
        self.w = []
        self.r = {}
        self.name = name

    def sub(self, name=""):
        return Buf(self.t, name)

    def __getitem__(self, idx):
        return V(self, self.t[idx])

    def full_v(self):
        return V(self, self.t)


class V:
    __slots__ = ("b", "ap")

    def __init__(self, b, ap):
        self.b = b
        self.ap = ap

    def __getitem__(self, idx):
        return V(self.b, self.ap[idx])

    def rearrange(self, s, **kw):
        return V(self.b, self.ap.rearrange(s, **kw))

    def bcast(self, shape):
        return V(self.b, self.ap.to_broadcast(shape))

    def bitcast(self, dt):
        return V(self.b, self.ap.bitcast(dt))

    def raw(self, extra_off, dims):
        a = self.ap
        return V(self.b, bass.AP(a.tensor, a.offset + extra_off, [list(a.ap[0])] + [list(d) for d in dims]))

    @property
    def shape(self):
        return self.ap.shape


class Instr:
    __slots__ = ("stream", "fn", "deps", "signal", "semval", "dma", "dsem", "dval", "idx")

    def __init__(self, stream, fn, dma=False):
        self.stream = stream
        self.fn = fn
        self.deps = []
        self.signal = False
        self.semval = None
        self.dma = dma
        self.dsem = None
        self.dval = None


class FW:
    STREAMS = ("pe", "act", "dve", "pool", "sp")

    def __init__(self, nc):
        self.nc = nc
        self.prog = {s: [] for s in self.STREAMS}
        self.dma_q = {"sp": (0, N_DMA_SEMS), "pool": (N_DMA_SEMS, 8), "act": (N_DMA_SEMS + 8, 8)}
        self.n_dsem = N_DMA_SEMS + 16
        self.dma_rr = {q: 0 for q in self.dma_q}
        self.dma_last = [None] * self.n_dsem
        self.dma_cnt = [0] * self.n_dsem
        self.all_dmas = []
        self.order = []

    def _add(self, stream, fn, reads, writes, dma=False, join=False, xr=(), xw=()):
        ins = Instr(stream, fn, dma)
        deps = []
        reads = list(reads) + list(xr)
        writes = list(writes) + list(xw)
        wb = set(id(v.b) for v in writes)
        for v in reads:
            b = v.b
            if id(b) in wb:
                continue
            for w in b.w:
                deps.append((w, 0))
        for v in writes:
            b = v.b
            if not join:
                for w in b.w:
                    deps.append((w, 0 if any(v2.b is b for v2 in reads) else 1))
            for r in b.r.values():
                deps.append((r, 1))
        if dma:
            base, cnt = self.dma_q[stream]
            k = base + self.dma_rr[stream]
            self.dma_rr[stream] = (self.dma_rr[stream] + 1) % cnt
            if self.dma_last[k] is not None:
                deps.append((self.dma_last[k], 0))
            self.dma_last[k] = ins
            self.dma_cnt[k] += 16
            ins.dsem = k
            ins.dval = self.dma_cnt[k]
            self.all_dmas.append(ins)
        seen = set()
        for d, kind in deps:
            if d is ins or id(d) in seen:
                continue
            if (not d.dma) and (not dma) and d.stream == stream:
                if stream == "pe" or not SAME_ENGINE_SYNC:
                    continue
            seen.add(id(d))
            ins.deps.append(d)
            d.signal = True
        rkey = ("dma", ins.dsem) if dma else stream
        for v in reads:
            if id(v.b) not in wb:
                v.b.r[rkey] = ins
        for v in writes:
            if join:
                v.b.w = v.b.w + [ins]
            else:
                v.b.w = [ins]
                v.b.r = {}
        self.prog[stream].append(ins)
        return ins

    def barrier(self):
        lasts = []
        for s in self.STREAMS:
            for ins in reversed(self.prog[s]):
                if not ins.dma and ins.fn is not None:
                    lasts.append(ins)
                    break
        lasts = lasts + [d for d in self.dma_last if d is not None]
        for s in self.STREAMS:
            ins = Instr(s, None)
            for d in lasts:
                if (not d.dma) and d.stream == s:
                    continue
                ins.deps.append(d)
                d.signal = True
            self.prog[s].append(ins)

    def wait_for(self, stream, instrs):
        ins = Instr(stream, None)
        for d in instrs:
            ins.deps.append(d)
            d.signal = True
        self.prog[stream].append(ins)

    def dma(self, out, in_, q="sp", join=False, **kw):
        o, i = out.ap, in_.ap
        return self._add(q, lambda e: e.dma_start(out=o, in_=i, **kw), [in_], [out], dma=True, join=join)

    def mm(self, out, lhsT, rhs, start=True, stop=True, xr=()):
        o, l, r = out.ap, lhsT.ap, rhs.ap
        return self._add("pe", lambda e: e.matmul(o, l, r, start=start, stop=stop), [lhsT, rhs], [out], xr=xr)

    def tr(self, out, in_, ident):
        o, i, d = out.ap, in_.ap, ident.ap
        return self._add("pe", lambda e: e.transpose(o, i, d), [in_, ident], [out])

    def act(self, out, in_, func, bias=None, scale=None, accum=None, xr=(), xw=()):
        o, i = out.ap, in_.ap
        kw = {}
        reads = [in_]
        writes = [out]
        if bias is not None:
            if isinstance(bias, V):
                kw["bias"] = bias.ap
                reads.append(bias)
            else:
                kw["bias"] = bias
        if scale is not None:
            if isinstance(scale, V):
                kw["scale"] = scale.ap
                reads.append(scale)
            else:
                kw["scale"] = scale
        if accum is not None:
            kw["accum_out"] = accum.ap
            writes.append(accum)
        return self._add("act", lambda e: e.activation(o, i, func, **kw), reads, writes, xr=xr, xw=xw)

    def _veng(self, eng):
        return eng

    def tt(self, eng, out, in0, in1, op):
        o, a, b = out.ap, in0.ap, in1.ap
        return self._add(eng, lambda e: e.tensor_tensor(o, a, b, op), [in0, in1], [out])

    def ts(self, eng, out, in0, s1, op0, s2=None, op1=None, accum=None):
        o, a = out.ap, in0.ap
        reads = [in0]
        writes = [out]
        if isinstance(s1, V):
            reads.append(s1)
            s1 = s1.ap
        if isinstance(s2, V):
            reads.append(s2)
            s2 = s2.ap
        kw = {}
        if op1 is not None:
            kw["op1"] = op1
        if accum is not None:
            kw["accum_out"] = accum.ap
            writes.append(accum)
        return self._add(eng, lambda e: e.tensor_scalar(o, a, s1, s2, op0, **kw), reads, writes)

    def stt(self, eng, out, in0, scalar, in1, op0, op1):
        o, a, b = out.ap, in0.ap, in1.ap
        reads = [in0, in1]
        if isinstance(scalar, V):
            reads.append(scalar)
            scalar = scalar.ap
        return self._add(eng, lambda e: e.scalar_tensor_tensor(o, a, scalar, b, op0, op1), reads, [out])

    def copy(self, eng, out, in_):
        o, i = out.ap, in_.ap
        if eng == "act":
            return self._add("act", lambda e: e.copy(o, i), [in_], [out])
        return self._add(eng, lambda e: e.tensor_copy(o, i), [in_], [out])

    def reduce(self, eng, out, in_, op, axis=AX.X):
        o, i = out.ap, in_.ap
        return self._add(eng, lambda e: e.tensor_reduce(o, i, axis, op), [in_], [out])

    def recip(self, out, in_):
        o, i = out.ap, in_.ap
        return self._add("dve", lambda e: e.reciprocal(o, i), [in_], [out])

    def memset(self, eng, out, val):
        o = out.ap
        return self._add(eng, lambda e: e.memset(o, val), [], [out])

    def emit(self):
        nc = self.nc
        from contextlib import ExitStack
        with ExitStack() as es:
            esem = {s: es.enter_context(nc.semaphore("s_" + s)) for s in self.STREAMS if s != "sp" or True}
            dsem = [es.enter_context(nc.semaphore("d%d" % k)) for k in range(self.n_dsem)]
            self.barrier()
            for s in self.STREAMS:
                c = 0
                for ins in self.prog[s]:
                    if ins.dma or ins.fn is None:
                        continue
                    if ins.signal:
                        c += 1
                        ins.semval = c
            plan = {}
            nwaits = 0
            for s in self.STREAMS:
                known = {}
                acts = []
                for ins in self.prog[s]:
                    for d in ins.deps:
                        if d.dma:
                            key, val, sem = ("d", d.dsem), d.dval, dsem[d.dsem]
                        else:
                            key, val, sem = ("e", d.stream), d.semval, esem[d.stream]
                        assert val is not None
                        if known.get(key, 0) >= val:
                            continue
                        known[key] = val
                        acts.append(("w", sem, val))
                        nwaits += 1
                    if ins.fn is not None:
                        acts.append(("i", ins))
                plan[s] = acts

            def run(eng, acts, s):
                for a in acts:
                    if a[0] == "w":
                        eng.wait_ge(a[1], a[2])
                    else:
                        ins = a[1]
                        bi = ins.fn(eng)
                        if ins.dma:
                            bi.then_inc(dsem[ins.dsem], 16)
                        elif ins.signal:
                            bi.then_inc(esem[s], 1)

            with nc.Block() as block:
                @block.sync
                def _(e):
                    run(e, plan["sp"], "sp")

                @block.tensor
                def _(e):
                    run(e, plan["pe"], "pe")

                @block.scalar
                def _(e):
                    run(e, plan["act"], "act")

                @block.vector
                def _(e):
                    run(e, plan["dve"], "dve")

                @block.gpsimd
                def _(e):
                    run(e, plan["pool"], "pool")
            self.stats = {s: len(self.prog[s]) for s in self.STREAMS}
            self.stats["waits"] = nwaits
from concourse.bass_utils import run_bass_kernel_spmd

import math
import ml_dtypes
from contextlib import ExitStack

S = 4096
D = 1024
NT = S // 128
NS = S // 512
NMEM = 256
DFF = 2816
NFC = DFF // 128
EPS = 1e-6
IN_COLS = 7696
OFF_QA, OFF_KA, OFF_VA, OFF_RA, OFF_ZA, OFF_QB, OFF_KB, OFF_VB = 0, 512, 1024, 2048, 3072, 3088, 4624, 6160
DILS = (1, 4, 16)
PI = math.pi

BIG_W = {
    "w_in": (1024, IN_COLS), "w_br_gla": (1024, 1024), "w_br_dil": (512, 1024),
    "w_merge_gate": (1024, 2048), "w_mix_out": (1024, 1024), "w_xq": (1024, 1024),
    "w_xkv": (1024, 2048), "w_xo": (1024, 1024), "w_ffn_up": (1024, 2 * DFF),
    "w_ffn_down": (DFF, 1024),
}
SMALL = {
    "norm_mix": (1024,), "w_gla_gate": (16, 512), "b_gla_gate": (512,), "gla_out_norm": (256,),
    "dil_q_norm": (64,), "dil_k_norm": (64,), "b_merge_gate": (2048,), "norm_x": (1024,),
    "norm_mem": (1024,), "x_q_norm": (256,), "x_k_norm": (256,), "norm_ffn": (1024,),
    "w_ffn_conv": (3, DFF), "b_ffn_conv": (DFF,),
}


class Rot:
    def __init__(self, items):
        self.items = items
        self.i = 0

    def next(self):
        it = self.items[self.i % len(self.items)]
        self.i += 1
        return it


def make_consts():
    c = np.zeros((8, 128, 128), np.float32)
    i = np.arange(128)
    c[0] = np.eye(128)
    c[1] = (i[:, None] <= i[None, :])
    c[2] = (i[:, None] > i[None, :])
    c[3] = (i[:, None] >= i[None, :])
    c[6] = 1.0
    inv_freq = (500000.0 ** (-np.arange(0, 16, 2, dtype=np.float32) / 16)).astype(np.float32)
    c[7, :, 0:8] = inv_freq[None, :]
    return c


def build(dbg=None, stop_after=None):
    nc = bass.Bass("TRN2", target_bir_lowering=False)
    fw = FW(nc)
    es = ExitStack()

    def din(name, shape, dt=F32):
        return Buf(nc.dram_tensor(name, list(shape), dt, kind="ExternalInput").ap(), name)

    def dscr(name, shape, dt):
        return Buf(nc.dram_tensor(name, list(shape), dt, kind="Internal").ap(), name)

    def dout(name, shape, dt=F32):
        return Buf(nc.dram_tensor(name, list(shape), dt, kind="ExternalOutput").ap(), name)

    x = din("x", (S, D))
    mem = din("mem", (NMEM, D))
    pos = din("positions", (S,), I32)
    consts = din("consts", (8, 128, 128))
    W32 = {k: din(k, v) for k, v in BIG_W.items()}
    SM = {k: din(k, v) for k, v in SMALL.items()}
    out = dout("out", (S, D))
    dbg_out = None
    if dbg is not None:
        dbg_out = dout("dbg", dbg[1], dbg[2])

    WB = {k: dscr(k + "_bf", v, BF16) for k, v in BIG_W.items()}
    oaT_d = dscr("oaT_d", (1024, S), BF16)
    obT_d = dscr("obT_d", (512, S), BF16)
    h1_d = dscr("h1_d", (S, D), F32)
    h2_d = dscr("h2_d", (S, D), F32)

    def sb(stack, name, shape, dt):
        return Buf(stack.enter_context(nc.sbuf_tensor(name, list(shape), dt))[:], name)

    def ps(stack, name, shape, dt):
        return Buf(stack.enter_context(nc.psum_tensor(name, list(shape), dt))[:], name)

    def dram_bc(buf, n, parts=128, off=0):
        a = buf.t
        return V(buf, bass.AP(a.tensor, a.offset + off, [[0, parts], [1, n]]))

    xstack = ExitStack()

    def finish():
        fw.emit()
        stB1w.close()
        xstack.close()
        es.close()
        return nc, fw

    win_gla_b = WB["w_in"].sub("w_in_gla")
    win_dil_b = WB["w_in"].sub("w_in_dil")
    top = es
    cst = sb(top, "cst", (128, 8, 128), F32)
    cstb = sb(top, "cstb", (128, 8, 128), BF16)
    epsb = sb(top, "epsb", (128, 2), F32)
    gall = sb(top, "gall", (128, 32), F32)
    gcol = {k: V(gall, gall.t[:, 8 * j:8 * j + 8]) for j, k in enumerate(("norm_mix", "norm_x", "norm_mem", "norm_ffn"))}

    fw.dma(cst[:, :, :], consts.full_v().rearrange("c p f -> p c f"))
    fw.copy("dve", cstb[:, :, :], cst[:, :, :])
    fw.memset("dve", epsb[:, 0:1], EPS)
    fw.memset("dve", epsb[:, 1:2], 1.0)
    with ExitStack() as st0:
        grow = sb(st0, "grow", (32, 128), F32)
        gps = ps(st0, "gps", (128, 512), F32)
        for j, k in enumerate(("norm_mix", "norm_x", "norm_mem", "norm_ffn")):
            fw.dma(grow[8 * j:8 * j + 8, :], SM[k].full_v().rearrange("(c p) -> c p", p=128))
        fw.tr(gps[:, 0:32], grow[:, :], cst[0:32, 0, 0:32])
        fw.copy("dve", gall[:, :], gps[:, 0:32])
        fw.barrier()
    ident_b = cstb[:, 0, :]
    ones_row_b = cstb[0:1, 6, :]
    eps_c = epsb[:, 0:1]
    one_c = epsb[:, 1:2]

    xn_t = sb(xstack, "xnT", (128, 8, S), BF16)
    xn_sub = [xn_t.sub("xn%d" % j) for j in range(NS)]
    xn_ro = xn_t.sub("xn_ro")

    def xn_w(i):
        return V(xn_sub[i // 4], xn_t.t[:, :, i * 128:(i + 1) * 128])

    def norm_a(nb, xt):
        junk, ssr, xsr, ptr = nb
        jk = junk.next()
        ss = ssr.next()
        fw.act(jk[:, :], xt, AF.Square, accum=ss[:, 0:1])
        fw.act(ss[:, 1:2], ss[:, 0:1], AF.Ln, scale=1.0 / 1024, bias=eps_c)
        fw.act(ss[:, 2:3], ss[:, 1:2], AF.Exp, scale=-0.5)
        xs = xsr.next()
        fw.ts("dve", xs[:, :], xt, ss[:, 2:3], ALU.mult)
        return xs

    def norm_b(nb, xs, g, dst_view):
        junk, ssr, xsr, ptr = nb
        pt = ptr.next()
        ptb = pt[:, :].bitcast(BF16)
        for c in range(8):
            fw.tr(ptb[:, c * 128:(c + 1) * 128], xs[:, c * 128:(c + 1) * 128], ident_b)
        fw.tt("dve", dst_view, ptb.rearrange("p (c t) -> p c t", c=8),
              g[:, :].raw(0, [[1, 8], [0, 128]]), ALU.mult)

    def norm_T(nb, xt, g, dst_view):
        norm_b(nb, norm_a(nb, xt), g, dst_view)

    class HeadNorm:
        def __init__(self, st, tag, n=3):
            self.slots = []
            for i in range(n):
                ss = sb(st, "hn_ss%s%d" % (tag, i), (128, 12), F32)
                jk = sb(st, "hn_jk%s%d" % (tag, i), (128, 4, 256), BF16)
                self.slots.append((ss, [ss.sub() for _ in range(4)], jk, [jk.sub() for _ in range(4)]))
            self.i = 0

        def stats(self, srcs):
            ss, ssub, jk, jsub = self.slots[self.i % len(self.slots)]
            self.i += 1
            for h in range(4):
                fw.act(V(jsub[h], jk.t[:, h, :]), srcs[h], AF.Square, accum=V(ssub[h], ss.t[:, h:h + 1]))
            fw.act(ss[:, 4:8], ss[:, 0:4], AF.Ln, scale=1.0 / 256, bias=eps_c,
                   xr=[V(ssub[h], ss.t[:, h:h + 1]) for h in range(4)])
            fw.act(ss[:, 8:12], ss[:, 4:8], AF.Exp, scale=-0.5)
            return ss

    def norm_bufs(st, tag, nj=2, nxs=2):
        return (Rot([sb(st, "junk%s%d" % (tag, i), (128, 1024), BF16) for i in range(nj)]),
                Rot([sb(st, "ss%s%d" % (tag, i), (128, 4), F32) for i in range(max(4, nxs + 1))]),
                Rot([sb(st, "xs%s%d" % (tag, i), (128, 1024), BF16) for i in range(nxs)]))

    win = V(win_gla_b, WB["w_in"].t.rearrange("(c p) n -> p c n", p=128))
    win_dil = V(win_dil_b, WB["w_in"].t.rearrange("(c p) n -> p c n", p=128))
    stB1w = ExitStack()
    wqk = sb(stB1w, "wqk", (128, 8, 1024), BF16)
    wv = sb(stB1w, "wv", (128, 8, 1024), BF16)
    wr = sb(stB1w, "wr", (128, 8, 1024), BF16)
    wz = sb(stB1w, "wz", (128, 8, 16), BF16)
    wgg = sb(stB1w, "wgg", (16, 512), BF16)
    bgg = sb(stB1w, "bgg", (1, 512), BF16)
    gon = sb(stB1w, "gon", (128, 256), F32)
    fw.dma(gon[:, :], dram_bc(SM["gla_out_norm"], 256))
    fw.dma(wgg[:, :], SM["w_gla_gate"].full_v(), q="pool")
    fw.dma(bgg[:, :], SM["b_gla_gate"].full_v().rearrange("(o n) -> o n", o=1), q="pool")
    b1_loads = []
    with ExitStack() as st:
        xts = Rot([sb(st, "xt%d" % i, (128, 1024), F32) for i in range(4)])
        nb = norm_bufs(st, "A") + (Rot([ps(st, "ptA%d" % i, (128, 512), F32) for i in range(2)]),)
        xloads = []
        for i in range(NT):
            xt = xts.next()
            xloads.append(fw.dma(xt[:, :], x[i * 128:(i + 1) * 128, :]))
            if i == 3:
                fw.wait_for("pool", xloads)
                for (c0, c1, bsub) in ((0, 1544, win_gla_b), (1544, 3088, win_gla_b)):
                    for r0 in (0, 512):
                        fw.dma(V(bsub, WB["w_in"].t[r0:r0 + 512, c0:c1]),
                               V(W32["w_in"], W32["w_in"].t[r0:r0 + 512, c0:c1]), q="pool", join=True)
            if i == NT - 1:
                b1_loads.append(fw.dma(wqk[:, :, :], win[:, :, 0:1024]))
                b1_loads.append(fw.dma(wv[:, :, :], win[:, :, 1024:2048]))
                b1_loads.append(fw.dma(wr[:, :, :], win[:, :, 2048:3072]))
                b1_loads.append(fw.dma(wz[:, :, :], win[:, :, 3072:3088], allow_slow_non_contiguous=True))
            norm_T(nb, xt[:, :], gcol["norm_mix"], xn_w(i))
        fw.barrier()
    fw.wait_for("pool", b1_loads)
    deferred_conv = []
    for (c0, c1, bsub) in ((3088, 4624, win_dil_b), (4624, 6160, win_dil_b), (6160, 7696, win_dil_b)):
        for r0 in (0, 512):
            deferred_conv.append((V(bsub, WB["w_in"].t[r0:r0 + 512, c0:c1]),
                                  V(W32["w_in"], W32["w_in"].t[r0:r0 + 512, c0:c1])))
    for k, shp in BIG_W.items():
        if k == "w_in":
            continue
        n = shp[0] * shp[1]
        rows = n // 2048
        src = W32[k].full_v().rearrange("a b -> (a b)").rearrange("(r c) -> r c", c=2048)
        dst = WB[k].full_v().rearrange("a b -> (a b)").rearrange("(r c) -> r c", c=2048)
        r0 = 0
        while r0 < rows:
            r1 = min(rows, r0 + 512)
            deferred_conv.append((dst[r0:r1, :], src[r0:r1, :]))
            r0 = r1

    def emit_conv(after=None, n=1):
        for _ in range(n):
            if not deferred_conv:
                return
            if after is not None:
                fw.wait_for("pool", [after])
            d_, s_ = deferred_conv.pop(0)
            fw.dma(d_, s_, q="pool", join=True)

    emit_conv(None, 2)

    DKS = 128 ** -0.5
    oaT_v = oaT_d.full_v().rearrange("(c p) t -> p c t", p=128)
    with ExitStack() as st:
        S32 = [sb(st, "S32_%d" % h, (128, 256), F32) for h in range(4)]
        Sbf = [sb(st, "Sbf_%d" % h, (128, 256), BF16) for h in range(4)]
        for h in range(4):
            fw.memset("pool", S32[h][:, :], 0.0)
            fw.memset("pool", Sbf[h][:, :], 0.0)
        PB = Rot([ps(st, "bP%d" % i, (128, 512), F32) for i in range(8)])
        zar = Rot([sb(st, "za%d" % i, (16, 512), BF16) for i in range(2)])
        e1r = Rot([sb(st, "e1_%d" % i, (128, 512), F32) for i in range(1)])
        spb = [sb(st, "sp_%d" % i, (128, 512), F32) for i in range(4)]
        eTpr = Rot([sb(st, "eTp%d" % i, (128, 512), F32) for i in range(2)])
        eTnr = Rot([sb(st, "eTn%d" % i, (128, 512), F32) for i in range(2)])
        ervr = Rot([sb(st, "erv%d" % i, (128, 512), F32) for i in range(2)])
        elr = Rot([sb(st, "elast%d" % i, (128, 16), F32) for i in range(2)])
        qes = [[sb(st, "qe%d_%d" % (i, h), (128, 512), BF16) for h in range(4)] for i in range(2)]
        kes = [[sb(st, "ke%d_%d" % (i, h), (128, 512), BF16) for h in range(4)] for i in range(2)]
        kls = [[sb(st, "kl%d_%d" % (i, c4), (128, 512), BF16) for c4 in range(4)] for i in range(2)]
        vsr = Rot([sb(st, "vs%d" % i, (128, 1024), BF16) for i in range(2)])
        atr = Rot([sb(st, "at%d" % i, (128, 512), BF16) for i in range(3)])
        Gr = Rot([sb(st, "G%d" % i, (128, 1024), F32) for i in range(2)])
        onr = Rot([sb(st, "on%d" % i, (128, 1024), BF16) for i in range(2)])
        oTr = Rot([sb(st, "oT%d" % i, (128, 8, 128), BF16) for i in range(2)])
        hn = HeadNorm(st, "B", 2)
        pst = {}
        cst_ = {}

        def xs_(kc, s_):
            return V(xn_ro, xn_t.t[:, kc, s_ * 512:(s_ + 1) * 512])

        def xc_(kc, c):
            return V(xn_ro, xn_t.t[:, kc, c * 128:(c + 1) * 128])

        def P_a(s_):
            d = pst.setdefault(s_, {})
            d["set"] = s_ % 2
            pza = PB.next()
            for kc in range(8):
                fw.mm(pza[0:16, :], wz[:, kc, :], xs_(kc, s_), start=(kc == 0), stop=(kc == 7))
            za = zar.next()
            fw.copy("act", za[:, :], pza[0:16, :])
            d["za"] = za

        def P_a2(s_):
            d = pst[s_]
            za = d["za"]
            for c4 in range(4):
                pz = PB.next()
                fw.mm(pz[:, :], za[:, c4 * 128:(c4 + 1) * 128], wgg[:, :], start=True, stop=False)
                fw.mm(pz[:, :], ones_row_b, bgg[:, :], start=False, stop=True)
                e1 = e1r.next()
                fw.act(e1[:, :], pz[:, :], AF.Exp, scale=-1.0)
                fw.act(spb[c4][:, :], e1[:, :], AF.Ln, bias=one_c, scale=1.0)
            d["el"] = elr.next()

        def P_b(s_, hs):
            d = pst[s_]
            el = d["el"]
            for h in hs:
                pc = PB.next()
                for c4 in range(4):
                    fw.mm(pc[:, c4 * 128:(c4 + 1) * 128], spb[c4][:, h * 128:(h + 1) * 128], cst[:, 1, :])
                eTp = eTpr.next()
                eTn = eTnr.next()
                fw.act(eTp[:, :], pc[:, :], AF.Exp, scale=-1.0 / 16)
                fw.act(eTn[:, :], pc[:, :], AF.Exp, scale=1.0 / 16)
                fw.copy("pool", el[:, h * 4:(h + 1) * 4], eTp[:, :].raw(127, [[128, 4]]))
                for which in range(2):
                    p_ = PB.next()
                    co = (0 if which == 0 else 512) + h * 128
                    for kc in range(8):
                        fw.mm(p_[:, :], wqk[:, kc, co:co + 128], xs_(kc, s_), start=(kc == 0), stop=(kc == 7))
                    if which == 0:
                        fw.stt("dve", qes[d["set"]][h][:, :], p_[:, :], DKS, eTp[:, :], ALU.mult, ALU.mult)
                    else:
                        fw.tt("dve", kes[d["set"]][h][:, :], p_[:, :], eTn[:, :], ALU.mult)

        def P_c(s_, c4):
            d = pst[s_]
            c = s_ * 4 + c4
            pr = PB.next()
            fw.mm(pr[:, :], cst[:, 2, :], spb[c4][:, :])
            erv = ervr.next()
            fw.act(erv[:, :], pr[:, :], AF.Exp, scale=-1.0 / 16)
            pkt = PB.next()
            for kc in range(8):
                fw.mm(pkt[:, :], xc_(kc, c), wqk[:, kc, 512:1024], start=(kc == 0), stop=(kc == 7))
            fw.tt("dve", kls[d["set"]][c4][:, :], pkt[:, :], erv[:, :], ALU.mult)

        def R1(c):
            s_, c4 = c // 4, c % 4
            d = pst[s_]
            cd = cst_.setdefault(c, {})
            qe, ke = qes[d["set"]], kes[d["set"]]
            tsl = slice(c4 * 128, (c4 + 1) * 128)
            pa = PB.next()
            for h in range(4):
                fw.mm(pa[:, h * 128:(h + 1) * 128], ke[h][:, tsl], qe[h][:, tsl])
            atm = atr.next()
            fw.tt("dve", atm[:, :].rearrange("p (h i) -> p h i", h=4), pa[:, :].rearrange("p (h i) -> p h i", h=4),
                  cst[:, 1, :].raw(0, [[0, 4], [1, 128]]), ALU.mult)
            cd["atm"] = atm
            vs = vsr.next()
            for half in range(2):
                pv = PB.next()
                for kc in range(8):
                    fw.mm(pv[:, :], xc_(kc, c), wv[:, kc, half * 512:(half + 1) * 512], start=(kc == 0), stop=(kc == 7))
                fw.copy("act", vs[:, half * 512:(half + 1) * 512], pv[:, :])
            cd["vs"] = vs
            G = Gr.next()
            for half in range(2):
                prr = PB.next()
                for kc in range(8):
                    fw.mm(prr[:, :], xc_(kc, c), wr[:, kc, half * 512:(half + 1) * 512], start=(kc == 0), stop=(kc == 7))
                fw.act(G[:, half * 512:(half + 1) * 512], prr[:, :], AF.Silu)
            fw.tt("pool", G[:, :].rearrange("p (h v) -> p h v", h=4), G[:, :].rearrange("p (h v) -> p h v", h=4),
                  gon[:, :].raw(0, [[0, 4], [1, 256]]), ALU.mult)
            cd["G"] = G

        def R2(c):
            s_, c4 = c // 4, c % 4
            d = pst[s_]
            cd = cst_[c]
            qe = qes[d["set"]]
            klw = kls[d["set"]][c4]
            vs = cd["vs"]
            el = d["el"]
            atm, G = cd["atm"], cd["G"]
            tsl = slice(c4 * 128, (c4 + 1) * 128)
            pos = []
            for hb in range(2):
                pob = PB.next()
                for hl in range(2):
                    h = hb * 2 + hl
                    fw.mm(pob[:, hl * 256:(hl + 1) * 256], atm[:, h * 128:(h + 1) * 128], vs[:, h * 256:(h + 1) * 256],
                          start=True, stop=False)
                    fw.mm(pob[:, hl * 256:(hl + 1) * 256], qe[h][:, tsl], Sbf[h][:, :], start=False, stop=True)
                pos.append(pob)
            for hb in range(2):
                pdb = PB.next()
                for hl in range(2):
                    h = hb * 2 + hl
                    fw.mm(pdb[:, hl * 256:(hl + 1) * 256], klw[:, h * 128:(h + 1) * 128], vs[:, h * 256:(h + 1) * 256])
                for hl in range(2):
                    h = hb * 2 + hl
                    fw.stt("dve", S32[h][:, :], S32[h][:, :], el[:, h * 4 + c4:h * 4 + c4 + 1],
                           pdb[:, hl * 256:(hl + 1) * 256], ALU.mult, ALU.add)
                    fw.copy("pool", Sbf[h][:, :], S32[h][:, :])
            srcs = [pos[h // 2][:, (h % 2) * 256:(h % 2 + 1) * 256] for h in range(4)]
            ss = hn.stats(srcs)
            on = onr.next()
            for h in range(4):
                fw.stt("dve", on[:, h * 256:(h + 1) * 256], srcs[h], ss[:, 8 + h:9 + h],
                       G[:, h * 256:(h + 1) * 256], ALU.mult, ALU.mult)
            cd["on"] = on

        def R3(c):
            cd = cst_[c]
            on = cd["on"]
            pt = PB.next()
            ptb = pt[:, :].bitcast(BF16)
            for cc in range(8):
                fw.tr(ptb[:, cc * 128:(cc + 1) * 128], on[:, cc * 128:(cc + 1) * 128], ident_b)
            oT = oTr.next()
            fw.copy("act", oT[:, :, :], ptb.rearrange("p (c t) -> p c t", c=8))
            st_ins = fw.dma(oaT_v[:, :, c * 128:(c + 1) * 128], oT[:, :, :], join=True)
            emit_conv(st_ins, 1)

        def P_quarter(s_, qi):
            if qi == 0:
                P_a(s_)
            elif qi == 1:
                P_a2(s_)
            elif qi == 2:
                P_b(s_, (0, 1))
            else:
                P_b(s_, (2, 3))
                for c4 in range(4):
                    P_c(s_, c4)

        for qi in range(4):
            P_quarter(0, qi)
        R1(0)
        for c in range(NT):
            s_, c4 = c // 4, c % 4
            R2(c)
            if c > 0:
                R3(c - 1)
            if s_ + 1 < NS:
                P_quarter(s_ + 1, c4)
            if c + 1 < NT:
                R1(c + 1)
        R3(NT - 1)
        emit_conv(None, 1000)
        fw.barrier()

    if dbg is not None and dbg[0] == "B1":
        with ExitStack() as st:
            tmp = sb(st, "dbgt", (128, 8, 512), BF16)
            dv = dbg_out.full_v().rearrange("(c p) t -> p c t", p=128)
            for j in range(8):
                fw.dma(tmp[:, :, :], oaT_v[:, :, j * 512:(j + 1) * 512])
                fw.dma(dv[:, :, j * 512:(j + 1) * 512], tmp[:, :, :])
            fw.barrier()
        return finish()

    def run_skewed(n_items, stages):
        K = len(stages)
        ctx = [dict() for _ in range(n_items)]
        for step in range(n_items + K - 1):
            for k in range(K):
                i = step - k
                if 0 <= i < n_items and stages[k] is not None:
                    stages[k](i, ctx[i])

    qkT_d = [dscr("qkT_d%d" % g, (4, 128, 2, S), BF16) for g in range(3)]
    vaug_d = [dscr("vaug_d%d" % g, (128, 32, 8, 65), BF16) for g in range(3)]

    def phase_c0():
        with ExitStack() as st:
            pqk = Rot([ps(st, "pqk%d" % i, (128, 1024), F32) for i in range(2)])
            pvv = Rot([ps(st, "pvv%d" % i, (128, 512), F32) for i in range(2)])
            ptt = Rot([ps(st, "ptt%d" % i, (128, 512), F32) for i in range(2)])
            invf = cst[:, 7, 0:8]
            csn = [sb(st, "cs%d" % g, (128, 32, 16), F32) for g in range(3)]
            snn = [sb(st, "sn%d" % g, (128, 32, 16), F32) for g in range(3)]
            gqk = sb(st, "gqk", (128, 16, 64), F32)
            for a_ in range(2):
                nm = "dil_q_norm" if a_ == 0 else "dil_k_norm"
                a = SM[nm].t
                fw.dma(gqk[:, a_ * 8:(a_ + 1) * 8, :], V(SM[nm], bass.AP(a.tensor, a.offset, [[0, 128], [0, 8], [1, 64]])))
            if True:
                prow_b = sb(st, "prow", (1, S), F32)
                prow = prow_b[:, :]
                ang = sb(st, "ang", (128, 256), F32)
                tA = sb(st, "tA", (128, 256), F32)
                tB = sb(st, "tB", (128, 256), F32)
                tI = sb(st, "tI", (128, 256), I32)
                TWO_PI = 2.0 * PI
                fw.dma(prow, pos.full_v().rearrange("(o n) -> o n", o=1), q="pool")
                aps = [ptt.items[1]]
                for g, dil in enumerate(DILS):
                    nb_ = 32 // dil
                    ap_ = aps[0]
                    for n in range(32):
                        r, bb = n // nb_, n % nb_
                        t0 = dil * 128 * bb + r
                        fw.mm(ap_[:, n * 8:(n + 1) * 8], prow[0:1, t0:t0 + 127 * dil + 1:dil], cst[0:1, 7, 0:8])
                    fw.copy("dve", ang[:, :], ap_[:, 0:256])
                    for which in range(2):
                        if which == 1:
                            fw.ts("dve", ang[:, :], ang[:, :], PI / 2, ALU.add)
                        fw.ts("dve", tA[:, :], ang[:, :], 1.0 / TWO_PI, ALU.mult)
                        fw.copy("dve", tI[:, :], tA[:, :])
                        fw.copy("dve", tA[:, :], tI[:, :])
                        fw.stt("dve", tB[:, :], tA[:, :], -TWO_PI, ang[:, :], ALU.mult, ALU.add)
                        fw.ts("dve", tA[:, :], tB[:, :], PI, ALU.is_gt)
                        fw.stt("dve", tB[:, :], tA[:, :], -TWO_PI, tB[:, :], ALU.mult, ALU.add)
                        fw.ts("dve", tA[:, :], tB[:, :], -PI, ALU.is_lt)
                        fw.stt("dve", tB[:, :], tA[:, :], TWO_PI, tB[:, :], ALU.mult, ALU.add)
                        fw.ts("dve", tB[:, :], tB[:, :], 3.141592, ALU.min, -3.141592, ALU.max)
                        tb3 = tB[:, :].rearrange("p (n e) -> p n e", e=8)
                        if which == 0:
                            fw.act(snn[g][:, :, 8:16], tb3, AF.Sin)
                            fw.ts("dve", snn[g][:, :, 0:8], snn[g][:, :, 8:16], -1.0, ALU.mult)
                        else:
                            fw.act(csn[g][:, :, 0:8], tb3, AF.Sin)
                            fw.copy("dve", csn[g][:, :, 8:16], csn[g][:, :, 0:8])

            wdr = Rot([sb(st, "wd%d" % i, (128, 8, 3, 512), BF16) for i in range(1)])
            sqr = Rot([sb(st, "sq%d" % i, (128, 1024), F32) for i in range(2)])
            s16r = Rot([sb(st, "s16_%d" % i, (128, 48), F32) for i in range(3)])
            qk1r = Rot([sb(st, "qk1_%d" % i, (128, 16, 64), F32) for i in range(3)])
            qk2r = Rot([sb(st, "qk2_%d" % i, (128, 16, 64), F32) for i in range(2)])
            Ar = Rot([sb(st, "ropeA%d" % i, (128, 16, 16), F32) for i in range(2)])
            Br = Rot([sb(st, "ropeB%d" % i, (128, 16, 16), F32) for i in range(2)])
            qkbr = Rot([sb(st, "qkb%d" % i, (128, 16, 64), BF16) for i in range(3)])
            oT4r = Rot([sb(st, "oT4_%d" % i, (128, 8, 512), BF16) for i in range(2)])
            vt4r = Rot([sb(st, "vt4_%d" % i, (128, 4, 8, 65), BF16) for i in range(2)])
            for vt in vt4r.items:
                fw.memset("pool", vt[:, :, :, :], 1.0)
            wds = {}
            c0items = [(g, n) for g in range(3) for n in range(32)]
            hold = {}

            def c0s1(it, c):
                g, n = c0items[it]
                dil = DILS[g]
                nb_ = 32 // dil
                if n == 0:
                    wd = wdr.next()
                    for j, off in enumerate((OFF_QB, OFF_KB, OFF_VB)):
                        fw.dma(wd[:, :, j, :], win_dil[:, :, off + g * 512:off + (g + 1) * 512])
                    wds[g] = wd
                wd = wds[g]
                r, bb = n // nb_, n % nb_
                t0 = dil * 128 * bb + r

                def lhsT(kc):
                    return V(xn_ro, xn_t.t[:, kc, t0:t0 + 127 * dil + 1:dil])
                pq_ = pqk.next()
                pv_ = pvv.next()
                for j in range(2):
                    for kc in range(8):
                        fw.mm(pq_[:, j * 512:(j + 1) * 512], lhsT(kc), wd[:, kc, j, :], start=(kc == 0), stop=(kc == 7))
                for kc in range(8):
                    fw.mm(pv_[:, :], lhsT(kc), wd[:, kc, 2, :], start=(kc == 0), stop=(kc == 7))
                sq = sqr.next()
                fw.act(sq[:, :], pq_[:, :], AF.Square)
                s16 = s16r.next()
                fw.reduce("dve", s16[:, 0:16], sq[:, :].rearrange("p (a d) -> p a d", a=16), ALU.add)
                fw.act(s16[:, 16:32], s16[:, 0:16], AF.Ln, scale=1.0 / 64, bias=eps_c)
                fw.act(s16[:, 32:48], s16[:, 16:32], AF.Exp, scale=-0.5)
                qk1 = qk1r.next()
                fw.tt("dve", qk1[:, :, :], pq_[:, :].rearrange("p (a d) -> p a d", a=16),
                      s16[:, 32:48].raw(0, [[1, 16], [0, 64]]), ALU.mult)
                c["qk1"] = qk1
                m4 = n % 4
                if m4 == 0:
                    hold["vt4"] = vt4r.next()
                vt4 = hold["vt4"]
                fw.copy("dve", vt4[:, m4, :, 0:64], pv_[:, :].rearrange("p (a d) -> p a d", a=8))
                if m4 == 3:
                    fw.dma(vaug_d[g].full_v()[:, n - 3:n + 1, :, :], vt4[:, :, :, :], join=True)

            def c0s2(it, c):
                g, n = c0items[it]
                qk1 = c["qk1"]
                qk2 = qk2r.next()
                fw.tt("pool", qk2[:, :, :], qk1[:, :, :], gqk[:, :, :], ALU.mult)
                A = Ar.next()
                B = Br.next()
                fw.tt("pool", A[:, :, :], qk2[:, :, 0:16], csn[g][:, n, :].raw(0, [[0, 16], [1, 16]]), ALU.mult)
                fw.tt("pool", B[:, :, 0:8], qk2[:, :, 8:16], snn[g][:, n, 0:8].raw(0, [[0, 16], [1, 8]]), ALU.mult)
                fw.tt("pool", B[:, :, 8:16], qk2[:, :, 0:8], snn[g][:, n, 8:16].raw(0, [[0, 16], [1, 8]]), ALU.mult)
                qkb = qkbr.next()
                fw.copy("act", qkb[:, :, 16:64], qk2[:, :, 16:64])
                fw.tt("dve", qkb[:, :, 0:16], A[:, :, :], B[:, :, :], ALU.add)
                c["qkb"] = qkb

            def c0s3(it, c):
                g, n = c0items[it]
                qkb = c["qkb"]
                pt_ = ptt.next()
                ptb = pt_[:, :].bitcast(BF16)
                qkb2 = qkb[:, :, :].rearrange("p a d -> p (a d)")
                for cc in range(8):
                    fw.tr(ptb[:, cc * 128:(cc + 1) * 128], qkb2[:, cc * 128:(cc + 1) * 128], ident_b)
                m4 = n % 4
                if m4 == 0:
                    hold["oT4"] = oT4r.next()
                oT4 = hold["oT4"]
                fw.copy("act", oT4[:, :, m4 * 128:(m4 + 1) * 128], ptb.rearrange("p (c t) -> p c t", c=8))
                if m4 == 3:
                    n0 = n - 3
                    dst = qkT_d[g].full_v().rearrange("h p a t -> p a h t")[:, :, :, n0 * 128:(n0 + 4) * 128]
                    fw.dma(dst, oT4[:, :, :].rearrange("p (a h) t -> p a h t", a=2), join=True)

            run_skewed(len(c0items), [c0s1, c0s2, c0s3])
            fw.barrier()

    def phase_c1():
        with ExitStack() as st:
            mask4 = sb(st, "mask4", (128, 4, 128), BF16)
            for a_ in range(4):
                fw.copy("dve", mask4[:, a_, :], cstb[:, 1 if a_ % 2 == 0 else 3, :])
            mask4f = mask4[:, :, :].rearrange("p a i -> p (a i)")
            obT_v = obT_d.full_v()
            accs = [[sb(st, "acc%d_%d" % (k_, hh), (65, S), F32) for hh in range(2)] for k_ in range(2)]
            norm_jobs = []
            qkTr = Rot([sb(st, "qkT%d" % i, (128, 2, S), BF16) for i in range(2)])
            vaugr = Rot([sb(st, "vaug%d" % i, (128, 32, 2, 65), BF16) for i in range(2)])
            pS = Rot([ps(st, "pS%d" % i, (128, 512), F32) for i in range(3)])
            pO = Rot([ps(st, "pO%d" % i, (128, 512), F32) for i in range(3)])
            pB = Rot([ps(st, "pB%d" % i, (128, 512), F32) for i in range(2)])
            ptr_ = Rot([sb(st, "pTs%d" % i, (128, 512), BF16) for i in range(9)])
            obr = Rot([sb(st, "ob%d" % i, (64, 512), BF16) for i in range(2)])
            rbr = Rot([sb(st, "rb%d" % i, (64, 512), F32) for i in range(2)])

            def loads(hp, g):
                qkT = qkTr.next()
                va = vaugr.next()
                fw.dma(qkT[:, :, :], V(qkT_d[g], qkT_d[g].t[hp]))
                fw.dma(va[:, :, :, :], vaug_d[g].full_v()[:, :, hp * 2:hp * 2 + 2, :])
                return qkT, va

            seq = [(hp, g) for hp in range(4) for g in range(3)]
            bufs = {0: loads(*seq[0])}
            items = [(idx, hh, n0) for idx in range(len(seq)) for hh in range(2) for n0 in range(0, 32, 2)]

            def s1(it, c):
                idx, hh, n0 = items[it]
                hp, g = seq[idx]
                nb_ = 32 // DILS[g]
                qkT, vaug = bufs[idx]
                hs = slice(hh * 64, (hh + 1) * 64)
                b0 = n0 % nb_
                pS_ = pS.next()
                for q_ in range(2):
                    n = n0 + q_
                    qv = qkT[hs, 0, n * 128:(n + 1) * 128]
                    fw.mm(pS_[:, (2 * q_) * 128:(2 * q_ + 1) * 128], qkT[hs, 1, n * 128:(n + 1) * 128], qv)
                    if b0 + q_ > 0:
                        fw.mm(pS_[:, (2 * q_ + 1) * 128:(2 * q_ + 2) * 128], qkT[hs, 1, (n - 1) * 128:n * 128], qv)
                pT_ = ptr_.next()
                meng = "pool" if (it % 3) != 2 else "dve"
                if b0 == 0:
                    fw.act(pT_[:, 0:128], pS_[:, 0:128], AF.Exp, scale=0.125)
                    fw.act(pT_[:, 256:512], pS_[:, 256:512], AF.Exp, scale=0.125)
                    fw.tt(meng, pT_[:, 0:128], pT_[:, 0:128], mask4f[:, 0:128], ALU.mult)
                    fw.tt(meng, pT_[:, 256:512], pT_[:, 256:512], mask4f[:, 256:512], ALU.mult)
                else:
                    fw.act(pT_[:, :], pS_[:, :], AF.Exp, scale=0.125)
                    fw.tt(meng, pT_[:, :], pT_[:, :], mask4f, ALU.mult)
                c["pT"] = pT_

            def do_norm(hp_, h2_, blk):
                acc_ = accs[hp_ % 2]
                pB_ = pB.next()
                fw.mm(pB_[0:64, :], cst[64:65, 6, 0:64], acc_[h2_][64:65, blk * 512:(blk + 1) * 512])
                ob = obr.next()
                rb = rbr.next()
                fw.recip(rb[:, :], pB_[0:64, :])
                fw.tt("dve", ob[:, :], acc_[h2_][0:64, blk * 512:(blk + 1) * 512], rb[:, :], ALU.mult)
                hg = hp_ * 2 + h2_
                fw.dma(obT_v[hg * 64:(hg + 1) * 64, blk * 512:(blk + 1) * 512], ob[:, :], join=True)

            def s2(it, c):
                idx, hh, n0 = items[it]
                hp, g = seq[idx]
                dil = DILS[g]
                nb_ = 32 // dil
                qkT, vaug = bufs[idx]
                r, b0 = n0 // nb_, n0 % nb_
                pT_ = c["pT"]
                if hh == 0 and n0 == 0 and idx + 1 < len(seq):
                    bufs[idx + 1] = loads(*seq[idx + 1])
                pO_ = pO.next()
                for q_ in range(2):
                    n = n0 + q_
                    bb = b0 + q_
                    fw.mm(pO_[0:65, q_ * 128:(q_ + 1) * 128], vaug[:, n, hh, :],
                          pT_[:, (2 * q_) * 128:(2 * q_ + 1) * 128], start=True, stop=(bb == 0))
                    if bb > 0:
                        fw.mm(pO_[0:65, q_ * 128:(q_ + 1) * 128], vaug[:, n - 1, hh, :],
                              pT_[:, (2 * q_ + 1) * 128:(2 * q_ + 2) * 128], start=False, stop=True)
                a0 = dil * 128 * b0 + r
                acc = accs[hp % 2]
                av = acc[hh][:, :].raw(a0, [[dil, 256]])
                if g == 0:
                    fw.copy("dve", av, pO_[0:65, 0:256])
                else:
                    fw.tt("dve", av, av, pO_[0:65, 0:256], ALU.add)
                if norm_jobs:
                    do_norm(*norm_jobs.pop(0))
                if g == 2 and hh == 1 and n0 == 30:
                    for h2_ in range(2):
                        for blk in range(8):
                            norm_jobs.append((hp, h2_, blk))

            run_skewed(len(items), [s1, None, None, None, None, s2])
            while norm_jobs:
                do_norm(*norm_jobs.pop(0))
            fw.barrier()

    stB1w.close()
    phase_c0()
    phase_c1()
    if dbg is not None and dbg[0] == "C1":
        with ExitStack() as st:
            tmp = sb(st, "dbgt", (128, 4, 512), BF16)
            sv = obT_d.full_v().rearrange("(c p) t -> p c t", p=128)
            dv = dbg_out.full_v().rearrange("(c p) t -> p c t", p=128)
            for j in range(8):
                fw.dma(tmp[:, :, :], sv[:, :, j * 512:(j + 1) * 512])
                fw.dma(dv[:, :, j * 512:(j + 1) * 512], tmp[:, :, :])
            fw.barrier()
        return finish()

    def phase_t1a():
        with ExitStack() as st:
            wbr = sb(st, "wbr", (128, 8, 1024), BF16)
            wbd = sb(st, "wbd", (128, 4, 1024), BF16)
            wmg = sb(st, "wmg", (128, 8, 2048), BF16)
            wmo = sb(st, "wmo", (128, 8, 1024), BF16)
            fw.dma(wbr[:, :, :], WB["w_br_gla"].full_v().rearrange("(c p) n -> p c n", p=128))
            fw.dma(wbd[:, :, :], WB["w_br_dil"].full_v().rearrange("(c p) n -> p c n", p=128))
            fw.dma(wmg[:, :, :], WB["w_merge_gate"].full_v().rearrange("(c p) n -> p c n", p=128))
            fw.dma(wmo[:, :, :], WB["w_mix_out"].full_v().rearrange("(c p) n -> p c n", p=128))
            bmg = sb(st, "bmg", (1, 2048), BF16)
            fw.dma(bmg[:, :], SM["b_merge_gate"].full_v().rearrange("(o n) -> o n", o=1), q="pool")
            P = Rot([ps(st, "qP%d" % i, (128, 512), F32) for i in range(8)])
            nb = norm_bufs(st, "T") + (P,)
            oatr = Rot([sb(st, "oat%d" % i, (128, 8, 128), BF16) for i in range(2)])
            obtr = Rot([sb(st, "obt%d" % i, (128, 4, 128), BF16) for i in range(2)])
            xtr = Rot([sb(st, "xtT%d" % i, (128, 1024), F32) for i in range(4)])
            gar = Rot([sb(st, "ga%d" % i, (128, 512), F32) for i in range(3)])
            mar = Rot([sb(st, "ma%d" % i, (128, 1024), F32) for i in range(2)])
            mrr = Rot([sb(st, "mr%d" % i, (128, 1024), BF16) for i in range(2)])
            mTr = Rot([sb(st, "mT%d" % i, (128, 8, 128), BF16) for i in range(2)])
            h1r = Rot([sb(st, "h1t%d" % i, (128, 1024), F32) for i in range(2)])
            oaT_v = oaT_d.full_v().rearrange("(c p) t -> p c t", p=128)
            obT_v = obT_d.full_v().rearrange("(c p) t -> p c t", p=128)

            def sL(i, c):
                c["oat"] = oatr.next()
                c["obt"] = obtr.next()
                c["xt"] = xtr.next()
                fw.dma(c["oat"][:, :, :], oaT_v[:, :, i * 128:(i + 1) * 128])
                fw.dma(c["obt"][:, :, :], obT_v[:, :, i * 128:(i + 1) * 128])
                fw.dma(c["xt"][:, :], x[i * 128:(i + 1) * 128, :])

            def s1(i, c):
                oat, obt = c["oat"], c["obt"]
                xnv = V(xn_sub[i // 4], xn_t.t[:, :, i * 128:(i + 1) * 128])
                ms = []
                for br in range(2):
                    ma = mar.next()
                    for half in range(2):
                        cs_ = slice(half * 512, (half + 1) * 512)
                        gs_ = slice(br * 1024 + half * 512, br * 1024 + (half + 1) * 512)
                        py = P.next()
                        if br == 0:
                            for kc in range(8):
                                fw.mm(py[:, :], oat[:, kc, :], wbr[:, kc, cs_], start=(kc == 0), stop=(kc == 7))
                        else:
                            for kc in range(4):
                                fw.mm(py[:, :], obt[:, kc, :], wbd[:, kc, cs_], start=(kc == 0), stop=(kc == 3))
                        pg = P.next()
                        for kc in range(8):
                            fw.mm(pg[:, :], xnv[:, kc, :], wmg[:, kc, gs_], start=(kc == 0), stop=False)
                        fw.mm(pg[:, :], ones_row_b, bmg[0:1, gs_], start=False, stop=True)
                        ga = gar.next()
                        fw.act(ga[:, :], pg[:, :], AF.Sigmoid)
                        fw.tt("dve", ma[:, cs_], py[:, :], ga[:, :], ALU.mult)
                    ms.append(ma)
                mr = mrr.next()
                fw.tt("pool", mr[:, :], ms[0][:, :], ms[1][:, :], ALU.add)
                c["mr"] = mr

            def s2(i, c):
                mr, xt = c["mr"], c["xt"]
                pt = P.next()
                ptb = pt[:, :].bitcast(BF16)
                for cc in range(8):
                    fw.tr(ptb[:, cc * 128:(cc + 1) * 128], mr[:, cc * 128:(cc + 1) * 128], ident_b)
                mT = mTr.next()
                fw.copy("act", mT[:, :, :], ptb.rearrange("p (c t) -> p c t", c=8))
                c["mT"] = mT

            def s2b(i, c):
                mT, xt = c["mT"], c["xt"]
                h1t = h1r.next()
                for half in range(2):
                    cs_ = slice(half * 512, (half + 1) * 512)
                    pm = P.next()
                    for kc in range(8):
                        fw.mm(pm[:, :], mT[:, kc, :], wmo[:, kc, cs_], start=(kc == 0), stop=(kc == 7))
                    fw.tt("dve", h1t[:, cs_], pm[:, :], xt[:, cs_], ALU.add)
                fw.dma(h1_d[i * 128:(i + 1) * 128, :], h1t[:, :], join=True)
                c["xs"] = norm_a(nb, h1t[:, :])

            def s3(i, c):
                xnv = V(xn_sub[i // 4], xn_t.t[:, :, i * 128:(i + 1) * 128])
                norm_b(nb, c["xs"], gcol["norm_x"], xnv)

            sL(0, None) if False else None
            ctxs = [dict() for _ in range(NT)]
            sL(0, ctxs[0])
            for step in range(NT + 3):
                if step + 1 < NT:
                    sL(step + 1, ctxs[step + 1])
                if step < NT:
                    s1(step, ctxs[step])
                if 0 <= step - 1 < NT:
                    s2(step - 1, ctxs[step - 1])
                if 0 <= step - 2 < NT:
                    s2b(step - 2, ctxs[step - 2])
                if 0 <= step - 3 < NT:
                    s3(step - 3, ctxs[step - 3])
            fw.barrier()

    def phase_t1b():
        with ExitStack() as st:
            wxq = sb(st, "wxq", (128, 8, 1024), BF16)
            wxo = sb(st, "wxo", (128, 8, 1024), BF16)
            gq = sb(st, "gqx", (128, 256), F32)
            gk = sb(st, "gkx", (128, 256), F32)
            fw.dma(gq[:, :], dram_bc(SM["x_q_norm"], 256))
            fw.dma(gk[:, :], dram_bc(SM["x_k_norm"], 256))
            kmT = sb(st, "kmT", (128, 8, 256), BF16)
            Vm = sb(st, "Vm", (128, 2, 1024), BF16)
            PB = Rot([ps(st, "rP%d" % i, (128, 512), F32) for i in range(8)])
            p1 = PB
            nb = norm_bufs(st, "X") + (PB,)
            ssr = Rot([sb(st, "ssXh%d" % i, (128, 12), F32) for i in range(3)])
            jkr = Rot([sb(st, "jkXh%d" % i, (128, 256), BF16) for i in range(2)])
            qnr = Rot([sb(st, "qn%d" % i, (128, 1024), BF16) for i in range(2)])

            def head_norm(psrc, gain, dst):
                ss = ssr.next()
                jk = jkr.next()
                for h in range(4):
                    fw.act(jk[:, :], psrc[:, h * 256:(h + 1) * 256], AF.Square, accum=ss[:, h:h + 1])
                fw.act(ss[:, 4:8], ss[:, 0:4], AF.Ln, scale=1.0 / 256, bias=eps_c)
                fw.act(ss[:, 8:12], ss[:, 4:8], AF.Exp, scale=-0.5)
                for h in range(4):
                    fw.stt("dve", dst[:, h * 256:(h + 1) * 256], psrc[:, h * 256:(h + 1) * 256], ss[:, 8 + h:9 + h],
                           gain[:, :], ALU.mult, ALU.mult)

            with ExitStack() as st2:
                wkv = sb(st2, "wkv", (128, 8, 2048), BF16)
                mnT = sb(st2, "mnT", (128, 8, 256), BF16)
                mtr = Rot([sb(st2, "mt%d" % i, (128, 1024), F32) for i in range(2)])
                mts = []
                for m in range(2):
                    mt = mtr.next()
                    fw.dma(mt[:, :], mem[m * 128:(m + 1) * 128, :])
                    mts.append(mt)
                fw.dma(wkv[:, :, :], WB["w_xkv"].full_v().rearrange("(c p) n -> p c n", p=128))
                fw.dma(wxq[:, :, :], WB["w_xq"].full_v().rearrange("(c p) n -> p c n", p=128))
                fw.dma(wxo[:, :, :], WB["w_xo"].full_v().rearrange("(c p) n -> p c n", p=128))
                for m in range(2):
                    norm_T(nb, mts[m][:, :], gcol["norm_mem"], mnT[:, :, m * 128:(m + 1) * 128])
                for m in range(2):
                    pk = []
                    for half in range(2):
                        cs_ = slice(half * 512, (half + 1) * 512)
                        p_ = PB.next()
                        for kc in range(8):
                            fw.mm(p_[:, :], mnT[:, kc, m * 128:(m + 1) * 128], wkv[:, kc, cs_],
                                  start=(kc == 0), stop=(kc == 7))
                        pk.append(p_)
                        pv_ = PB.next()
                        for kc in range(8):
                            fw.mm(pv_[:, :], mnT[:, kc, m * 128:(m + 1) * 128],
                                  wkv[:, kc, 1024 + half * 512:1024 + (half + 1) * 512], start=(kc == 0), stop=(kc == 7))
                        fw.copy("act", Vm[:, m, cs_], pv_[:, :])
                    srcs = [pk[h // 2][:, (h % 2) * 256:(h % 2 + 1) * 256] for h in range(4)]
                    hnk = HeadNorm(st2, "K%d" % m, 1)
                    ssk = hnk.stats(srcs)
                    kn = qnr.next()
                    for h in range(4):
                        fw.stt("dve", kn[:, h * 256:(h + 1) * 256], srcs[h], ssk[:, 8 + h:9 + h], gk[:, :], ALU.mult, ALU.mult)
                    pt = PB.next()
                    ptb = pt[:, :].bitcast(BF16)
                    for cc in range(8):
                        fw.tr(ptb[:, cc * 128:(cc + 1) * 128], kn[:, cc * 128:(cc + 1) * 128], ident_b)
                    fw.copy("act", kmT[:, :, m * 128:(m + 1) * 128], ptb.rearrange("p (c t) -> p c t", c=8))
                fw.barrier()

            h1r = Rot([sb(st, "h1x%d" % i, (128, 1024), F32) for i in range(7)])
            qTr = Rot([sb(st, "qTx%d" % i, (128, 8, 128), BF16) for i in range(2)])
            ptsr = Rot([sb(st, "ptsx%d" % i, (128, 1024), BF16) for i in range(2)])
            rdr = Rot([sb(st, "rdx%d" % i, (128, 512), F32) for i in range(2)])
            oxr = Rot([sb(st, "oxT%d" % i, (128, 8, 128), BF16) for i in range(2)])
            h2r = Rot([sb(st, "h2x%d" % i, (128, 1024), F32) for i in range(2)])
            qn2r = Rot([sb(st, "qn2_%d" % i, (128, 1024), BF16) for i in range(2)])
            ones_b = cstb[:, 6, :]
            hn = HeadNorm(st, "X")

            def sL(i, c):
                c["h1"] = h1r.next()
                fw.dma(c["h1"][:, :], h1_d[i * 128:(i + 1) * 128, :])

            def s1(i, c):
                xnv = V(xn_ro, xn_t.t[:, :, i * 128:(i + 1) * 128])
                pq = []
                for half in range(2):
                    cs_ = slice(half * 512, (half + 1) * 512)
                    p_ = PB.next()
                    for kc in range(8):
                        fw.mm(p_[:, :], xnv[:, kc, :], wxq[:, kc, cs_], start=(kc == 0), stop=(kc == 7))
                    pq.append(p_)
                srcs = [pq[h // 2][:, (h % 2) * 256:(h % 2 + 1) * 256] for h in range(4)]
                ss = hn.stats(srcs)
                qn = qn2r.next()
                for h in range(4):
                    fw.stt("dve", qn[:, h * 256:(h + 1) * 256], srcs[h], ss[:, 8 + h:9 + h], gq[:, :], ALU.mult, ALU.mult)
                c["qn"] = qn

            def s2(i, c):
                qn = c["qn"]
                pt = PB.next()
                ptb = pt[:, :].bitcast(BF16)
                for cc in range(8):
                    fw.tr(ptb[:, cc * 128:(cc + 1) * 128], qn[:, cc * 128:(cc + 1) * 128], ident_b)
                qT = qTr.next()
                fw.copy("act", qT[:, :, :], ptb.rearrange("p (c t) -> p c t", c=8))
                c["qT"] = qT

            def s2b(i, c):
                qT = c["qT"]
                pts = ptsr.next()
                for hb in range(2):
                    pS2 = PB.next()
                    for hl in range(2):
                        h = hb * 2 + hl
                        for mb in range(2):
                            sl = slice((hl * 2 + mb) * 128, (hl * 2 + mb + 1) * 128)
                            for dc in range(2):
                                fw.mm(pS2[:, sl], kmT[:, h * 2 + dc, mb * 128:(mb + 1) * 128], qT[:, h * 2 + dc, :],
                                      start=(dc == 0), stop=(dc == 1))
                    fw.act(pts[:, hb * 512:(hb + 1) * 512], pS2[:, :], AF.Exp, scale=1.0 / 16)
                c["pts"] = pts

            def s3(i, c):
                pts = c["pts"]
                pD = PB.next()
                for h in range(4):
                    for mb in range(2):
                        fw.mm(pD[:, h * 128:(h + 1) * 128], ones_b, pts[:, (h * 2 + mb) * 128:(h * 2 + mb + 1) * 128],
                              start=(mb == 0), stop=(mb == 1))
                rd = rdr.next()
                fw.recip(rd[:, :], pD[:, :])
                ox = oxr.next()
                for hb in range(2):
                    pO2 = PB.next()
                    for hl in range(2):
                        h = hb * 2 + hl
                        for dc in range(2):
                            sl = slice((hl * 2 + dc) * 128, (hl * 2 + dc + 1) * 128)
                            for mb in range(2):
                                fw.mm(pO2[:, sl], Vm[:, mb, h * 256 + dc * 128:h * 256 + (dc + 1) * 128],
                                      pts[:, (h * 2 + mb) * 128:(h * 2 + mb + 1) * 128], start=(mb == 0), stop=(mb == 1))
                    fw.tt("dve", ox[:, hb * 4:(hb + 1) * 4, :].rearrange("p (h c) t -> p h c t", h=2),
                          pO2[:, :].rearrange("p (h c t) -> p h c t", h=2, c=2),
                          rd[:, hb * 256:(hb + 1) * 256].raw(0, [[128, 2], [0, 2], [1, 128]]), ALU.mult)
                c["ox"] = ox

            def s4(i, c):
                ox, h1t = c["ox"], c["h1"]
                h2t = h2r.next()
                for half in range(2):
                    cs_ = slice(half * 512, (half + 1) * 512)
                    px = PB.next()
                    for kc in range(8):
                        fw.mm(px[:, :], ox[:, kc, :], wxo[:, kc, cs_], start=(kc == 0), stop=(kc == 7))
                    fw.tt("dve", h2t[:, cs_], px[:, :], h1t[:, cs_], ALU.add)
                fw.dma(h2_d[i * 128:(i + 1) * 128, :], h2t[:, :], join=True)

            ctxs = [dict() for _ in range(NT)]
            sL(0, ctxs[0])
            for step in range(NT + 4):
                if step + 1 < NT:
                    sL(step + 1, ctxs[step + 1])
                for k, fn in enumerate((s1, s2, s2b, s3, s4)):
                    i = step - k
                    if 0 <= i < NT:
                        fn(i, ctxs[i])
            fw.barrier()

    def phase_t2():
        with ExitStack() as st:
            wup = sb(st, "wup", (128, 8, 2 * DFF), BF16)
            wdn = sb(st, "wdn", (128, NFC, 1024), BF16)
            wcb = sb(st, "wcb", (128, 4 * NFC), F32)
            crow = sb(st, "crow", (4 * NFC, 128), F32)
            fw.dma(crow[0:3 * NFC, :], SM["w_ffn_conv"].full_v().rearrange("t (c p) -> (t c) p", p=128))
            fw.dma(crow[3 * NFC:4 * NFC, :], SM["b_ffn_conv"].full_v().rearrange("(c p) -> c p", p=128))
            hist = sb(st, "hist", (128, NFC, 2), F32)
            hsub = [hist.sub("hist%d" % fc) for fc in range(NFC)]
            for fc in range(NFC):
                fw.memset("pool", V(hsub[fc], hist.t[:, fc, :]), 0.0)
            p1 = Rot([ps(st, "s1_%d" % i, (128, 512), F32) for i in range(4)])
            p2 = Rot([ps(st, "s2_%d" % i, (128, 1024), F32) for i in range(2)])
            nb = norm_bufs(st, "F", 1, 4) + (p1,)
            cps = p1.next()
            fw.tr(cps[:, 0:4 * NFC], crow[:, :], cst[0:4 * NFC, 0, 0:4 * NFC])
            fw.copy("dve", wcb[:, :], cps[:, 0:4 * NFC])
            h2r = Rot([sb(st, "h2f%d" % i, (128, 1024), F32) for i in range(2)])
            x3r = Rot([sb(st, "x3T%d" % i, (128, 8, 512), BF16) for i in range(1)])
            abr = Rot([sb(st, "abt%d" % i, (128, 514), F32) for i in range(2)])
            c1r = Rot([sb(st, "cv1_%d" % i, (128, 512), F32) for i in range(2)])
            c2r = Rot([sb(st, "cv2_%d" % i, (128, 512), F32) for i in range(2)])
            glr = Rot([sb(st, "gl%d" % i, (128, 512), F32) for i in range(2)])
            yT = sb(st, "yT", (128, NFC, 512), BF16)
            otr = Rot([sb(st, "ot%d" % i, (128, 1024), F32) for i in range(1)])

            def stage_na(s_):
                xsl = []
                for t_ in range(4):
                    i = s_ * 4 + t_
                    h2t = h2r.next()
                    fw.dma(h2t[:, :], h2_d[i * 128:(i + 1) * 128, :])
                    xsl.append(norm_a(nb, h2t[:, :]))
                return xsl

            def stage_nb(xsl):
                x3 = x3r.next()
                for t_ in range(4):
                    norm_b(nb, xsl[t_], gcol["norm_ffn"], x3[:, :, t_ * 128:(t_ + 1) * 128])
                return x3

            xsl0 = stage_na(0)
            wupv = WB["w_ffn_up"].full_v().rearrange("(c p) n -> p c n", p=128)
            for cb in range(4):
                for base in (0, DFF):
                    c0_ = base + cb * 704
                    fw.dma(wup[:, :, c0_:c0_ + 704], wupv[:, :, c0_:c0_ + 704])
            wdnv = WB["w_ffn_down"].full_v().rearrange("(c p) n -> p c n", p=128)
            for c0 in range(0, NFC, 6):
                c1_ = min(NFC, c0 + 6)
                fw.dma(wdn[:, c0:c1_, :], wdnv[:, c0:c1_, :])
            x3n = stage_nb(xsl0)
            for s_ in range(NS):
                x3 = x3n
                xsl_next = [] if s_ + 1 < NS else None
                pend = {}
                for fc in range(NFC):
                    if xsl_next is not None and fc % 5 == 0 and fc // 5 < 4:
                        t_ = fc // 5
                        i_ = (s_ + 1) * 4 + t_
                        h2n = h2r.next()
                        fw.dma(h2n[:, :], h2_d[i_ * 128:(i_ + 1) * 128, :])
                        pend[t_] = h2n
                    if xsl_next is not None and fc % 5 == 3 and fc // 5 < 4:
                        xsl_next.append(norm_a(nb, pend[fc // 5][:, :]))
                    pa = p1.next()
                    for kc in range(8):
                        fw.mm(pa[:, :], wup[:, kc, fc * 128:(fc + 1) * 128], x3[:, kc, :], start=(kc == 0), stop=(kc == 7))
                    pu = p1.next()
                    for kc in range(8):
                        fw.mm(pu[:, :], wup[:, kc, DFF + fc * 128:DFF + (fc + 1) * 128], x3[:, kc, :],
                              start=(kc == 0), stop=(kc == 7))
                    ab = abr.next()
                    hv = V(hsub[fc], hist.t[:, fc, :])
                    fw.copy("pool", ab[:, 0:2], hv)
                    fw.copy("act", ab[:, 2:514], pa[:, :])
                    c1 = c1r.next()
                    c2 = c2r.next()
                    fw.ts("dve", c1[:, :], ab[:, 2:514], wcb[:, 2 * NFC + fc:2 * NFC + fc + 1], ALU.mult, wcb[:, 3 * NFC + fc:3 * NFC + fc + 1], ALU.add)
                    fw.stt("dve", c2[:, :], ab[:, 1:513], wcb[:, NFC + fc:NFC + fc + 1], c1[:, :], ALU.mult, ALU.add)
                    fw.stt("dve", c1[:, :], ab[:, 0:512], wcb[:, fc:fc + 1], c2[:, :], ALU.mult, ALU.add)
                    fw.copy("pool", hv, ab[:, 512:514])
                    gl = glr.next()
                    fw.act(gl[:, :], c1[:, :], AF.Gelu)
                    fw.tt("dve", yT[:, fc, :], gl[:, :], pu[:, :], ALU.mult)
                if xsl_next is not None:
                    x3n = stage_nb(xsl_next)
                for t_ in range(4):
                    i = s_ * 4 + t_
                    h2t = h2r.next()
                    fw.dma(h2t[:, :], h2_d[i * 128:(i + 1) * 128, :])
                    pdn = p2.next()
                    for half in range(2):
                        cs_ = slice(half * 512, (half + 1) * 512)
                        for fc in range(NFC):
                            fw.mm(pdn[:, cs_], yT[:, fc, t_ * 128:(t_ + 1) * 128], wdn[:, fc, cs_],
                                  start=(fc == 0), stop=(fc == NFC - 1))
                    ot = otr.next()
                    fw.tt("dve", ot[:, :], pdn[:, :], h2t[:, :], ALU.add)
                    fw.dma(out[i * 128:(i + 1) * 128, :], ot[:, :], join=True)
            fw.barrier()

    def dump_f32(src_d):
        with ExitStack() as st:
            tmp = Rot([sb(st, "dbgf%d" % i, (128, 1024), F32) for i in range(2)])
            for i in range(NT):
                t_ = tmp.next()
                fw.dma(t_[:, :], src_d[i * 128:(i + 1) * 128, :])
                fw.dma(dbg_out[i * 128:(i + 1) * 128, :], t_[:, :])
            fw.barrier()

    phase_t1a()
    if dbg is not None and dbg[0] == "T1a":
        dump_f32(h1_d)
        return finish()
    phase_t1b()
    if dbg is not None and dbg[0] == "T1b":
        dump_f32(h2_d)
        return finish()
    xstack.close()
    phase_t2()
    return finish()


def make_in_maps(inputs):
    consts = make_consts()
    maps = []
    for b in range(8):
        m = {"x": np.ascontiguousarray(inputs["x"][b]), "mem": np.ascontiguousarray(inputs["mem"][b]),
             "positions": np.ascontiguousarray(inputs["positions"][b]).astype(np.int32), "consts": consts}
        for k in BIG_W:
            m[k] = np.ascontiguousarray(np.asarray(inputs[k])[0])
        for k in SMALL:
            m[k] = np.ascontiguousarray(np.asarray(inputs[k])[0])
        maps.append(m)
    return maps


def kernel(**inputs):
    inputs = {k: np.asarray(v) for k, v in inputs.items()}
    nc, fw = build()
    res = run_bass_kernel_spmd(nc, make_in_maps(inputs), core_ids=list(range(8)))
    return np.stack([np.asarray(r["out"]) for r in res.results], axis=0).astype(np.float32)
```
